# Optimizing a Trainium2 kernel written in Bass

```python
import jax
import jax.numpy as jnp
from jax import lax
import numpy as np

D_MODEL = 1024
BATCH = 8
SEQ = 2048
DEPTH = 1
DEC_BATCH = 32
DEC_SEQ = 32
PAST_LEN = 1024

CHUNK = 64
HEAD_DIM = 64
ATT_WIDTH = D_MODEL // 2
N_HEADS = ATT_WIDTH // HEAD_DIM
N_KV_HEADS = N_HEADS // 2
GQA_GROUP = N_HEADS // N_KV_HEADS
ROT_DIM = HEAD_DIM // 4
ROPE_THETA = 500000.0
IDX_HEADS = 8
IDX_DIM = 64
TOPK_MAX = 256
RWKV_WIDTH = D_MODEL - ATT_WIDTH
RWKV_HEADS = RWKV_WIDTH // HEAD_DIM
W_LORA = 32
A_LORA = 32
G_LORA = 96
SHIFT_WIDTH = 3 * RWKV_WIDTH + W_LORA + A_LORA + G_LORA
PROJ_WIDTH = N_HEADS * HEAD_DIM + 2 * N_KV_HEADS * HEAD_DIM + IDX_HEADS * IDX_DIM + IDX_DIM + IDX_HEADS + SHIFT_WIDTH
D_FF = 4 * D_MODEL
NORM_EPS = 1e-6
GN_EPS = 64e-5
L2_EPS = 1e-12

kernel_name = 'hybrid_dsa_rwkv7_stream_step'


def _split(z, sizes):
    offs = np.cumsum(np.array(sizes))[:-1].tolist()
    return jnp.split(z, offs, axis=-1)


def rms_norm(x, g):
    xf = x.astype(jnp.float32)
    y = xf * lax.rsqrt(jnp.mean(xf * xf, axis=-1, keepdims=True) + NORM_EPS)
    return (y * g.astype(jnp.float32)).astype(x.dtype)


def layer_norm(x, g, b):
    xf = x.astype(jnp.float32)
    mu = jnp.mean(xf, axis=-1, keepdims=True)
    var = jnp.mean(jnp.square(xf - mu), axis=-1, keepdims=True)
    y = (xf - mu) * lax.rsqrt(var + NORM_EPS)
    return (y * g.astype(jnp.float32) + b.astype(jnp.float32)).astype(x.dtype)


def rope_partial(x, pos):
    half = ROT_DIM // 2
    inv = ROPE_THETA ** (-jnp.arange(0, ROT_DIM, 2, dtype=jnp.float32) / ROT_DIM)
    ang = pos.astype(jnp.float32)[:, None] * inv[None, :]
    cos = jnp.cos(ang)[:, None, :]
    sin = jnp.sin(ang)[:, None, :]
    xf = x.astype(jnp.float32)
    x1 = xf[..., :half]
    x2 = xf[..., half:ROT_DIM]
    out = jnp.concatenate([x1 * cos - x2 * sin, x2 * cos + x1 * sin, xf[..., ROT_DIM:]], axis=-1)
    return out.astype(x.dtype)


def dsa_block(q, qi, wi, qpos, k_all, v_all, ki_all, topk):
    B, Tq = q.shape[0], q.shape[1]
    L = k_all.shape[1]
    limit = (qpos // CHUNK + 1) * CHUNK
    admissible = jnp.arange(L)[None, :] < limit[:, None]
    dots = jnp.einsum('bthd,bsd->bths', qi.astype(jnp.float32), ki_all.astype(jnp.float32)) * (IDX_DIM ** -0.5)
    score = jnp.einsum('bths,bth->bts', jax.nn.relu(dots), wi.astype(jnp.float32))
    score = jnp.where(admissible[None], score, -jnp.inf)
    _, idx = lax.top_k(score, topk)
    valid = idx < limit[None, :, None]
    kg = jax.vmap(lambda kb, ib: kb[ib])(k_all, idx)
    vg = jax.vmap(lambda vb, ib: vb[ib])(v_all, idx)
    qg = q.reshape(B, Tq, N_KV_HEADS, GQA_GROUP, HEAD_DIM)
    s = jnp.einsum('btkgd,btskd->btkgs', qg.astype(jnp.float32), kg.astype(jnp.float32)) * (HEAD_DIM ** -0.5)
    s = jnp.where(valid[:, :, None, None, :], s, -jnp.inf)
    p = jax.nn.softmax(s, axis=-1)
    o = jnp.einsum('btkgs,btskd->btkgd', p.astype(vg.dtype), vg)
    return o.reshape(B, Tq, N_HEADS * HEAD_DIM).astype(q.dtype)


def dsa_attention(q, qi, wi, pos, k_all, v_all, ki_all):
    B, T = q.shape[0], q.shape[1]
    L = k_all.shape[1]
    topk = min(TOPK_MAX, L // 4)
    qblk = min(T, CHUNK)
    nblk = T // qblk

    def blocks(a):
        return jnp.swapaxes(a.reshape((B, nblk, qblk) + a.shape[2:]), 0, 1)

    def one_block(args):
        qb, qib, wib, pb = args
        return dsa_block(qb, qib, wib, pb, k_all, v_all, ki_all, topk)

    o = lax.map(one_block, (blocks(q), blocks(qi), blocks(wi), pos.reshape(nblk, qblk)))
    return jnp.swapaxes(o, 0, 1).reshape(B, T, N_HEADS * HEAD_DIM)


def rwkv7_time_mix(zs, shift0, wkv0, mu_shift, w0, w2, a0, a2, g2, k_k, k_a, r_k, ln_x_g, ln_x_b):
    B, T = zs.shape[0], zs.shape[1]
    f32 = jnp.float32
    prev = jnp.concatenate([shift0.astype(zs.dtype), zs[:, :-1]], axis=1)
    zm = zs + mu_shift * (prev - zs)
    r, k, v, zw, za, zg = _split(zm, (RWKV_WIDTH, RWKV_WIDTH, RWKV_WIDTH, W_LORA, A_LORA, G_LORA))
    log_w = -jax.nn.softplus(-(w0 + jnp.tanh(zw) @ w2).astype(f32)) - 0.5
    decay = jnp.exp(-jnp.exp(log_w))
    a = jax.nn.sigmoid((a0 + za @ a2).astype(f32))
    g = (jax.nn.sigmoid(zg) @ g2).astype(f32)

    def heads(t):
        return t.astype(f32).reshape(B, T, RWKV_HEADS, HEAD_DIM)

    r_h, k_h, v_h, a_h, w_h = heads(r), heads(k), heads(v), heads(a), heads(decay)
    kk = k_h * k_k.astype(f32).reshape(RWKV_HEADS, HEAD_DIM)
    kk = kk / jnp.maximum(jnp.sqrt(jnp.sum(kk * kk, axis=-1, keepdims=True)), L2_EPS)
    k_h = k_h * (1.0 + (a_h - 1.0) * k_a.astype(f32).reshape(RWKV_HEADS, HEAD_DIM))

    def step(S, inp):
        r_t, w_t, k_t, v_t, kk_t, b_t = inp
        s_kk = jnp.einsum('bhvk,bhk->bhv', S, kk_t)
        S = S * w_t[:, :, None, :] - s_kk[..., None] * b_t[:, :, None, :] + v_t[..., None] * k_t[:, :, None, :]
        return S, jnp.einsum('bhvk,bhk->bhv', S, r_t)

    def tm(t):
        return jnp.swapaxes(t, 0, 1)

    s_fin, y = lax.scan(step, wkv0.astype(f32), (tm(r_h), tm(w_h), tm(k_h), tm(v_h), tm(kk), tm(kk * a_h)))
    y = tm(y)
    mu = jnp.mean(y, axis=-1, keepdims=True)
    var = jnp.mean(jnp.square(y - mu), axis=-1, keepdims=True)
    y = ((y - mu) * lax.rsqrt(var + GN_EPS)).reshape(B, T, RWKV_WIDTH) * ln_x_g.astype(f32) + ln_x_b.astype(f32)
    bonus = jnp.sum(r_h * k_h * r_k.astype(f32), axis=-1, keepdims=True) * v_h
    o = (y + bonus.reshape(B, T, RWKV_WIDTH)) * g
    return o.astype(zs.dtype), s_fin


def hybrid_layer(x, pos, k_past, v_past, ki_past, wkv0, shift0,
                 norm_mix, w_in, q_gain, k_gain, kidx_ln_g, kidx_ln_b, mu_shift,
                 w0, w2, a0, a2, g2, k_k, k_a, r_k, ln_x_g, ln_x_b, w_out,
                 norm_ffn, w_ff1, w_ff2):
    B, T = x.shape[0], x.shape[1]
    dt = x.dtype
    xn = rms_norm(x, norm_mix)
    zq, zk, zv, ziq, zik, ziw, zs = _split(
        xn @ w_in,
        (N_HEADS * HEAD_DIM, N_KV_HEADS * HEAD_DIM, N_KV_HEADS * HEAD_DIM,
         IDX_HEADS * IDX_DIM, IDX_DIM, IDX_HEADS, SHIFT_WIDTH))
    q = rope_partial(rms_norm(zq.reshape(B, T, N_HEADS, HEAD_DIM), q_gain), pos)
    k = rope_partial(rms_norm(zk.reshape(B, T, N_KV_HEADS, HEAD_DIM), k_gain), pos)
    v = zv.reshape(B, T, N_KV_HEADS, HEAD_DIM)
    qi = rope_partial(ziq.reshape(B, T, IDX_HEADS, IDX_DIM), pos)
    ki = rope_partial(layer_norm(zik, kidx_ln_g, kidx_ln_b)[:, :, None, :], pos)[:, :, 0, :]
    wi = ziw * (IDX_HEADS ** -0.5)
    k_all = jnp.concatenate([k_past.astype(k.dtype), k], axis=1)
    v_all = jnp.concatenate([v_past.astype(v.dtype), v], axis=1)
    ki_all = jnp.concatenate([ki_past.astype(ki.dtype), ki], axis=1)
    o_att = dsa_attention(q, qi, wi, pos, k_all, v_all, ki_all)
    o_rwkv, wkv_new = rwkv7_time_mix(zs, shift0, wkv0, mu_shift, w0, w2, a0, a2, g2,
                                     k_k, k_a, r_k, ln_x_g, ln_x_b)
    h = x + (jnp.concatenate([o_att, o_rwkv.astype(o_att.dtype)], axis=-1) @ w_out).astype(dt)
    hn = rms_norm(h, norm_ffn)
    y = h + (jnp.square(jax.nn.relu(hn @ w_ff1)) @ w_ff2).astype(dt)
    return y, k, v, ki, wkv_new.astype(dt), zs[:, -1:]


def setup_inputs(seed: int = 0) -> dict:
    key = jax.random.key(seed)
    ks = iter(jax.random.split(key, 40))

    def nrm(shape, scale):
        return jax.random.normal(next(ks), shape, jnp.float32) * scale

    L = DEPTH
    return {
        'x_prompt': nrm((BATCH, SEQ, D_MODEL), 1.0),
        'x_sample': nrm((DEC_BATCH, DEC_SEQ, D_MODEL), 1.0),
        'cache_k': nrm((L, DEC_BATCH, PAST_LEN, N_KV_HEADS, HEAD_DIM), 1.0),
        'cache_v': nrm((L, DEC_BATCH, PAST_LEN, N_KV_HEADS, HEAD_DIM), 1.0),
        'cache_kidx': nrm((L, DEC_BATCH, PAST_LEN, IDX_DIM), 1.0),
        'state_wkv': nrm((L, DEC_BATCH, RWKV_HEADS, HEAD_DIM, HEAD_DIM), 0.3),
        'state_shift': nrm((L, DEC_BATCH, 1, SHIFT_WIDTH), 1.0),
        'norm_mix': 1.0 + nrm((L, D_MODEL), 0.02),
        'w_in': nrm((L, D_MODEL, PROJ_WIDTH), D_MODEL ** -0.5),
        'q_gain': 1.0 + nrm((L, HEAD_DIM), 0.02),
        'k_gain': 1.0 + nrm((L, HEAD_DIM), 0.02),
        'kidx_ln_g': 1.0 + nrm((L, IDX_DIM), 0.02),
        'kidx_ln_b': nrm((L, IDX_DIM), 0.02),
        'mu_shift': jax.random.uniform(next(ks), (L, SHIFT_WIDTH), jnp.float32),
        'w0': nrm((L, RWKV_WIDTH), 0.5),
        'w2': nrm((L, W_LORA, RWKV_WIDTH), 0.5 * W_LORA ** -0.5),
        'a0': nrm((L, RWKV_WIDTH), 0.1),
        'a2': nrm((L, A_LORA, RWKV_WIDTH), 0.5 * A_LORA ** -0.5),
        'g2': nrm((L, G_LORA, RWKV_WIDTH), G_LORA ** -0.5),
        'k_k': 0.85 + nrm((L, RWKV_WIDTH), 0.05),
        'k_a': 1.0 + nrm((L, RWKV_WIDTH), 0.05),
        'r_k': nrm((L, RWKV_HEADS, HEAD_DIM), 0.1),
        'ln_x_g': 1.0 + nrm((L, RWKV_WIDTH), 0.02),
        'ln_x_b': nrm((L, RWKV_WIDTH), 0.02),
        'w_out': nrm((L, D_MODEL, D_MODEL), D_MODEL ** -0.5),
        'norm_ffn': 1.0 + nrm((L, D_MODEL), 0.02),
        'w_ff1': nrm((L, D_MODEL, D_FF), D_MODEL ** -0.5),
        'w_ff2': nrm((L, D_FF, D_MODEL), D_FF ** -0.5),
    }


def reference(x_prompt, x_sample, cache_k, cache_v, cache_kidx, state_wkv, state_shift,
              norm_mix, w_in, q_gain, k_gain, kidx_ln_g, kidx_ln_b, mu_shift,
              w0, w2, a0, a2, g2, k_k, k_a, r_k, ln_x_g, ln_x_b, w_out,
              norm_ffn, w_ff1, w_ff2):
    B, T = x_prompt.shape[0], x_prompt.shape[1]
    dt = x_prompt.dtype
    pos_p = jnp.arange(T, dtype=jnp.int32)
    pos_s = cache_k.shape[2] + jnp.arange(x_sample.shape[1], dtype=jnp.int32)
    no_k = jnp.zeros((B, 0, N_KV_HEADS, HEAD_DIM), dt)
    no_ki = jnp.zeros((B, 0, IDX_DIM), dt)
    wkv_zero = jnp.zeros((B, RWKV_HEADS, HEAD_DIM, HEAD_DIM), jnp.float32)
    shift_zero = jnp.zeros((B, 1, SHIFT_WIDTH), dt)

    y_p, y_s = x_prompt, x_sample
    kp, vp, kip, sp, shp = [], [], [], [], []
    ks_, vs_, kis, ss, shs = [], [], [], [], []
    for l in range(DEPTH):
        wl = (norm_mix[l], w_in[l], q_gain[l], k_gain[l], kidx_ln_g[l], kidx_ln_b[l], mu_shift[l],
              w0[l], w2[l], a0[l], a2[l], g2[l], k_k[l], k_a[l], r_k[l], ln_x_g[l], ln_x_b[l],
              w_out[l], norm_ffn[l], w_ff1[l], w_ff2[l])
        y_p, k1, v1, ki1, s1, sh1 = hybrid_layer(y_p, pos_p, no_k, no_k, no_ki, wkv_zero, shift_zero, *wl)
        y_s, k2, v2, ki2, s2, sh2 = hybrid_layer(y_s, pos_s, cache_k[l], cache_v[l], cache_kidx[l],
                                                 state_wkv[l], state_shift[l], *wl)
        kp.append(k1); vp.append(v1); kip.append(ki1); sp.append(s1); shp.append(sh1)
        ks_.append(k2); vs_.append(v2); kis.append(ki2); ss.append(s2); shs.append(sh2)
    return (y_p, y_s,
            jnp.stack(kp), jnp.stack(vp), jnp.stack(kip), jnp.stack(sp), jnp.stack(shp),
            jnp.stack(ks_), jnp.stack(vs_), jnp.stack(kis), jnp.stack(ss), jnp.stack(shs))
```

```python
import numpy as np
from contextlib import ExitStack
import concourse.bass as bass
import concourse.mybir as mybir
from concourse.bass_utils import run_bass_kernel_spmd

F32 = mybir.dt.float32
BF16 = mybir.dt.bfloat16
ALU = mybir.AluOpType
AF = mybir.ActivationFunctionType
AX = mybir.AxisListType

D = 1024
NCORE = 8
SEQ = 2048
DEC_B = 32
DEC_T = 32
PAST = 1024
NPT = SEQ // 128
NT = NPT + 1
NTOK = NT * 128
HD = 64
PROJ = 3304
SW = 1696
NA = 1608
DFF = 4096
TOPK = 256
NORM_EPS = 1e-6
GN_EPS = 64e-5


class Tok:
    __slots__ = ("name", "w", "rd", "rd_dma")

    def __init__(self, name):
        self.name = name
        self.w = None
        self.rd = {}
        self.rd_dma = []


class Op:
    __slots__ = ("eng", "fn", "deps", "signal", "dma", "sig", "clock")

    def __init__(self, eng, fn, dma):
        self.eng = eng
        self.fn = fn
        self.deps = set()
        self.signal = False
        self.dma = dma
        self.sig = None
        self.clock = None


class Prog:
    NSLOT = 40

    def __init__(self, nc, es):
        self.nc = nc
        self.E = {"pe": nc.tensor, "dve": nc.vector, "act": nc.scalar, "pool": nc.gpsimd, "sp": nc.sync}
        self.ops = []
        self.sems = {e: es.enter_context(nc.semaphore("sem_" + e)) for e in self.E}
        self.slots = [es.enter_context(nc.semaphore("dsl%d" % i)) for i in range(self.NSLOT)]
        self.dma_ops = []
        self.bar_toks = []
        self.tiny = {}

    def tok(self, name="t"):
        return Tok(name)

    def toks(self, n, name="t"):
        return [Tok(name + str(i)) for i in range(n)]

    def _record(self, eng, fn, r, w, dma):
        idx = len(self.ops)
        op = Op(eng, fn, dma)

        def add(d, kind):
            if d is None:
                return
            dop = self.ops[d]
            if (not dma) and (not dop.dma) and dop.eng == eng:
                if eng == "pe":
                    return
            op.deps.add(d)
            dop.signal = True

        for t in r:
            add(t.w, "raw")
        for t in w:
            add(t.w, "waw")
            for d in t.rd.values():
                add(d, "war")
            for d in t.rd_dma:
                add(d, "war")
        for t in w:
            t.w = idx
            t.rd = {}
            t.rd_dma = []
        for t in r:
            if dma:
                t.rd_dma.append(idx)
            else:
                t.rd[eng] = idx
        self.ops.append(op)
        if dma:
            self.dma_ops.append(idx)
            op.signal = True
        return idx

    def op(self, eng, fn, r=(), w=()):
        return self._record(eng, fn, r, w, False)

    def dma(self, q, fn, r=(), w=()):
        return self._record(q, fn, list(r) + self.bar_toks, w, True)

    def barrier(self):
        CE = ["pe", "dve", "act", "pool"]
        bt = {e: Tok("bar_" + e) for e in CE}
        pend = list(self.dma_ops)
        self.dma_ops = []
        for e in CE:
            fn, xr, xw = self.tiny[e]
            i = self.op(e, fn, r=xr, w=[bt[e]] + xw)
            if e == "act":
                for d in pend:
                    self.ops[i].deps.add(d)
        bt2 = {e: Tok("bar2_" + e) for e in CE}
        for e in CE:
            fn, xr, xw = self.tiny[e]
            self.op(e, fn, r=list(bt.values()) + xr, w=[bt2[e]] + xw)
        self.bar_toks = list(bt.values())

    def emit(self):
        clock = {e: {} for e in self.E}
        count = {e: 0 for e in self.E}
        slot_uses = [0] * self.NSLOT
        nslot = 0
        nwait = 0
        for op in self.ops:
            e = op.eng
            eng = self.E[e]
            ck = clock[e]
            deps = sorted(op.deps, reverse=True)
            for d in deps:
                dop = self.ops[d]
                key, val = dop.sig
                if ck.get(key, 0) >= val:
                    continue
                sem = self.sems[key] if isinstance(key, str) else self.slots[key]
                eng.wait_ge(sem, val)
                nwait += 1
                for k2, v2 in dop.clock.items():
                    if ck.get(k2, 0) < v2:
                        ck[k2] = v2
            if op.dma:
                j = nslot % self.NSLOT
                nslot += 1
                prev = 16 * slot_uses[j]
                if ck.get(j, 0) < prev:
                    eng.wait_ge(self.slots[j], prev)
                    ck[j] = prev
                ins = op.fn()
                ins.then_inc(self.slots[j], 16)
                slot_uses[j] += 1
                op.sig = (j, 16 * slot_uses[j])
                op.clock = dict(ck)
                op.clock[j] = op.sig[1]
            else:
                ins = op.fn()
                if op.signal:
                    count[e] += 1
                    ins.then_inc(self.sems[e], 1)
                    op.sig = (e, count[e])
                    op.clock = dict(ck)
                    op.clock[e] = count[e]
        for j in range(self.NSLOT):
            if slot_uses[j]:
                self.nc.sync.wait_ge(self.slots[j], 16 * slot_uses[j])
        return dict(nops=len(self.ops), nwait=nwait, count=count)


class _Stop(Exception):
    pass


def build_program(stage=3, stop=0, bar_mode=3):
    nc = bass.Bass("TRN2", target_bir_lowering=False)

    def din(name, shape):
        return nc.dram_tensor(name, list(shape), F32, kind="ExternalInput").ap()

    def dout(name, shape):
        return nc.dram_tensor(name, list(shape), F32, kind="ExternalOutput").ap()

    x_all = din("x_all", [NTOK, D])
    ck_in = din("ck", [4, PAST, 256])
    cv_in = din("cv", [4, PAST, 256])
    cki_in = din("cki", [4, PAST, 64])
    swkv_in = din("swkv", [4, 8, 64, 64])
    ssh_in = din("ssh", [4, SW])
    w_in = din("w_in", [D, PROJ])
    w_out = din("w_out", [D, D])
    w_ff1 = din("w_ff1", [D, DFF])
    w_ff2 = din("w_ff2", [DFF, D])
    norm_mix = din("norm_mix", [D])
    norm_ffn = din("norm_ffn", [D])
    q_gain = din("q_gain", [64])
    k_gain = din("k_gain", [64])
    kln_g = din("kidx_ln_g", [64])
    kln_b = din("kidx_ln_b", [64])
    mu_in = din("mu_shift", [SW])
    w0_in = din("w0", [512])
    w2_in = din("w2", [32, 512])
    a0_in = din("a0", [512])
    a2_in = din("a2", [32, 512])
    g2_in = din("g2", [96, 512])
    kk_in = din("k_k", [512])
    ka_in = din("k_a", [512])
    rk_in = din("r_k", [512])
    lng_in = din("ln_x_g", [512])
    lnb_in = din("ln_x_b", [512])
    c_ident = din("c_ident", [128, 128])
    c_rope = din("c_rope", [128, NT, 16])
    c_masks = din("c_masks", [128, 8, 128])
    c_small = din("c_small", [128, 16])
    c_colm = din("c_colm", [64, 6, 128])
    c_sels = din("c_sels", [4, 128])

    y_all = dout("y_all", [NTOK, D])
    k_all = dout("k_all", [NTOK, 256])
    v_all = dout("v_all", [NTOK, 256])
    ki_all = dout("ki_all", [NTOK, 64])
    wkv_p = dout("wkv_p", [8, 64, 64])
    wkv_s = dout("wkv_s", [4, 8, 64, 64])
    sh_p = dout("sh_p", [1, SW])
    sh_s = dout("sh_s", [4, SW])
    o_scr = nc.dram_tensor("o_scr", [NT, 128, 8, 128], BF16, kind="Internal").ap()

    es = ExitStack()

    def ckpt(n):
        if stop == n:
            raise _Stop()

    with es:
      P = Prog(nc, es)
      try:

        def sb(name, shape, dt=F32):
            return es.enter_context(nc.sbuf_tensor(name, list(shape), dt))

        def ps(name, shape, dt=F32):
            return es.enter_context(nc.psum_tensor(name, list(shape), dt))

        V, A, PL, PE, SP = "dve", "act", "pool", "pe", "sp"

        V, A, PL, PE, SP = "dve", "act", "pool", "pe", "sp"
        STAGE = stage
        BAR_MODE = bar_mode
        ATT = (bar_mode & 4) == 0

        ident_f = sb("ident_f", [128, 128])
        ident_b = sb("ident_b", [128, 128], BF16)
        rope = sb("rope", [128, NT, 16])
        gmix = sb("gmix", [128, 8])
        gffn = sb("gffn", [128, 8])
        qg_bc = sb("qg_bc", [128, 64])
        kg_bc = sb("kg_bc", [128, 64])
        lg_bc = sb("lg_bc", [128, 64])
        lb_bc = sb("lb_bc", [128, 64])
        zero_b = sb("zero_b", [128, 512], BF16)
        t_const = P.tok("const")

        def bcast_load(dst, src, t=None):
            P.dma(SP, lambda: nc.sync.dma_start(out=dst[:], in_=src.partition_broadcast(128)), w=[t or t_const])

        P.dma(SP, lambda: nc.sync.dma_start(out=ident_f[:], in_=c_ident[:, :]), w=[t_const])
        P.dma(SP, lambda: nc.sync.dma_start(out=rope[:], in_=c_rope[:, :, :]), w=[t_const])
        P.dma(SP, lambda: nc.sync.dma_start(out=gmix[:], in_=norm_mix.rearrange("(c p) -> p c", p=128), allow_slow_non_contiguous=True), w=[t_const])
        P.dma(SP, lambda: nc.sync.dma_start(out=gffn[:], in_=norm_ffn.rearrange("(c p) -> p c", p=128), allow_slow_non_contiguous=True), w=[t_const])
        bcast_load(qg_bc, q_gain)
        bcast_load(kg_bc, k_gain)
        bcast_load(lg_bc, kln_g)
        bcast_load(lb_bc, kln_b)
        t_const2 = P.tok("const2")
        P.op(V, lambda: nc.vector.tensor_copy(out=ident_b[:], in_=ident_f[:]), r=[t_const], w=[t_const2])
        P.op(V, lambda: nc.vector.memset(zero_b[:], 0.0), w=[t_const2])
        eps_c = sb("eps_c", [128, 2])
        P.op(V, lambda: nc.vector.memset(eps_c[:, 0:1], NORM_EPS), w=[t_const2])
        P.op(V, lambda: nc.vector.memset(eps_c[:, 1:2], GN_EPS), w=[t_const2])
        CONST = [t_const, t_const2]
        pbank = [ps("pb%d" % i, [128, 512]) for i in range(8)]
        t_pb = P.toks(8, "pb")

        def pb_bf(i):
            return pbank[i][:].bitcast(BF16)

        ckpt(1)
        bscr_v = sb("bscr_v", [128, 2])
        bscr_a = sb("bscr_a", [128, 2])
        bscr_p = sb("bscr_p", [128, 2])
        t_bsv, t_bsa, t_bsp = P.tok("bsv"), P.tok("bsa"), P.tok("bsp")
        P.tiny = {
            "dve": (lambda: nc.vector.memset(bscr_v[:, 0:1], 0.0), [], [t_bsv]),
            "act": (lambda: nc.scalar.activation(out=bscr_a[:, 0:1], in_=eps_c[:, 0:1], func=AF.Copy), CONST, [t_bsa]),
            "pool": (lambda: nc.gpsimd.memset(bscr_p[:, 0:1], 0.0), [], [t_bsp]),
            "pe": (lambda: nc.tensor.transpose(pb_bf(7)[0:1, 0:1], ident_b[0:1, 0:1], ident_b[0:1, 0:1]), CONST, [t_pb[7]]),
        }
        def norm_tile(ti, xtb, t_x, junk, t_junk, xn, t_xn, stat, xT, t_xT, st_):
            r0 = ti * 128
            P.dma(SP, lambda: nc.sync.dma_start(out=xtb[:], in_=x_all[r0:r0 + 128, :]), w=[t_x])
            P.op(A, lambda: nc.scalar.activation(out=junk[:], in_=xtb[:], func=AF.Square, accum_out=stat[:, 0:1]),
                 r=[t_x], w=[t_junk, st_])
            P.op(A, lambda: nc.scalar.activation(out=stat[:, 1:2], in_=stat[:, 0:1], func=AF.Ln, scale=1.0 / D, bias=eps_c[:, 0:1]), r=[st_] + CONST, w=[st_])
            P.op(A, lambda: nc.scalar.activation(out=stat[:, 2:3], in_=stat[:, 1:2], func=AF.Exp, scale=-0.5), r=[st_], w=[st_])
            P.op(V, lambda: nc.vector.tensor_scalar(out=xn[:], in0=xtb[:], scalar1=stat[:, 2:3], scalar2=None, op0=ALU.mult),
                 r=[t_x, st_], w=[t_xn])
            pbT = pb_bf(7)
            for c in range(8):
                P.op(PE, lambda c=c: nc.tensor.transpose(pbT[:, c * 128:(c + 1) * 128], xn[:, c * 128:(c + 1) * 128], ident_b[:]),
                     r=[t_xn] + CONST, w=[t_pb[7]])
            P.op(V, lambda: nc.vector.tensor_tensor(
                out=xT[:], in0=pbT.rearrange("p (c t) -> p c t", c=8),
                in1=gmix[:].unsqueeze(2).broadcast_to([128, 8, 128]), op=ALU.mult), r=[t_pb[7]] + CONST, w=[t_xT])


        phA = ExitStack()
        es.enter_context(phA)

        def sbA(name, shape, dt=F32):
            return phA.enter_context(nc.sbuf_tensor(name, list(shape), dt))

        WA = sbA("WA", [128, 8, NA], BF16)
        t_W = P.tok("W")
        with ExitStack() as st:
            stg = [st.enter_context(nc.sbuf_tensor("wstg%d" % i, [128, NA], F32)) for i in range(2)]
            t_stg = P.toks(2, "wstg")
            for c in range(8):
                b = c % 2
                P.dma(SP, lambda c=c, b=b: nc.sync.dma_start(out=stg[b][:], in_=w_in[c * 128:(c + 1) * 128, 0:NA]), w=[t_stg[b]])
                P.op(A, lambda c=c, b=b: nc.scalar.activation(out=WA[:, c, 0:800], in_=stg[b][:, 0:800], func=AF.Copy), r=[t_stg[b]], w=[t_W])
                P.op(V, lambda c=c, b=b: nc.vector.tensor_copy(out=WA[:, c, 800:NA], in_=stg[b][:, 800:NA]), r=[t_stg[b]], w=[t_W])
            P.barrier()

        ckpt(2)
        xt = [sbA("xt%d" % i, [128, D]) for i in range(2)]
        t_xt = P.toks(2, "xt")
        junk = sbA("junk", [128, D], BF16)
        t_junk = P.tok("junk")
        xn = sbA("xn", [128, D], BF16)
        t_xn = P.tok("xn")
        xnT = [sbA("xnT%d" % i, [128, 8, 128], BF16) for i in range(2)]
        t_xnT = P.toks(2, "xnT")
        stat = sbA("stat", [128, 64])
        t_statA = P.tok("statA")
        t_st2 = P.tok("st2")
        RB = sbA("RB", [128, 21, 64])
        t_RB = P.tok("RB")
        vf = sbA("vf", [128, 256])
        t_vf = P.tok("vf")
        sq = sbA("sq", [128, 768])
        t_sq = P.tok("sq")
        wi_abs = sbA("wi_abs", [128, 8])
        wi_sgn = sbA("wi_sgn", [128, 8])
        t_wi = P.tok("wi")
        oatt = sbA("oatt", [128, 4, 128], BF16)
        t_oatt = P.tok("oatt")

        LKMAX = 2048
        KT = sbA("KT", [64, 4, LKMAX], BF16)
        KiT = sbA("KiT", [64, LKMAX], BF16)
        Vx = sbA("Vx", [128, 16, 4, 65], BF16)
        t_KT = P.toks(17, "KT")
        t_KiT = P.toks(17, "KiT")
        t_Vx = P.toks(17, "Vx")
        RBb = sbA("RBb", [128, 21, 64], BF16)
        t_RBb = P.tok("RBb")
        QT = sbA("QT", [64, 8, 128], BF16)
        QiT = sbA("QiT", [64, 8, 128], BF16)
        KTn = sbA("KTn", [64, 5, 128], BF16)
        t_QT, t_QiT, t_KTn = P.tok("QT"), P.tok("QiT"), P.tok("KTn")
        score = sbA("score", [128, LKMAX])
        t_score = P.tok("score")
        rl = [sbA("rl%d" % i, [128, 512], BF16) for i in range(2)]
        t_rl = P.toks(2, "rl")
        maskb = sbA("maskb", [128, LKMAX], BF16)
        t_mask = P.tok("mask")
        maskT = sbA("maskT", [128, 16, 128], BF16)
        t_maskT = P.tok("maskT")
        PTall = sbA("PTall", [128, 16, 8, 128], BF16)
        t_PTk = P.toks(16, "PTk")
        PTm = [sbA("PTm%d" % i, [128, 8, 128], BF16) for i in range(2)]
        t_PTm = P.toks(2, "PTm")
        bs = sbA("bs", [128, 64])
        dsg = sbA("dsg", [128, 8, 128], BF16)
        t_dsg = P.tok("dsg")
        sgn_s = sbA("sgn_s", [32, 8])
        t_sgn_s = P.tok("sgn_s")
        t_bs = P.tok("bs")
        pw2 = sbA("pw2", [128, 20])
        osb = sbA("osb", [128, 512], BF16)
        t_osb = P.tok("osb")
        kstg = sbA("kstg", [128, 8, 256])
        t_kstg = P.tok("kstg")
        kstb = sbA("kstb", [128, 8, 256], BF16)
        t_kstb = P.tok("kstb")
        NBIS = 16
        for j in range(NBIS):
            P.op(V, lambda j=j: nc.vector.memset(pw2[:, j:j + 1], 0.5 ** (j + 1)), w=[t_const2])
        P.op(V, lambda: nc.vector.memset(Vx[:], 1.0), w=t_Vx)

        def attn_unit(nq, qc0, L, kt_toks, ki_toks, v_toks, static_mask, causal_tail, out_col0, sgn, t_sgn):
            nkt = (L + 127) // 128
            tb = t_bs
            def emit_st_exp():
                for kt in range(nkt):
                    nk = min(128, L - kt * 128)
                    for hb in range(2):
                        for gg in range(2):
                            g = hb * 2 + gg
                            P.op(PE, lambda g=g, gg=gg, hb=hb, kt=kt, nk=nk: nc.tensor.matmul(
                                pbank[hb][0:nk, gg * 2 * nq:(gg + 1) * 2 * nq].rearrange("p (a q) -> p a q", a=2),
                                lhsT=KT[:, g, kt * 128:kt * 128 + nk], rhs=QT[:, 2 * g:2 * g + 2, qc0:qc0 + nq], start=True, stop=True),
                                r=[t_QT] + kt_toks, w=[t_pb[hb]])
                        P.op(A, lambda hb=hb, nk=nk, kt=kt: nc.scalar.activation(
                            out=PTall[0:nk, kt, hb * 4:hb * 4 + 4, 0:nq], in_=pbank[hb][0:nk, 0:4 * nq].rearrange("p (a q) -> p a q", a=4),
                            func=AF.Exp, scale=0.125), r=[t_pb[hb]], w=[t_PTk[kt]])

            if not static_mask:
                for h in range(8):
                    P.op(V, lambda h=h: nc.vector.tensor_scalar(out=dsg[0:nq, h, 0:nq], in0=ident_b[0:nq, 0:nq], scalar1=sgn[0:nq, h:h + 1],
                                                                scalar2=None, op0=ALU.mult), r=[t_sgn] + CONST, w=[t_dsg])
                cnt = 0
                for kc in range(0, L, 512):
                    n = min(512, L - kc)

                    def acc(h, rb, n=n):
                        P.op(PE, lambda: nc.tensor.matmul(pbank[6][0:nq, 0:n], lhsT=dsg[0:nq, h, 0:nq], rhs=rl[rb][0:nq, 0:n],
                                                          start=(h == 0), stop=(h == 7)), r=[t_dsg, t_rl[rb]], w=[t_pb[6]])

                    prev = None
                    for h in range(8):
                        bk = 4 + cnt % 2
                        rb = cnt % 2
                        cnt += 1
                        P.op(PE, lambda h=h, kc=kc, n=n, bk=bk: nc.tensor.matmul(
                            pbank[bk][0:nq, 0:n], lhsT=QiT[:, h, qc0:qc0 + nq], rhs=KiT[:, kc:kc + n], start=True, stop=True),
                            r=[t_QiT] + ki_toks, w=[t_pb[bk]])
                        P.op(A, lambda n=n, bk=bk, rb=rb: nc.scalar.activation(out=rl[rb][0:nq, 0:n], in_=pbank[bk][0:nq, 0:n], func=AF.Relu),
                             r=[t_pb[bk]], w=[t_rl[rb]])
                        if prev is not None:
                            acc(*prev)
                        prev = (h, rb)
                    acc(*prev)
                    P.op(A, lambda kc=kc, n=n: nc.scalar.activation(out=score[0:nq, kc:kc + n], in_=pbank[6][0:nq, 0:n], func=AF.Copy),
                         r=[t_pb[6]], w=[t_score])
                emit_st_exp()
                P.op(V, lambda: nc.vector.tensor_reduce(out=bs[0:nq, 0:1], in_=score[0:nq, 0:L], axis=AX.X, op=ALU.min), r=[t_score], w=[tb])
                P.op(V, lambda: nc.vector.tensor_reduce(out=bs[0:nq, 1:2], in_=score[0:nq, 0:L], axis=AX.X, op=ALU.max), r=[t_score], w=[tb])
                if causal_tail:
                    P.op(V, lambda: nc.vector.memset(score[0:64, L - 64:L], -1e30), w=[t_score])
                P.op(V, lambda: nc.vector.tensor_tensor(out=bs[0:nq, 2:3], in0=bs[0:nq, 1:2], in1=bs[0:nq, 0:1], op=ALU.subtract), r=[tb], w=[tb])
                P.op(V, lambda: nc.vector.tensor_scalar(out=bs[0:nq, 8:8 + NBIS], in0=pw2[0:nq, 0:NBIS], scalar1=bs[0:nq, 2:3], scalar2=None, op0=ALU.mult),
                     r=[tb] + CONST, w=[tb])
                P.op(V, lambda: nc.vector.tensor_tensor(out=bs[0:nq, 3:4], in0=bs[0:nq, 0:1], in1=bs[0:nq, 8:9], op=ALU.add), r=[tb], w=[tb])
                for j in range(NBIS):
                    P.op(V, lambda: nc.vector.tensor_scalar(out=maskb[0:nq, 0:L], in0=score[0:nq, 0:L], scalar1=bs[0:nq, 3:4], scalar2=None,
                                                            op0=ALU.is_ge, op1=ALU.add, accum_out=bs[0:nq, 4:5]), r=[tb, t_score], w=[t_mask, tb])
                    if j < NBIS - 1:
                        P.op(V, lambda j=j: nc.vector.tensor_scalar(out=bs[0:nq, 5:6], in0=bs[0:nq, 4:5], scalar1=TOPK - 0.5, scalar2=bs[0:nq, 8 + j:9 + j],
                                                                    op0=ALU.is_ge, op1=ALU.mult), r=[tb], w=[tb])
                        P.op(V, lambda j=j: nc.vector.scalar_tensor_tensor(out=bs[0:nq, 3:4], in0=bs[0:nq, 5:6], scalar=bs[0:nq, 9 + j:10 + j], in1=bs[0:nq, 3:4],
                                                                           op0=ALU.subtract, op1=ALU.add), r=[tb], w=[tb])
                P.op(V, lambda: nc.vector.tensor_tensor(out=bs[0:nq, 0:1], in0=bs[0:nq, 3:4], in1=bs[0:nq, 8 + NBIS - 1:8 + NBIS], op=ALU.subtract), r=[tb], w=[tb])
                P.op(V, lambda: nc.vector.tensor_scalar(out=maskb[0:nq, 0:L], in0=score[0:nq, 0:L], scalar1=bs[0:nq, 0:1], scalar2=None, op0=ALU.is_ge),
                     r=[tb, t_score], w=[t_mask])
            else:
                emit_st_exp()
                P.op(V, lambda: nc.vector.memset(maskb[0:nq, 0:L], 1.0), w=[t_mask])
                if causal_tail:
                    P.op(V, lambda: nc.vector.memset(maskb[0:64, L - 64:L], 0.0), w=[t_mask])
            for g0 in range(0, nkt, 8):
                pbm = pb_bf(6)
                ks = list(range(g0, min(nkt, g0 + 8)))
                for kt in ks:
                    nk = min(128, L - kt * 128)
                    P.op(PE, lambda kt=kt, nk=nk, g0=g0, pbm=pbm: nc.tensor.transpose(
                        pbm[0:nk, (kt - g0) * 128:(kt - g0) * 128 + nq], maskb[0:nq, kt * 128:kt * 128 + nk], ident_b[0:nq, 0:nq]),
                        r=[t_mask] + CONST, w=[t_pb[6]])
                nfull = len([kt for kt in ks if L - kt * 128 >= 128])
                if nfull:
                    P.op(A, lambda g0=g0, nfull=nfull, pbm=pbm: nc.scalar.activation(
                        out=maskT[:, g0:g0 + nfull, 0:nq], in_=pbm[:, 0:nfull * 128].rearrange("p (k q) -> p k q", q=128)[:, :, 0:nq], func=AF.Copy),
                        r=[t_pb[6]], w=[t_maskT])
                if nfull < len(ks):
                    kt = ks[-1]
                    nk = L - kt * 128
                    P.op(A, lambda kt=kt, nk=nk, g0=g0, pbm=pbm: nc.scalar.activation(
                        out=maskT[0:nk, kt, 0:nq], in_=pbm[0:nk, (kt - g0) * 128:(kt - g0) * 128 + nq], func=AF.Copy), r=[t_pb[6]], w=[t_maskT])
            for ob in (2, 3):
                P.op(PE, lambda ob=ob: nc.tensor.matmul(pbank[ob][0:nq, 0:260], lhsT=zero_b[0:1, 0:nq], rhs=zero_b[0:1, 0:260], start=True, stop=False),
                     r=CONST, w=[t_pb[ob]])
            for kt in range(nkt):
                nk = min(128, L - kt * 128)
                pb_ = kt % 2
                eng, en = (nc.vector, V)
                P.op(en, lambda eng=eng, nk=nk, pb_=pb_, kt=kt: eng.tensor_tensor(
                    out=PTm[pb_][0:nk, :, 0:nq], in0=PTall[0:nk, kt, :, 0:nq],
                    in1=maskT[0:nk, kt, 0:nq].unsqueeze(1).broadcast_to([nk, 8, nq]), op=ALU.mult), r=[t_PTk[kt], t_maskT], w=[t_PTm[pb_]])
                for h in range(8):
                    ob = 2 + h // 4
                    P.op(PE, lambda h=h, ob=ob, kt=kt, nk=nk, pb_=pb_: nc.tensor.matmul(
                        pbank[ob][0:nq, (h % 4) * 65:(h % 4) * 65 + 65], lhsT=PTm[pb_][0:nk, h, 0:nq], rhs=Vx[0:nk, kt, h // 2, :],
                        start=False, stop=(kt == nkt - 1 and h % 4 == 3)), r=[t_PTm[pb_]] + v_toks, w=[t_pb[ob]])
            to = t_bs
            for ob in range(2):
                ov = pbank[2 + ob][0:nq, 0:260].rearrange("p (h d) -> p h d", d=65)
                P.op(V, lambda ov=ov, ob=ob: nc.vector.reciprocal(out=bs[0:nq, 40 + 4 * ob:44 + 4 * ob], in_=ov[:, :, 64]), r=[t_pb[2 + ob]], w=[to])
                P.op(V, lambda ov=ov, ob=ob: nc.vector.tensor_tensor(
                    out=osb[0:nq, ob * 256:(ob + 1) * 256].rearrange("p (h d) -> p h d", d=64), in0=ov[:, :, 0:64],
                    in1=bs[0:nq, 40 + 4 * ob:44 + 4 * ob].unsqueeze(2).broadcast_to([nq, 4, 64]), op=ALU.mult), r=[t_pb[2 + ob], to], w=[t_osb])
            pbo = pb_bf(7)
            for c in range(4):
                P.op(PE, lambda c=c, pbo=pbo: nc.tensor.transpose(pbo[:, c * 128:c * 128 + nq], osb[0:nq, c * 128:(c + 1) * 128], ident_b[0:nq, 0:nq]),
                     r=[t_osb] + CONST, w=[t_pb[7]])
            P.op(A, lambda pbo=pbo: nc.scalar.activation(out=oatt[:, :, out_col0:out_col0 + nq],
                                                         in_=pbo[:, 0:512].rearrange("p (c q) -> p c q", c=4)[:, :, 0:nq], func=AF.Copy),
                 r=[t_pb[7]], w=[t_oatt])

        def attention_tile(ti):
            sample = ti == NPT
            P.op(V, lambda: nc.vector.tensor_copy(out=RBb[:, 0:12, :], in_=RB[:, 0:12, :]), r=[t_RB], w=[t_RBb])
            P.op(V, lambda: nc.vector.tensor_copy(out=RBb[:, 20, :], in_=RB[:, 20, :]), r=[t_RB], w=[t_RBb])
            P.op(V, lambda: nc.vector.tensor_tensor(out=RBb[:, 12:20, :], in0=RB[:, 12:20, :],
                                                     in1=wi_abs[:].unsqueeze(2).broadcast_to([128, 8, 64]), op=ALU.mult), r=[t_RB, t_wi], w=[t_RBb])
            pq, pqi, pk = pb_bf(7), pb_bf(6), pb_bf(5)
            for h in range(8):
                P.op(PE, lambda h=h: nc.tensor.transpose(pq[0:64, h * 128:(h + 1) * 128], RBb[:, h, :], ident_b[:]), r=[t_RBb] + CONST, w=[t_pb[7]])
            P.op(A, lambda: nc.scalar.activation(out=QT[:], in_=pq[0:64, :].rearrange("p (h t) -> p h t", h=8), func=AF.Copy), r=[t_pb[7]], w=[t_QT])
            for h in range(8):
                P.op(PE, lambda h=h: nc.tensor.transpose(pqi[0:64, h * 128:(h + 1) * 128], RBb[:, 12 + h, :], ident_b[:]), r=[t_RBb] + CONST, w=[t_pb[6]])
            P.op(V, lambda: nc.vector.tensor_copy(out=QiT[:], in_=pqi[0:64, :].rearrange("p (h t) -> p h t", h=8)), r=[t_pb[6]], w=[t_QiT])
            for g in range(5):
                src = 8 + g if g < 4 else 20
                P.op(PE, lambda g=g, src=src: nc.tensor.transpose(pk[0:64, g * 128:(g + 1) * 128], RBb[:, src, :], ident_b[:]), r=[t_RBb] + CONST, w=[t_pb[5]])
            if not sample:
                r0 = ti * 128
                P.op(A, lambda r0=r0: nc.scalar.activation(out=KT[:, :, r0:r0 + 128], in_=pk[0:64, 0:512].rearrange("p (g t) -> p g t", g=4), func=AF.Copy),
                     r=[t_pb[5]], w=[t_KT[ti]])
                P.op(A, lambda r0=r0: nc.scalar.activation(out=KiT[:, r0:r0 + 128], in_=pk[0:64, 512:640], func=AF.Copy), r=[t_pb[5]], w=[t_KiT[ti]])
                P.op(V, lambda ti=ti: nc.vector.tensor_copy(out=Vx[:, ti, :, 0:64], in_=vf[:].rearrange("p (g d) -> p g d", g=4)), r=[t_vf], w=[t_Vx[ti]])
                L = (ti + 1) * 128
                attn_unit(128, 0, L, t_KT[0:ti + 1], t_KiT[0:ti + 1], t_Vx[0:ti + 1], static_mask=(L <= TOPK), causal_tail=True, out_col0=0, sgn=wi_sgn, t_sgn=t_wi)
            else:
                P.op(A, lambda: nc.scalar.activation(out=KTn[:], in_=pk[0:64, 0:640].rearrange("p (g t) -> p g t", g=5), func=AF.Copy), r=[t_pb[5]], w=[t_KTn])
                for s in range(4):
                    allk = t_KT + t_KiT + t_Vx
                    P.dma(SP, lambda s=s: nc.sync.dma_start(out=kstg[:], in_=ck_in[s].rearrange("(k p) c -> p k c", p=128)), w=[t_kstg])
                    P.op(V, lambda: nc.vector.tensor_copy(out=kstb[:], in_=kstg[:]), r=[t_kstg], w=[t_kstb])
                    for g in range(4):
                        pkc = pb_bf(4 + g % 2)
                        for kt in range(8):
                            P.op(PE, lambda g=g, kt=kt, pkc=pkc: nc.tensor.transpose(pkc[0:64, kt * 128:(kt + 1) * 128], kstb[:, kt, g * 64:(g + 1) * 64], ident_b[:]),
                                 r=[t_kstb] + CONST, w=[t_pb[4 + g % 2]])
                        P.op(A, lambda g=g, pkc=pkc: nc.scalar.activation(out=KT[:, g, 0:1024], in_=pkc[0:64, :], func=AF.Copy), r=[t_pb[4 + g % 2]], w=allk)
                    P.op(V, lambda s=s: nc.vector.tensor_copy(out=KT[:, :, 1024:1056], in_=KTn[:, 0:4, 32 * s:32 * s + 32]), r=[t_KTn], w=allk)
                    P.dma(SP, lambda s=s: nc.sync.dma_start(out=kstg[:, :, 0:64], in_=cki_in[s].rearrange("(k p) c -> p k c", p=128)), w=[t_kstg])
                    P.op(V, lambda: nc.vector.tensor_copy(out=kstb[:, :, 0:64], in_=kstg[:, :, 0:64]), r=[t_kstg], w=[t_kstb])
                    pkc = pb_bf(4)
                    for kt in range(8):
                        P.op(PE, lambda kt=kt, pkc=pkc: nc.tensor.transpose(pkc[0:64, kt * 128:(kt + 1) * 128], kstb[:, kt, 0:64], ident_b[:]),
                             r=[t_kstb] + CONST, w=[t_pb[4]])
                    P.op(A, lambda pkc=pkc: nc.scalar.activation(out=KiT[:, 0:1024], in_=pkc[0:64, :], func=AF.Copy), r=[t_pb[4]], w=allk)
                    P.op(V, lambda s=s: nc.vector.tensor_copy(out=KiT[:, 1024:1056], in_=KTn[:, 4, 32 * s:32 * s + 32]), r=[t_KTn], w=allk)
                    P.dma(SP, lambda s=s: nc.sync.dma_start(out=kstg[:], in_=cv_in[s].rearrange("(k p) c -> p k c", p=128)), w=[t_kstg])
                    P.op(PL, lambda: nc.gpsimd.tensor_copy(out=Vx[:, 0:8, :, 0:64], in_=kstg[:].rearrange("p k (g d) -> p k g d", g=4)), r=[t_kstg], w=allk)
                    P.dma(SP, lambda s=s: nc.sync.dma_start(out=kstg[0:32, 0, :], in_=vf[32 * s:32 * s + 32, :]), r=[t_vf], w=[t_kstg])
                    P.op(PL, lambda: nc.gpsimd.tensor_copy(out=Vx[0:32, 8, :, 0:64], in_=kstg[0:32, 0, :].rearrange("p (g d) -> p g d", g=4)), r=[t_kstg], w=allk)
                    P.dma(SP, lambda s=s: nc.sync.dma_start(out=sgn_s[0:32, :], in_=wi_sgn[32 * s:32 * s + 32, :]), r=[t_wi], w=[t_sgn_s])
                    attn_unit(32, 32 * s, PAST + 32, allk, allk, allk, static_mask=False, causal_tail=False, out_col0=32 * s, sgn=sgn_s, t_sgn=t_sgn_s)


        def normA(tj):
            bj = tj % 2
            norm_tile(tj, xt[bj], t_xt[bj], junk, t_junk, xn, t_xn, stat, xnT[bj], t_xnT[bj], t_statA)

        normA(0)
        for ti in range(NT):
            sample = ti == NPT
            b = ti % 2
            xT, t_xT = xnT[b], t_xnT[b]
            r0 = ti * 128
            groupsA = [(0, 512, 0), (512, 512, 1), (1024, 512, 2), (1536, 72, 3)]
            for (c0, n, bk) in groupsA:
                for c in range(8):
                    P.op(PE, lambda c=c, c0=c0, n=n, bk=bk, xT=xT: nc.tensor.matmul(
                        pbank[bk][:, 0:n], lhsT=xT[:, c, :], rhs=WA[:, c, c0:c0 + n], start=(c == 0), stop=(c == 7)),
                        r=[t_xT, t_W], w=[t_pb[bk]])
            if ti + 1 < NT:
                normA(ti + 1)
            st2 = t_st2
            P.op(A, lambda: nc.scalar.activation(out=sq[:, 0:512], in_=pbank[0][:, 0:512], func=AF.Square), r=[t_pb[0]], w=[t_sq])
            P.op(A, lambda: nc.scalar.activation(out=sq[:, 512:768], in_=pbank[1][:, 0:256], func=AF.Square), r=[t_pb[1]], w=[t_sq])
            P.op(V, lambda: nc.vector.tensor_reduce(out=stat[:, 8:20], in_=sq[:].rearrange("p (h d) -> p h d", d=64), axis=AX.X, op=ALU.add),
                 r=[t_sq], w=[st2])
            P.op(V, lambda: nc.vector.tensor_reduce(out=stat[:, 24:25], in_=pbank[3][:, 0:64], axis=AX.X, op=ALU.add), r=[t_pb[3]], w=[st2])
            P.op(A, lambda: nc.scalar.activation(out=junk[:, 0:64], in_=pbank[3][:, 0:64], func=AF.Square, accum_out=stat[:, 25:26]),
                 r=[t_pb[3]], w=[t_junk, st2])
            P.op(A, lambda: nc.scalar.activation(out=stat[:, 8:20], in_=stat[:, 8:20], func=AF.Ln, scale=1.0 / 64, bias=eps_c[:, 0:1]), r=[st2] + CONST, w=[st2])
            P.op(A, lambda: nc.scalar.activation(out=stat[:, 8:20], in_=stat[:, 8:20], func=AF.Exp, scale=-0.5), r=[st2], w=[st2])
            P.op(V, lambda: nc.vector.tensor_scalar(out=stat[:, 26:27], in0=stat[:, 24:25], scalar1=1.0 / 64, scalar2=None, op0=ALU.mult), r=[st2], w=[st2])
            P.op(V, lambda: nc.vector.tensor_tensor(out=stat[:, 27:28], in0=stat[:, 26:27], in1=stat[:, 26:27], op=ALU.mult), r=[st2], w=[st2])
            P.op(V, lambda: nc.vector.scalar_tensor_tensor(out=stat[:, 28:29], in0=stat[:, 25:26], scalar=1.0 / 64, in1=stat[:, 27:28],
                                                          op0=ALU.mult, op1=ALU.subtract), r=[st2], w=[st2])
            P.op(A, lambda: nc.scalar.activation(out=stat[:, 28:29], in_=stat[:, 28:29], func=AF.Ln, bias=eps_c[:, 0:1]), r=[st2] + CONST, w=[st2])
            P.op(A, lambda: nc.scalar.activation(out=stat[:, 28:29], in_=stat[:, 28:29], func=AF.Exp, scale=-0.5), r=[st2], w=[st2])
            P.op(V, lambda: nc.vector.tensor_tensor(out=RB[:, 0:8, :], in0=pbank[0][:, 0:512].rearrange("p (h d) -> p h d", d=64),
                                                    in1=stat[:, 8:16].unsqueeze(2).broadcast_to([128, 8, 64]), op=ALU.mult),
                 r=[t_pb[0], st2], w=[t_RB])
            P.op(V, lambda: nc.vector.tensor_tensor(out=RB[:, 8:12, :], in0=pbank[1][:, 0:256].rearrange("p (h d) -> p h d", d=64),
                                                    in1=stat[:, 16:20].unsqueeze(2).broadcast_to([128, 4, 64]), op=ALU.mult),
                 r=[t_pb[1], st2], w=[t_RB])
            P.op(V, lambda: nc.vector.tensor_tensor(out=RB[:, 0:8, :], in0=RB[:, 0:8, :],
                                                     in1=qg_bc[:].unsqueeze(1).broadcast_to([128, 8, 64]), op=ALU.mult), r=[t_RB] + CONST, w=[t_RB])
            P.op(V, lambda: nc.vector.tensor_tensor(out=RB[:, 8:12, :], in0=RB[:, 8:12, :],
                                                     in1=kg_bc[:].unsqueeze(1).broadcast_to([128, 4, 64]), op=ALU.mult), r=[t_RB] + CONST, w=[t_RB])
            P.op(A, lambda: nc.scalar.activation(out=vf[:], in_=pbank[1][:, 256:512], func=AF.Copy), r=[t_pb[1]], w=[t_vf])
            P.op(A, lambda: nc.scalar.activation(out=RB[:, 12:20, :], in_=pbank[2][:, 0:512].rearrange("p (h d) -> p h d", d=64), func=AF.Copy),
                 r=[t_pb[2]], w=[t_RB])
            P.op(V, lambda: nc.vector.tensor_scalar(out=RB[:, 20, :], in0=pbank[3][:, 0:64], scalar1=stat[:, 26:27], scalar2=stat[:, 28:29],
                                                    op0=ALU.subtract, op1=ALU.mult), r=[t_pb[3], st2], w=[t_RB])
            P.op(V, lambda: nc.vector.tensor_tensor(out=RB[:, 20, :], in0=RB[:, 20, :], in1=lg_bc[:], op=ALU.mult), r=[t_RB] + CONST, w=[t_RB])
            P.op(V, lambda: nc.vector.tensor_tensor(out=RB[:, 20, :], in0=RB[:, 20, :], in1=lb_bc[:], op=ALU.add), r=[t_RB] + CONST, w=[t_RB])
            P.op(A, lambda: nc.scalar.activation(out=wi_abs[:], in_=pbank[3][:, 64:72], func=AF.Abs), r=[t_pb[3]], w=[t_wi])
            P.op(A, lambda: nc.scalar.activation(out=wi_sgn[:], in_=pbank[3][:, 64:72], func=AF.Sign), r=[t_pb[3]], w=[t_wi])
            cosb = rope[:, ti, 0:8].unsqueeze(1).broadcast_to([128, 21, 8])
            sinb = rope[:, ti, 8:16].unsqueeze(1).broadcast_to([128, 21, 8])
            rt = sq[:, 0:21 * 32].rearrange("p (h f) -> p h f", f=32)
            P.op(V, lambda rt=rt, cosb=cosb: nc.vector.tensor_tensor(out=rt[:, :, 0:8], in0=RB[:, :, 0:8], in1=cosb, op=ALU.mult), r=[t_RB] + CONST, w=[t_sq])
            P.op(V, lambda rt=rt, sinb=sinb: nc.vector.tensor_tensor(out=rt[:, :, 8:16], in0=RB[:, :, 8:16], in1=sinb, op=ALU.mult), r=[t_RB] + CONST, w=[t_sq])
            P.op(V, lambda rt=rt, cosb=cosb: nc.vector.tensor_tensor(out=rt[:, :, 16:24], in0=RB[:, :, 8:16], in1=cosb, op=ALU.mult), r=[t_RB] + CONST, w=[t_sq])
            P.op(V, lambda rt=rt, sinb=sinb: nc.vector.tensor_tensor(out=rt[:, :, 24:32], in0=RB[:, :, 0:8], in1=sinb, op=ALU.mult), r=[t_RB] + CONST, w=[t_sq])
            P.op(V, lambda rt=rt: nc.vector.tensor_tensor(out=RB[:, :, 0:8], in0=rt[:, :, 0:8], in1=rt[:, :, 8:16], op=ALU.subtract), r=[t_sq], w=[t_RB])
            P.op(V, lambda rt=rt: nc.vector.tensor_tensor(out=RB[:, :, 8:16], in0=rt[:, :, 16:24], in1=rt[:, :, 24:32], op=ALU.add), r=[t_sq], w=[t_RB])
            P.dma(SP, lambda r0=r0: nc.sync.dma_start(out=k_all[r0:r0 + 128, :], in_=RB[:, 8:12, :].rearrange("p h d -> p (h d)")), r=[t_RB])
            P.dma(SP, lambda r0=r0: nc.sync.dma_start(out=v_all[r0:r0 + 128, :], in_=vf[:]), r=[t_vf])
            P.dma(SP, lambda r0=r0: nc.sync.dma_start(out=ki_all[r0:r0 + 128, :], in_=RB[:, 20, :]), r=[t_RB])
            if ti == 0:
                ckpt(3)
            if STAGE >= 2 and ATT:
                attention_tile(ti)
            else:
                P.op(V, lambda: nc.vector.memset(oatt[:], 0.0), w=[t_oatt])
            P.dma(SP, lambda ti=ti: nc.sync.dma_start(out=o_scr[ti, :, 0:4, :], in_=oatt[:]), r=[t_oatt])
            if ti == 0:
                ckpt(31)
            if ti == 1:
                ckpt(32)
            if ti == 8:
                ckpt(33)
            if ti == 15:
                ckpt(34)

        ckpt(4)
        P.barrier()
        phA.close()

        phB = ExitStack()
        es.enter_context(phB)

        def sbB(name, shape, dt=F32):
            return phB.enter_context(nc.sbuf_tensor(name, list(shape), dt))

        W1 = sbB("W1", [128, 8, SW], BF16)
        W2 = sbB("W2", [128, 8, SW], BF16)
        t_W12 = P.tok("W12")
        sh0mu = sbB("sh0mu", [4, SW], BF16)
        sels_b = sbB("sels_b", [4, 128], BF16)
        t_rc = P.tok("rconst")
        lwb = sbB("lwb", [96, 3, 512], BF16)
        mask_f = sbB("mask_f", [128, 4, 128])
        mask_b = sbB("mask_b", [128, 8, 128], BF16)
        small_c = sbB("small_c", [128, 16])
        colm = sbB("colm", [64, 6, 128], BF16)
        bcbuf = [sbB("bcbuf%d" % i, [128, 512]) for i in range(2)]
        t_bcbuf = P.toks(2, "bcbuf")
        bc_src = {"w0": w0_in, "a0": a0_in, "kk": kk_in, "ka": ka_in, "rk": rk_in, "lng": lng_in, "lnb": lnb_in}
        bc_n = [0]

        def bc(nm):
            i = bc_n[0] % 2
            bc_n[0] += 1
            src = bc_src[nm]
            P.dma(SP, lambda: nc.sync.dma_start(out=bcbuf[i][:], in_=src.partition_broadcast(128)), w=[t_bcbuf[i]])
            return bcbuf[i], t_bcbuf[i]
        tiny_c = sbB("tiny_c", [128, 1])
        with ExitStack() as st:
            mu_bc = st.enter_context(nc.sbuf_tensor("mu_bc", [128, SW], F32))
            omm_bc = st.enter_context(nc.sbuf_tensor("omm_bc", [128, SW], F32))
            ssh_t = st.enter_context(nc.sbuf_tensor("ssh_t", [4, SW], F32))
            sels_f = st.enter_context(nc.sbuf_tensor("sels_f", [4, 128], F32))
            lw_f = st.enter_context(nc.sbuf_tensor("lw_f", [96, 3, 512], F32))
            colm_f = st.enter_context(nc.sbuf_tensor("colm_f", [64, 6, 128], F32))
            mask_t = st.enter_context(nc.sbuf_tensor("mask_t", [128, 8, 128], F32))
            t_mu = P.tok("mu")
            bcast_load(mu_bc, mu_in, t_mu)
            P.dma(SP, lambda: nc.sync.dma_start(out=ssh_t[:], in_=ssh_in[:, :]), w=[t_mu])
            P.dma(SP, lambda: nc.sync.dma_start(out=sels_f[:], in_=c_sels[:, :]), w=[t_mu])
            P.op(V, lambda: nc.vector.tensor_scalar(out=omm_bc[:], in0=mu_bc[:], scalar1=-1.0, scalar2=1.0,
                                                    op0=ALU.mult, op1=ALU.add), r=[t_mu], w=[t_mu])
            P.op(V, lambda: nc.vector.tensor_tensor(out=sh0mu[:], in0=ssh_t[:], in1=mu_bc[0:4, :], op=ALU.mult), r=[t_mu], w=[t_rc])
            P.op(V, lambda: nc.vector.tensor_copy(out=sels_b[:], in_=sels_f[:]), r=[t_mu], w=[t_rc])
            stg_B = [st.enter_context(nc.sbuf_tensor("wstgB%d" % i, [128, SW], F32)) for i in range(2)]
            t_stg_B = P.toks(2, "wstgB")
            for c in range(8):
                b = c % 2
                P.dma(SP, lambda c=c, b=b: nc.sync.dma_start(out=stg_B[b][:], in_=w_in[c * 128:(c + 1) * 128, NA:PROJ]), w=[t_stg_B[b]])
                P.op(V, lambda c=c, b=b: nc.vector.tensor_tensor(out=W2[:, c, :], in0=stg_B[b][:], in1=mu_bc[:], op=ALU.mult),
                     r=[t_stg_B[b], t_mu], w=[t_W12])
                P.op(PL, lambda c=c, b=b: nc.gpsimd.tensor_tensor(out=W1[:, c, :], in0=stg_B[b][:], in1=omm_bc[:], op=ALU.mult),
                     r=[t_stg_B[b], t_mu], w=[t_W12])
            P.dma(SP, lambda: nc.sync.dma_start(out=lw_f[0:32, 0, :], in_=w2_in[:, :]), w=[t_mu])
            P.dma(SP, lambda: nc.sync.dma_start(out=lw_f[0:32, 1, :], in_=a2_in[:, :]), w=[t_mu])
            P.dma(SP, lambda: nc.sync.dma_start(out=lw_f[0:96, 2, :], in_=g2_in[:, :]), w=[t_mu])
            P.op(V, lambda: nc.vector.tensor_copy(out=lwb[0:32, 0:2, :], in_=lw_f[0:32, 0:2, :]), r=[t_mu], w=[t_rc])
            P.op(V, lambda: nc.vector.tensor_copy(out=lwb[0:96, 2, :], in_=lw_f[0:96, 2, :]), r=[t_mu], w=[t_rc])
            P.dma(SP, lambda: nc.sync.dma_start(out=mask_t[:], in_=c_masks[:, :, :]), w=[t_mu])
            for di, si in enumerate([2, 6, 5, 7]):
                P.dma(SP, lambda di=di, si=si: nc.sync.dma_start(out=mask_f[:, di, :], in_=c_masks[:, si, :]), w=[t_rc])
            P.dma(SP, lambda: nc.sync.dma_start(out=small_c[:], in_=c_small[:, :]), w=[t_rc])
            P.dma(SP, lambda: nc.sync.dma_start(out=colm_f[:], in_=c_colm[:, :, :]), w=[t_mu])
            P.op(V, lambda: nc.vector.tensor_copy(out=mask_b[:], in_=mask_t[:]), r=[t_mu], w=[t_rc])
            P.op(V, lambda: nc.vector.tensor_copy(out=colm[:], in_=colm_f[:]), r=[t_mu], w=[t_rc])
            P.op(V, lambda: nc.vector.memset(tiny_c[:], 1e-24), w=[t_rc])
            P.barrier()
        RC = [t_rc]
        ckpt(5)

        def BT(name, shape, dt=F32):
            return sbB(name, shape, dt), P.tok(name)

        xt_B = [sbB("xtB0", [128, D])] * 2
        t_xt_B = [P.tok("xtB")] * 2
        xn_B, t_xn_B = BT("xnB", [128, D], BF16)
        xnT_B = [sbB("xnTB%d" % i, [128, 8, 128], BF16) for i in range(2)]
        t_xnT_B = P.toks(2, "xnTB")
        xsh, t_xsh = BT("xsh", [128, 8, 128], BF16)
        stat_B, t_statB = BT("statB", [128, 64])
        shrow, t_shrow = BT("shrow", [4, 512])
        orw, t_orw = BT("orw", [128, 4, 128], BF16)
        rS, t_rS = BT("rS", [128, 512])
        kS, t_kS = BT("kS", [128, 512])
        vS, t_vS = BT("vS", [128, 512])
        gS, t_gS = BT("gS", [128, 512])
        vb, t_vb = BT("vb", [128, 512], BF16)
        tl, t_tl = BT("tl", [128, 160])
        li, t_li = BT("li", [128, 160], BF16)
        loraT, t_loraT = BT("loraT", [96, 384], BF16)
        XW, t_XW = BT("XW", [128, 1024])
        EG, t_EG = BT("EG", [128, 512])
        ENG, t_ENG = BT("ENG", [128, 512])
        EGM, t_EGM = BT("EGM", [128, 512])
        EEND, t_EEND = BT("EEND", [128, 512])
        EGE, t_EGE = BT("EGE", [128, 512])
        KK, t_KK = BT("KK", [128, 512])
        lwt, t_lwt = KK, t_KK
        KP, t_KP = BT("KP", [128, 512])
        GendS, t_GendS = KP, t_KP
        BB, t_BB = BT("BB", [128, 512])
        T1, t_T1 = BT("T1", [128, 512])
        GX, t_GX = T1, t_T1
        st8, t_st8 = BT("st8", [128, 64])
        rt_, t_rt = BT("rt_", [128, 512], BF16)
        at_, t_at = BT("at_", [128, 512], BF16)
        kt_, t_kt = BT("kt_", [128, 512], BF16)
        bt_, t_bt = BT("bt_", [128, 512], BF16)
        kh_, t_kh = BT("kh_", [128, 512], BF16)
        bh_, t_bh = BT("bh_", [128, 512], BF16)
        Bm, t_Bm = BT("Bm", [128, 512], BF16)
        Km, t_Km = BT("Km", [128, 512], BF16)
        rT, t_rT = BT("rT", [64, 8, 128], BF16)
        aT, t_aT = BT("aT", [64, 8, 128], BF16)
        kT, t_kT = BT("kT", [64, 8, 128], BF16)
        bT, t_bT = BT("bT", [64, 8, 128], BF16)
        Nm = [BT("Nm%d" % i, [128, 8, 128], BF16) for i in range(2)]
        Mm = [BT("Mm%d" % i, [128, 8, 128], BF16) for i in range(2)]
        ArbT, t_ArbT = BT("ArbT", [128, 8, 128], BF16)
        AakT, t_AakT = Mm[1]
        ArkT, t_ArkT = BT("ArkT", [128, 8, 128], BF16)
        Xf, t_Xf = BT("Xf", [128, 8, 128])
        Xb, t_Xb = BT("Xb", [128, 8, 128], BF16)
        junk_B, t_junk_B = Xb[:].rearrange("p h t -> p (h t)"), t_Xb
        GT, t_GT = BT("GT", [64, 512], BF16)
        RT2, t_RT2 = BT("RT2", [64, 8, 128], BF16)
        RTm, t_RTm = BT("RTm", [64, 8, 128], BF16)
        GAM, t_GAM = BT("GAM", [64, 8, 4])
        Hf, t_Hf = BT("Hf", [64, 8, 64])
        Hb, t_Hb = BT("Hb", [64, 8, 64], BF16)
        Tg, t_Tg = BT("Tg", [64, 8, 64])
        S0, t_S0 = BT("S0", [64, 8, 64])
        So, t_So = Tg, t_Tg
        YS, t_YS = EG, t_EG
        YN, t_YN = ENG, t_ENG
        ob_, t_ob = BT("ob_", [128, 512], BF16)
        P.op(V, lambda: nc.vector.memset(Hf[:], 0.0), w=[t_Hf])
        P.op(V, lambda: nc.vector.memset(Hb[:], 0.0), w=[t_Hb])
        NEG_E = -0.6065306597126334
        ckpt(60)

        def h3(ap, d=64):
            return ap.rearrange("p (h d) -> p h d", d=d)

        def rwkv_tile(ti, xT, t_xT, part):
            sample = ti == NPT
            C = 32 if sample else 64
            NCH = 128 // C
            mi = 3 if sample else 0
            MSL, MSU, MU, BLK = mi, mi + 1, mi + 2, (7 if sample else 6)
            selc = small_c[:, 2:6] if sample else small_c[:, 0:2]
            chi = small_c[:, 8:12] if sample else small_c[:, 6:8]
            cm0 = 2 if sample else 0
            nlev = 5 if sample else 6
            for (c0, n, bk) in ([(1536, 160, 3), (0, 512, 0), (512, 512, 1), (1024, 512, 2)] if part == 1 else []):
                for c in range(8):
                    P.op(PE, lambda c=c, c0=c0, n=n, bk=bk: nc.tensor.matmul(pbank[bk][:, 0:n], lhsT=xT[:, c, :], rhs=W1[:, c, c0:c0 + n],
                                                                           start=(c == 0), stop=False), r=[t_xT, t_W12], w=[t_pb[bk]])
                for c in range(8):
                    P.op(PE, lambda c=c, c0=c0, n=n, bk=bk: nc.tensor.matmul(pbank[bk][:, 0:n], lhsT=xsh[:, c, :], rhs=W2[:, c, c0:c0 + n],
                                                                           start=False, stop=(c == 7 and not sample)), r=[t_xsh, t_W12], w=[t_pb[bk]])
                if sample:
                    P.op(PE, lambda c0=c0, n=n, bk=bk: nc.tensor.matmul(pbank[bk][:, 0:n], lhsT=sels_b[0:4, :], rhs=sh0mu[0:4, c0:c0 + n],
                                                                      start=False, stop=True), r=RC, w=[t_pb[bk]])
            if part == 1:
                return
            P.op(A, lambda: nc.scalar.activation(out=rS[:], in_=pbank[0][:, :], func=AF.Copy), r=[t_pb[0]], w=[t_rS])
            P.op(A, lambda: nc.scalar.activation(out=kS[:], in_=pbank[1][:, :], func=AF.Copy), r=[t_pb[1]], w=[t_kS])
            P.op(A, lambda: nc.scalar.activation(out=vS[:], in_=pbank[2][:, :], func=AF.Copy), r=[t_pb[2]], w=[t_vS])
            P.op(A, lambda: nc.scalar.activation(out=vb[:], in_=vS[:], func=AF.Copy), r=[t_vS], w=[t_vb])
            if ti == 0:
                ckpt(62)
            P.op(A, lambda: nc.scalar.activation(out=tl[:, 0:32], in_=pbank[3][:, 0:32], func=AF.Exp, scale=2.0), r=[t_pb[3]], w=[t_tl])
            P.op(A, lambda: nc.scalar.activation(out=tl[:, 64:160], in_=pbank[3][:, 64:160], func=AF.Exp, scale=-1.0), r=[t_pb[3]], w=[t_tl])
            P.op(A, lambda: nc.scalar.activation(out=li[:, 32:64], in_=pbank[3][:, 32:64], func=AF.Copy), r=[t_pb[3]], w=[t_li])
            P.op(V, lambda: nc.vector.tensor_scalar(out=tl[:, 0:32], in0=tl[:, 0:32], scalar1=1.0, scalar2=None, op0=ALU.add), r=[t_tl], w=[t_tl])
            P.op(V, lambda: nc.vector.tensor_scalar(out=tl[:, 64:160], in0=tl[:, 64:160], scalar1=1.0, scalar2=None, op0=ALU.add), r=[t_tl], w=[t_tl])
            P.op(V, lambda: nc.vector.reciprocal(out=tl[:, 0:32], in_=tl[:, 0:32]), r=[t_tl], w=[t_tl])
            P.op(V, lambda: nc.vector.reciprocal(out=tl[:, 64:160], in_=tl[:, 64:160]), r=[t_tl], w=[t_tl])
            P.op(V, lambda: nc.vector.tensor_scalar(out=li[:, 0:32], in0=tl[:, 0:32], scalar1=-2.0, scalar2=1.0, op0=ALU.mult, op1=ALU.add), r=[t_tl], w=[t_li])
            P.op(V, lambda: nc.vector.tensor_copy(out=li[:, 64:160], in_=tl[:, 64:160]), r=[t_tl], w=[t_li])
            p7 = pb_bf(7)
            P.op(PE, lambda: nc.tensor.transpose(p7[0:32, 0:128], li[:, 0:32], ident_b[:]), r=[t_li] + CONST, w=[t_pb[7]])
            P.op(PE, lambda: nc.tensor.transpose(p7[0:32, 128:256], li[:, 32:64], ident_b[:]), r=[t_li] + CONST, w=[t_pb[7]])
            P.op(PE, lambda: nc.tensor.transpose(p7[0:96, 256:384], li[:, 64:160], ident_b[:]), r=[t_li] + CONST, w=[t_pb[7]])
            P.op(A, lambda: nc.scalar.activation(out=loraT[0:32, 0:256], in_=p7[0:32, 0:256], func=AF.Copy), r=[t_pb[7]], w=[t_loraT])
            P.op(A, lambda: nc.scalar.activation(out=loraT[0:96, 256:384], in_=p7[0:96, 256:384], func=AF.Copy), r=[t_pb[7]], w=[t_loraT])
            P.op(PE, lambda: nc.tensor.matmul(pbank[4][:, :], lhsT=loraT[0:32, 0:128], rhs=lwb[0:32, 0, :], start=True, stop=True), r=[t_loraT] + RC, w=[t_pb[4]])
            P.op(PE, lambda: nc.tensor.matmul(pbank[5][:, :], lhsT=loraT[0:32, 128:256], rhs=lwb[0:32, 1, :], start=True, stop=True), r=[t_loraT] + RC, w=[t_pb[5]])
            P.op(PE, lambda: nc.tensor.matmul(pbank[6][:, :], lhsT=loraT[0:96, 256:384], rhs=lwb[0:96, 2, :], start=True, stop=True), r=[t_loraT] + RC, w=[t_pb[6]])
            b1, tb1 = bc("w0")
            P.op(V, lambda: nc.vector.tensor_tensor(out=XW[:, 0:512], in0=pbank[4][:, :], in1=b1[:], op=ALU.add), r=[t_pb[4], tb1], w=[t_XW])
            b2, tb2 = bc("a0")
            P.op(V, lambda: nc.vector.tensor_tensor(out=XW[:, 512:1024], in0=pbank[5][:, :], in1=b2[:], op=ALU.add), r=[t_pb[5], tb2], w=[t_XW])
            P.op(A, lambda: nc.scalar.activation(out=XW[:], in_=XW[:], func=AF.Exp, scale=-1.0), r=[t_XW], w=[t_XW])
            P.op(V, lambda: nc.vector.tensor_scalar(out=XW[:], in0=XW[:], scalar1=1.0, scalar2=None, op0=ALU.add), r=[t_XW], w=[t_XW])
            P.op(V, lambda: nc.vector.reciprocal(out=XW[:], in_=XW[:]), r=[t_XW], w=[t_XW])
            P.op(A, lambda: nc.scalar.activation(out=gS[:], in_=pbank[6][:, :], func=AF.Copy), r=[t_pb[6]], w=[t_gS])
            AA = XW[:, 512:1024]
            if ti == 0:
                ckpt(64)
            P.op(A, lambda: nc.scalar.activation(out=lwt[:], in_=XW[:, 0:512], func=AF.Copy, scale=NEG_E), r=[t_XW], w=[t_lwt])
            P.op(PE, lambda: nc.tensor.matmul(pbank[3][:, :], lhsT=mask_f[:, (2 if sample else 0), :], rhs=lwt[:], start=True, stop=True), r=[t_lwt] + RC, w=[t_pb[3]])
            P.op(PE, lambda: nc.tensor.matmul(pbank[4][:, :], lhsT=mask_f[:, (3 if sample else 1), :], rhs=lwt[:], start=True, stop=True), r=[t_lwt] + RC, w=[t_pb[4]])
            GinS = XW[:, 0:512]
            P.op(A, lambda: nc.scalar.activation(out=GinS, in_=pbank[3][:, :], func=AF.Copy), r=[t_pb[3], t_lwt], w=[t_XW])
            P.op(A, lambda: nc.scalar.activation(out=GendS[:], in_=pbank[4][:, :], func=AF.Copy), r=[t_pb[4]], w=[t_GendS])
            P.op(A, lambda: nc.scalar.activation(out=EG[:], in_=GinS, func=AF.Exp), r=[t_XW], w=[t_EG])
            P.op(A, lambda: nc.scalar.activation(out=ENG[:], in_=GinS, func=AF.Exp, scale=-1.0), r=[t_XW], w=[t_ENG])
            P.op(V, lambda: nc.vector.tensor_tensor(out=GX[:], in0=GinS, in1=lwt[:], op=ALU.subtract), r=[t_XW, t_lwt], w=[t_GX])
            P.op(A, lambda: nc.scalar.activation(out=EGM[:], in_=GX[:], func=AF.Exp), r=[t_GX], w=[t_EGM])
            P.op(V, lambda: nc.vector.tensor_tensor(out=GX[:], in0=GendS[:], in1=GinS, op=ALU.subtract), r=[t_XW, t_GendS, t_EGM], w=[t_GX])
            P.op(A, lambda: nc.scalar.activation(out=EEND[:], in_=GX[:], func=AF.Exp), r=[t_GX], w=[t_EEND])
            P.op(A, lambda: nc.scalar.activation(out=EGE[:], in_=GendS[:], func=AF.Exp), r=[t_GendS], w=[t_EGE])
            if ti == 0:
                ckpt(65)
            b3, tb3 = bc("kk")
            P.op(V, lambda: nc.vector.tensor_tensor(out=KK[:], in0=kS[:], in1=b3[:], op=ALU.mult), r=[t_kS, tb3], w=[t_KK])
            P.op(A, lambda: nc.scalar.activation(out=T1[:], in_=KK[:], func=AF.Square), r=[t_KK], w=[t_T1])
            P.op(V, lambda: nc.vector.tensor_reduce(out=st8[:, 0:8], in_=h3(T1[:]), axis=AX.X, op=ALU.add), r=[t_T1], w=[t_st8])
            P.op(A, lambda: nc.scalar.activation(out=st8[:, 0:8], in_=st8[:, 0:8], func=AF.Ln, bias=tiny_c[:, 0:1]), r=[t_st8] + RC, w=[t_st8])
            P.op(A, lambda: nc.scalar.activation(out=st8[:, 0:8], in_=st8[:, 0:8], func=AF.Exp, scale=-0.5), r=[t_st8], w=[t_st8])
            P.op(V, lambda: nc.vector.tensor_tensor(out=h3(KK[:]), in0=h3(KK[:]), in1=st8[:, 0:8].unsqueeze(2).broadcast_to([128, 8, 64]), op=ALU.mult),
                 r=[t_KK, t_st8], w=[t_KK])
            b4, tb4 = bc("ka")
            P.op(V, lambda: nc.vector.scalar_tensor_tensor(out=KP[:], in0=AA, scalar=-1.0, in1=b4[:], op0=ALU.add, op1=ALU.mult), r=[t_XW, tb4], w=[t_KP])
            P.op(V, lambda: nc.vector.scalar_tensor_tensor(out=KP[:], in0=KP[:], scalar=1.0, in1=kS[:], op0=ALU.add, op1=ALU.mult), r=[t_KP, t_kS], w=[t_KP])
            P.op(V, lambda: nc.vector.tensor_tensor(out=BB[:], in0=KK[:], in1=AA, op=ALU.mult), r=[t_KK, t_XW], w=[t_BB])
            P.op(PL, lambda: nc.gpsimd.tensor_tensor(out=T1[:], in0=rS[:], in1=KP[:], op=ALU.mult), r=[t_rS, t_KP, t_st8], w=[t_T1])
            b5, tb5 = bc("rk")
            P.op(PL, lambda: nc.gpsimd.tensor_tensor(out=T1[:], in0=T1[:], in1=b5[:], op=ALU.mult), r=[t_T1, tb5], w=[t_T1])
            P.op(V, lambda: nc.vector.tensor_reduce(out=st8[:, 8:16], in_=h3(T1[:]), axis=AX.X, op=ALU.add), r=[t_T1], w=[t_st8])
            P.op(V, lambda: nc.vector.tensor_tensor(out=rt_[:], in0=rS[:], in1=EG[:], op=ALU.mult), r=[t_rS, t_EG], w=[t_rt])
            P.op(V, lambda: nc.vector.tensor_tensor(out=at_[:], in0=KK[:], in1=EGM[:], op=ALU.mult), r=[t_KK, t_EGM], w=[t_at])
            P.op(V, lambda: nc.vector.tensor_tensor(out=kt_[:], in0=KP[:], in1=ENG[:], op=ALU.mult), r=[t_KP, t_ENG], w=[t_kt])
            P.op(V, lambda: nc.vector.scalar_tensor_tensor(out=bt_[:], in0=BB[:], scalar=-1.0, in1=ENG[:], op0=ALU.mult, op1=ALU.mult), r=[t_BB, t_ENG], w=[t_bt])
            P.op(V, lambda: nc.vector.tensor_tensor(out=kh_[:], in0=KP[:], in1=EEND[:], op=ALU.mult), r=[t_KP, t_EEND], w=[t_kh])
            P.op(V, lambda: nc.vector.scalar_tensor_tensor(out=bh_[:], in0=BB[:], scalar=-1.0, in1=EEND[:], op0=ALU.mult, op1=ALU.mult), r=[t_BB, t_EEND], w=[t_bh])
            if ti == 0:
                ckpt(67)
            for (src, t_src, dst, t_dst, bk, eng) in [(rt_, t_rt, rT, t_rT, 0, A), (at_, t_at, aT, t_aT, 1, V), (kt_, t_kt, kT, t_kT, 2, A), (bt_, t_bt, bT, t_bT, 5, V)]:
                pv = pb_bf(bk)
                for h in range(8):
                    P.op(PE, lambda h=h, pv=pv, src=src: nc.tensor.transpose(pv[0:64, h * 128:(h + 1) * 128], src[:, h * 64:(h + 1) * 64], ident_b[:]),
                         r=[t_src] + CONST, w=[t_pb[bk]])
                if eng == A:
                    P.op(A, lambda pv=pv, dst=dst: nc.scalar.activation(out=dst[:], in_=pv[0:64, :].rearrange("p (h t) -> p h t", h=8), func=AF.Copy), r=[t_pb[bk]], w=[t_dst])
                else:
                    P.op(V, lambda pv=pv, dst=dst: nc.vector.tensor_copy(out=dst[:], in_=pv[0:64, :].rearrange("p (h t) -> p h t", h=8)), r=[t_pb[bk]], w=[t_dst])
            if ti == 0:
                ckpt(68)
            N0, t_N0 = Nm[0]
            M0, t_M0 = Mm[0]
            specs = [(aT, t_aT, bT, t_bT, N0, t_N0, MSL, 0), (bT, t_bT, aT, t_aT, M0, t_M0, MSU, 1), (bT, t_bT, rT, t_rT, ArbT, t_ArbT, MU, 2),
                     (kT, t_kT, aT, t_aT, AakT, t_AakT, MSU, 5), (kT, t_kT, rT, t_rT, ArkT, t_ArkT, MU, 6)]
            for hg in range(2):
                for (L_, tL, R_, tR, dst, t_dst, mk, bk) in specs:
                    for j in range(4):
                        h = hg * 4 + j
                        P.op(PE, lambda h=h, j=j, L_=L_, R_=R_, bk=bk: nc.tensor.matmul(pbank[bk][:, j * 128:(j + 1) * 128], lhsT=L_[:, h, :], rhs=R_[:, h, :],
                                                                                     start=True, stop=True), r=[tL, tR], w=[t_pb[bk]])
                    P.op(V, lambda hg=hg, dst=dst, mk=mk, bk=bk: nc.vector.tensor_tensor(
                        out=dst[:, hg * 4:hg * 4 + 4, :], in0=pbank[bk][:, :].rearrange("p (h t) -> p h t", h=4),
                        in1=mask_b[:, mk, :].unsqueeze(1).broadcast_to([128, 4, 128]), op=ALU.mult), r=[t_pb[bk]] + RC, w=[t_dst])
            if ti == 0:
                ckpt(69)
            for h in range(8):
                P.op(PE, lambda h=h: nc.tensor.matmul(pbank[7][:, h * 64:(h + 1) * 64], lhsT=AakT[:, h, :], rhs=vb[:, h * 64:(h + 1) * 64], start=True, stop=True),
                     r=[t_AakT, t_vb], w=[t_pb[7]])
            P.op(A, lambda: nc.scalar.activation(out=Xf[:, :, 64:128], in_=h3(pbank[7][:, :]), func=AF.Copy), r=[t_pb[7]], w=[t_Xf])
            P.op(A, lambda: nc.scalar.activation(out=Xf[:, :, 0:64], in_=h3(at_[:]), func=AF.Copy), r=[t_at], w=[t_Xf])
            P.op(A, lambda: nc.scalar.activation(out=Xb[:], in_=Xf[:], func=AF.Copy), r=[t_Xf], w=[t_Xb])
            if ti == 0:
                ckpt(70)
            cur = 0
            for lev in range(nlev):
                Mc, t_Mc = Mm[cur]
                Nc, t_Nc = Nm[cur]
                for h in range(8):
                    bk = 3 + h // 4
                    P.op(PE, lambda h=h, bk=bk, Mc=Mc: nc.tensor.matmul(pbank[bk][:, (h % 4) * 128:(h % 4 + 1) * 128], lhsT=Mc[:, h, :], rhs=Xb[:, h, :], start=True, stop=True),
                         r=[t_Mc, t_Xb], w=[t_pb[bk]])
                for hg in range(2):
                    P.op(V, lambda hg=hg: nc.vector.tensor_tensor(out=Xf[:, hg * 4:hg * 4 + 4, :], in0=pbank[3 + hg][:, :].rearrange("p (h t) -> p h t", h=4),
                                                                  in1=Xf[:, hg * 4:hg * 4 + 4, :], op=ALU.add), r=[t_pb[3 + hg], t_Xf], w=[t_Xf])
                P.op(A, lambda: nc.scalar.activation(out=Xb[:], in_=Xf[:], func=AF.Copy), r=[t_Xf], w=[t_Xb])
                if lev < nlev - 1:
                    Mn, t_Mn = Mm[1 - cur]
                    Nn, t_Nn = Nm[1 - cur]
                    for hg in range(2):
                        for j in range(4):
                            h = hg * 4 + j
                            P.op(PE, lambda h=h, j=j, hg=hg, Mc=Mc, Nc=Nc: nc.tensor.matmul(pbank[0 + hg][:, j * 128:(j + 1) * 128], lhsT=Nc[:, h, :], rhs=Mc[:, h, :], start=True, stop=True),
                                 r=[t_Mc, t_Nc], w=[t_pb[0 + hg]])
                        P.op(A, lambda hg=hg, Mn=Mn: nc.scalar.activation(out=Mn[:, hg * 4:hg * 4 + 4, :], in_=pbank[0 + hg][:, :].rearrange("p (h t) -> p h t", h=4), func=AF.Copy),
                             r=[t_pb[0 + hg]], w=[t_Mn])
                        if lev < nlev - 2:
                            bkn = 2 if hg == 0 else 5
                            for j in range(4):
                                h = hg * 4 + j
                                P.op(PE, lambda h=h, j=j, bkn=bkn, Mc=Mc, Nc=Nc: nc.tensor.matmul(pbank[bkn][:, j * 128:(j + 1) * 128], lhsT=Mc[:, h, :], rhs=Nc[:, h, :], start=True, stop=True),
                                     r=[t_Mc, t_Nc], w=[t_pb[bkn]])
                            P.op(A, lambda hg=hg, bkn=bkn, Nn=Nn: nc.scalar.activation(out=Nn[:, hg * 4:hg * 4 + 4, :], in_=pbank[bkn][:, :].rearrange("p (h t) -> p h t", h=4), func=AF.Copy),
                                 r=[t_pb[bkn]], w=[t_Nn])
                    cur = 1 - cur
            if ti == 0:
                ckpt(71)
            for h in range(8):
                bk = h // 4
                P.op(PE, lambda h=h, bk=bk: nc.tensor.matmul(pbank[bk][0:64, (h % 4) * 128:(h % 4 + 1) * 128], lhsT=Xb[:, h, 0:64], rhs=ArbT[:, h, :], start=True, stop=True),
                     r=[t_Xb, t_ArbT], w=[t_pb[bk]])
            for hg in range(2):
                P.op(V, lambda hg=hg: nc.vector.tensor_tensor(out=RT2[:, hg * 4:hg * 4 + 4, :], in0=pbank[hg][0:64, :].rearrange("p (h t) -> p h t", h=4),
                                                              in1=rT[:, hg * 4:hg * 4 + 4, :], op=ALU.add), r=[t_pb[hg], t_rT], w=[t_RT2])
            for h in range(8):
                P.op(PE, lambda h=h: nc.tensor.matmul(pbank[7][0:64, h * NCH:(h + 1) * NCH], lhsT=EGE[:, h * 64:(h + 1) * 64], rhs=selc, start=True, stop=True),
                     r=[t_EGE] + RC, w=[t_pb[7]])
            P.op(A, lambda: nc.scalar.activation(out=GAM[:, :, 0:NCH], in_=pbank[7][0:64, 0:8 * NCH].rearrange("p (h c) -> p h c", c=NCH), func=AF.Copy), r=[t_pb[7]], w=[t_GAM])
            if ti == 0:
                ckpt(72)
            P.op(PE, lambda: nc.tensor.matmul(pbank[2][:, :], lhsT=zero_b[0:1, 0:128], rhs=zero_b[0:1, 0:512], start=True, stop=False), r=CONST, w=[t_pb[2]])
            for h in range(8):
                P.op(PE, lambda h=h: nc.tensor.matmul(pbank[2][:, h * 64:(h + 1) * 64], lhsT=ArbT[:, h, :], rhs=Xb[:, h, 64:128], start=False, stop=False),
                     r=[t_ArbT, t_Xb], w=[t_pb[2]])
                P.op(PE, lambda h=h: nc.tensor.matmul(pbank[2][:, h * 64:(h + 1) * 64], lhsT=ArkT[:, h, :], rhs=vb[:, h * 64:(h + 1) * 64], start=False, stop=False),
                     r=[t_ArkT, t_vb], w=[t_pb[2]])
            if ti == 0:
                ckpt(73)
            for c in range(NCH):
                if sample:
                    P.dma(SP, lambda c=c: nc.sync.dma_start(out=S0[:], in_=swkv_in[c].rearrange("h v k -> v h k")), w=[t_S0])
                    for h in range(8):
                        P.op(PE, lambda h=h: nc.tensor.matmul(pbank[5][0:64, h * 64:(h + 1) * 64], lhsT=S0[:, h, :], rhs=ident_f[0:64, 0:64], start=True, stop=True),
                             r=[t_S0] + CONST, w=[t_pb[5]])
                    P.op(V, lambda: nc.vector.tensor_copy(out=Hf[:], in_=h3(pbank[5][0:64, :])), r=[t_pb[5]], w=[t_Hf])
                    P.op(A, lambda: nc.scalar.activation(out=Hb[:], in_=Hf[:], func=AF.Copy), r=[t_Hf], w=[t_Hb])
                P.op(V, lambda c=c: nc.vector.tensor_scalar(out=Bm[:], in0=bh_[:], scalar1=chi[:, c:c + 1], scalar2=None, op0=ALU.mult), r=[t_bh] + RC, w=[t_Bm])
                P.op(V, lambda c=c: nc.vector.tensor_scalar(out=Km[:], in0=kh_[:], scalar1=chi[:, c:c + 1], scalar2=None, op0=ALU.mult), r=[t_kh] + RC, w=[t_Km])
                P.op(V, lambda c=c: nc.vector.tensor_tensor(out=RTm[:], in0=RT2[:], in1=colm[:, cm0 + c, :].unsqueeze(1).broadcast_to([64, 8, 128]), op=ALU.mult),
                     r=[t_RT2] + RC, w=[t_RTm])
                for h in range(8):
                    P.op(PE, lambda h=h: nc.tensor.matmul(pbank[6][0:64, h * 64:(h + 1) * 64], lhsT=Xb[:, h, 0:64], rhs=Bm[:, h * 64:(h + 1) * 64], start=True, stop=True),
                         r=[t_Xb, t_Bm], w=[t_pb[6]])
                P.op(A, lambda: nc.scalar.activation(out=GT[:], in_=pbank[6][0:64, :], func=AF.Copy), r=[t_pb[6]], w=[t_GT])
                for h in range(8):
                    P.op(PE, lambda c=c, h=h: nc.tensor.matmul(pbank[2][:, h * 64:(h + 1) * 64], lhsT=RTm[:, h, :], rhs=Hb[:, h, :], start=False, stop=(c == NCH - 1 and h == 7)),
                         r=[t_RTm, t_Hb], w=[t_pb[2]])
                P.op(PE, lambda: nc.tensor.matmul(pbank[5][0:64, :], lhsT=zero_b[0:1, 0:64], rhs=zero_b[0:1, 0:512], start=True, stop=False), r=CONST, w=[t_pb[5]])
                for h in range(8):
                    P.op(PE, lambda c=c, h=h: nc.tensor.matmul(pbank[5][0:64, h * 64:(h + 1) * 64], lhsT=Bm[:, h * 64:(h + 1) * 64], rhs=Xb[:, h, 64:128], start=False, stop=False),
                         r=[t_Bm, t_Xb], w=[t_pb[5]])
                    P.op(PE, lambda c=c, h=h: nc.tensor.matmul(pbank[5][0:64, h * 64:(h + 1) * 64], lhsT=Km[:, h * 64:(h + 1) * 64], rhs=vb[:, h * 64:(h + 1) * 64], start=False, stop=False),
                         r=[t_Km, t_vb], w=[t_pb[5]])
                    P.op(PE, lambda c=c, h=h: nc.tensor.matmul(pbank[5][0:64, h * 64:(h + 1) * 64], lhsT=GT[:, h * 64:(h + 1) * 64], rhs=Hb[:, h, :], start=False, stop=(h == 7)),
                         r=[t_GT, t_Hb], w=[t_pb[5]])
                P.op(V, lambda c=c: nc.vector.tensor_tensor(out=Tg[:], in0=Hf[:], in1=GAM[:, :, c:c + 1].broadcast_to([64, 8, 64]), op=ALU.mult), r=[t_Hf, t_GAM], w=[t_Tg])
                P.op(V, lambda: nc.vector.tensor_tensor(out=Hf[:], in0=Tg[:], in1=h3(pbank[5][0:64, :]), op=ALU.add), r=[t_Tg, t_pb[5]], w=[t_Hf])
                P.op(A, lambda: nc.scalar.activation(out=Hb[:], in_=Hf[:], func=AF.Copy), r=[t_Hf], w=[t_Hb])
                if sample or (ti == NPT - 1 and c == NCH - 1):
                    for h in range(8):
                        P.op(PE, lambda h=h: nc.tensor.matmul(pbank[7][0:64, h * 64:(h + 1) * 64], lhsT=Hf[:, h, :], rhs=ident_f[0:64, 0:64], start=True, stop=True),
                             r=[t_Hf] + CONST, w=[t_pb[7]])
                    P.op(V, lambda: nc.vector.tensor_copy(out=So[:], in_=h3(pbank[7][0:64, :])), r=[t_pb[7]], w=[t_So])
                    dstw = wkv_s[c] if sample else wkv_p
                    P.dma(SP, lambda dstw=dstw: nc.sync.dma_start(out=dstw.rearrange("h v k -> v h k"), in_=So[:]), r=[t_So])
            if ti == 0:
                ckpt(74)
            P.op(A, lambda: nc.scalar.activation(out=YS[:], in_=pbank[2][:, :], func=AF.Copy), r=[t_pb[2]], w=[t_YS])
            P.op(V, lambda: nc.vector.tensor_reduce(out=st8[:, 16:24], in_=h3(YS[:]), axis=AX.X, op=ALU.add), r=[t_YS], w=[t_st8])
            P.op(A, lambda: nc.scalar.activation(out=T1[:], in_=YS[:], func=AF.Square), r=[t_YS, t_st8], w=[t_T1])
            P.op(V, lambda: nc.vector.tensor_reduce(out=st8[:, 24:32], in_=h3(T1[:]), axis=AX.X, op=ALU.add), r=[t_T1], w=[t_st8])
            P.op(V, lambda: nc.vector.tensor_scalar(out=st8[:, 16:24], in0=st8[:, 16:24], scalar1=1.0 / 64, scalar2=None, op0=ALU.mult), r=[t_st8], w=[t_st8])
            P.op(V, lambda: nc.vector.tensor_tensor(out=st8[:, 32:40], in0=st8[:, 16:24], in1=st8[:, 16:24], op=ALU.mult), r=[t_st8], w=[t_st8])
            P.op(V, lambda: nc.vector.scalar_tensor_tensor(out=st8[:, 24:32], in0=st8[:, 24:32], scalar=1.0 / 64, in1=st8[:, 32:40], op0=ALU.mult, op1=ALU.subtract),
                 r=[t_st8], w=[t_st8])
            P.op(A, lambda: nc.scalar.activation(out=st8[:, 24:32], in_=st8[:, 24:32], func=AF.Ln, bias=eps_c[:, 1:2]), r=[t_st8] + CONST, w=[t_st8])
            P.op(A, lambda: nc.scalar.activation(out=st8[:, 24:32], in_=st8[:, 24:32], func=AF.Exp, scale=-0.5), r=[t_st8], w=[t_st8])
            P.op(V, lambda: nc.vector.tensor_tensor(out=h3(YN[:]), in0=h3(YS[:]), in1=st8[:, 16:24].unsqueeze(2).broadcast_to([128, 8, 64]), op=ALU.subtract),
                 r=[t_YS, t_st8], w=[t_YN])
            P.op(V, lambda: nc.vector.tensor_tensor(out=h3(YN[:]), in0=h3(YN[:]), in1=st8[:, 24:32].unsqueeze(2).broadcast_to([128, 8, 64]), op=ALU.mult),
                 r=[t_YN, t_st8], w=[t_YN])
            b6, tb6 = bc("lng")
            P.op(V, lambda: nc.vector.tensor_tensor(out=YN[:], in0=YN[:], in1=b6[:], op=ALU.mult), r=[t_YN, tb6], w=[t_YN])
            b7, tb7 = bc("lnb")
            P.op(V, lambda: nc.vector.tensor_tensor(out=YN[:], in0=YN[:], in1=b7[:], op=ALU.add), r=[t_YN, tb7], w=[t_YN])
            P.op(V, lambda: nc.vector.tensor_tensor(out=h3(T1[:]), in0=h3(vS[:]), in1=st8[:, 8:16].unsqueeze(2).broadcast_to([128, 8, 64]), op=ALU.mult),
                 r=[t_vS, t_st8], w=[t_T1])
            P.op(V, lambda: nc.vector.tensor_tensor(out=YN[:], in0=YN[:], in1=T1[:], op=ALU.add), r=[t_YN, t_T1], w=[t_YN])
            P.op(V, lambda: nc.vector.tensor_tensor(out=ob_[:], in0=YN[:], in1=gS[:], op=ALU.mult), r=[t_YN, t_gS], w=[t_ob])
            p7b = pb_bf(7)
            for c in range(4):
                P.op(PE, lambda c=c: nc.tensor.transpose(p7b[:, c * 128:(c + 1) * 128], ob_[:, c * 128:(c + 1) * 128], ident_b[:]), r=[t_ob] + CONST, w=[t_pb[7]])
            P.op(A, lambda: nc.scalar.activation(out=orw[:], in_=p7b[:, 0:512].rearrange("p (c t) -> p c t", c=4), func=AF.Copy), r=[t_pb[7]], w=[t_orw])


        t_shdummy = P.tok("shd")

        def normB(tj):
            bj = tj % 2
            xTj, t_xTj = xnT_B[bj], t_xnT_B[bj]
            norm_tile(tj, xt_B[bj], t_xt_B[bj], junk_B, t_junk_B, xn_B, t_xn_B, stat_B, xTj, t_xTj, t_statB)
            if tj == NPT:
                P.op(V, lambda: nc.vector.tensor_copy(
                    out=xsh[:].rearrange("p c (s t) -> p c s t", s=4)[:, :, :, 1:32],
                    in_=xTj[:].rearrange("p c (s t) -> p c s t", s=4)[:, :, :, 0:31]), r=[t_xTj], w=[t_xsh])
                P.op(V, lambda: nc.vector.memset(xsh[:].rearrange("p c (s t) -> p c s t", s=4)[:, :, :, 0:1], 0.0), w=[t_xsh])
            else:
                P.op(V, lambda: nc.vector.tensor_copy(out=xsh[:, :, 1:128], in_=xTj[:, :, 0:127]), r=[t_xTj], w=[t_xsh])
                if tj == 0:
                    P.op(V, lambda: nc.vector.memset(xsh[:, :, 0:1], 0.0), w=[t_xsh])
                else:
                    xTp, t_xTp = xnT_B[1 - bj], t_xnT_B[1 - bj]
                    P.op(V, lambda: nc.vector.tensor_copy(out=xsh[:, :, 0:1], in_=xTp[:, :, 127:128]), r=[t_xTp], w=[t_xsh])

        normB(0)
        for ti in range(NT):
            sample = ti == NPT
            b = ti % 2
            xT, t_xT = xnT_B[b], t_xnT_B[b]
            if STAGE >= 3:
                rwkv_tile(ti, xT, t_xT, 1)
            if ti + 1 < NT:
                normB(ti + 1)
            if STAGE >= 3:
                rwkv_tile(ti, xT, t_xT, 2)
            else:
                P.op(V, lambda: nc.vector.memset(orw[:], 0.0), w=[t_orw])
            P.dma(SP, lambda ti=ti: nc.sync.dma_start(out=o_scr[ti, :, 4:8, :], in_=orw[:]), r=[t_orw])
            if ti == 1:
                ckpt(51)
            if ti == NPT - 1 or sample:
                nrow = 4 if sample else 1
                if sample:
                    lastc = sbB("lastc", [128, 8, 4], BF16)
                    t_lastc = P.tok("lastc")
                    for s_ in range(4):
                        P.op(V, lambda s_=s_, xT=xT: nc.vector.tensor_copy(out=lastc[:, :, s_:s_ + 1], in_=xT[:, :, 32 * s_ + 31:32 * s_ + 32]), r=[t_xT], w=[t_lastc])
                for gi, (c0, n) in enumerate([(0, 512), (512, 512), (1024, 512), (1536, 160)]):
                    bk = 4 + gi % 2
                    for c in range(8):
                        if sample:
                            lh, tl_ = lastc[:, c, :], t_lastc
                        else:
                            lh, tl_ = xT[:, c, 127:128], t_xT
                        P.op(PE, lambda c=c, c0=c0, n=n, bk=bk, lh=lh, nrow=nrow: nc.tensor.matmul(
                            pbank[bk][0:nrow, 0:n], lhsT=lh, rhs=W1[:, c, c0:c0 + n], start=(c == 0), stop=False), r=[tl_, t_W12], w=[t_pb[bk]])
                        P.op(PE, lambda c=c, c0=c0, n=n, bk=bk, lh=lh, nrow=nrow: nc.tensor.matmul(
                            pbank[bk][0:nrow, 0:n], lhsT=lh, rhs=W2[:, c, c0:c0 + n], start=False, stop=(c == 7)), r=[tl_, t_W12], w=[t_pb[bk]])
                    P.op(A, lambda c0=c0, n=n, bk=bk, nrow=nrow: nc.scalar.activation(out=shrow[0:nrow, 0:n], in_=pbank[bk][0:nrow, 0:n], func=AF.Copy),
                         r=[t_pb[bk]], w=[t_shrow])
                    dst = sh_s if sample else sh_p
                    P.dma(SP, lambda dst=dst, nrow=nrow, c0=c0, n=n: nc.sync.dma_start(out=dst[0:nrow, c0:c0 + n], in_=shrow[0:nrow, 0:n]), r=[t_shrow], w=[t_shdummy])

        ckpt(6)
        P.barrier()
        phB.close()

        ph2 = ExitStack()
        es.enter_context(ph2)

        def sb2(name, shape, dt=F32):
            return ph2.enter_context(nc.sbuf_tensor(name, list(shape), dt))

        hnT = sb2("hnT", [128, 8, NTOK], BF16)
        t_hnT = P.toks(NT, "hnT")
        yacc = sb2("yacc", [128, NT, D])
        t_yacc = P.toks(NT, "yacc")
        FFb = sb2("FFb", [128, 2, 16384], BF16)
        t_FF = P.toks(2, "FF")
        stg2 = [sb2("stg2_%d" % i, [128, 1024]) for i in range(2)]
        t_stg2 = P.toks(2, "stg2")
        ndma = [0]
        cvt_engs = [(A, lambda o, i: nc.scalar.activation(out=o, in_=i, func=AF.Copy)),
                    (V, lambda o, i: nc.vector.tensor_copy(out=o, in_=i)),
                    (PL, lambda o, i: nc.gpsimd.tensor_copy(out=o, in_=i))]

        def load_cvt(src_ap, dst_ap, t_dst):
            b = ndma[0] % 2
            ndma[0] += 1
            P.dma(SP, lambda: nc.sync.dma_start(out=stg2[b][:], in_=src_ap), w=[t_stg2[b]])
            for k in range(2):
                en, f = cvt_engs[(2 * ndma[0] + k) % 3]
                P.op(en, lambda k=k, f=f: f(dst_ap[:, k * 512:(k + 1) * 512], stg2[b][:, k * 512:(k + 1) * 512]),
                     r=[t_stg2[b]], w=[t_dst])

        WO = FFb[:, 1, 0:8192].rearrange("p (c n) -> p c n", c=8)
        for c in range(8):
            load_cvt(w_out[c * 128:(c + 1) * 128, :], WO[:, c, :], t_FF[1])

        def load_quarter(q):
            bq = q % 2
            f1 = FFb[:, bq, 0:8192].rearrange("p (c n) -> p c n", c=8)
            f2 = FFb[:, bq, 8192:16384].rearrange("p (c n) -> p c n", c=8)
            for c in range(8):
                load_cvt(w_ff1[c * 128:(c + 1) * 128, q * 1024:(q + 1) * 1024], f1[:, c, :], t_FF[bq])
            for j in range(8):
                load_cvt(w_ff2[(q * 8 + j) * 128:(q * 8 + j + 1) * 128, :], f2[:, j, :], t_FF[bq])

        load_quarter(0)
        ot = [sb2("ot%d" % i, [128, 8, 128], BF16) for i in range(2)]
        t_ot = P.toks(2, "ot")
        xt2 = [sb2("xt2_%d" % i, [128, D]) for i in range(2)]
        t_xt2 = P.toks(2, "xt2")
        hn = sb2("hn", [128, D], BF16)
        t_hn = P.tok("hn")
        stat2 = sb2("stat2", [128, 8])
        t_stat2p = P.tok("stat2p")
        def p2a_front(ti):
            b = ti % 2
            r0 = ti * 128
            P.dma(SP, lambda: nc.sync.dma_start(out=xt2[b][:], in_=x_all[r0:r0 + 128, :]), w=[t_xt2[b]])
            P.dma(SP, lambda: nc.sync.dma_start(out=ot[b][:], in_=o_scr[ti, :, :, :]), w=[t_ot[b]])
            for half in range(2):
                for c in range(8):
                    P.op(PE, lambda c=c, half=half: nc.tensor.matmul(
                        pbank[half][:, :], lhsT=ot[b][:, c, :], rhs=WO[:, c, half * 512:(half + 1) * 512],
                        start=(c == 0), stop=(c == 7)), r=[t_ot[b], t_FF[1]], w=[t_pb[half]])
                P.op(V, lambda half=half: nc.vector.tensor_tensor(out=yacc[:, ti, half * 512:(half + 1) * 512], in0=pbank[half][:, :],
                                                                  in1=xt2[b][:, half * 512:(half + 1) * 512], op=ALU.add),
                     r=[t_pb[half], t_xt2[b]], w=[t_yacc[ti]])

        def p2a_back(ti):
            r0 = ti * 128
            st_ = t_stat2p
            P.op(A, lambda: nc.scalar.activation(out=hn[:], in_=yacc[:, ti, :], func=AF.Square, accum_out=stat2[:, 0:1]), r=[t_yacc[ti]], w=[t_hn, st_])
            P.op(A, lambda: nc.scalar.activation(out=stat2[:, 1:2], in_=stat2[:, 0:1], func=AF.Ln, scale=1.0 / D, bias=eps_c[:, 0:1]), r=[st_] + CONST, w=[st_])
            P.op(A, lambda: nc.scalar.activation(out=stat2[:, 2:3], in_=stat2[:, 1:2], func=AF.Exp, scale=-0.5), r=[st_], w=[st_])
            P.op(V, lambda: nc.vector.tensor_scalar(out=hn[:], in0=yacc[:, ti, :], scalar1=stat2[:, 2:3], scalar2=None, op0=ALU.mult),
                 r=[t_yacc[ti], st_], w=[t_hn])
            pbT = pb_bf(7)
            for c in range(8):
                P.op(PE, lambda c=c: nc.tensor.transpose(pbT[:, c * 128:(c + 1) * 128], hn[:, c * 128:(c + 1) * 128], ident_b[:]),
                     r=[t_hn] + CONST, w=[t_pb[7]])
            P.op(V, lambda: nc.vector.tensor_tensor(
                out=hnT[:, :, r0:r0 + 128], in0=pbT.rearrange("p (c t) -> p c t", c=8),
                in1=gffn[:].unsqueeze(2).broadcast_to([128, 8, 128]), op=ALU.mult), r=[t_pb[7]] + CONST, w=[t_hnT[ti]])

        p2a_front(0)
        for ti in range(NT):
            if ti + 1 < NT:
                p2a_front(ti + 1)
            p2a_back(ti)

        ckpt(7)
        hidT = sb2("hidT", [128, 8, 512], BF16)
        t_hid = P.tok("hid")
        relu_t = [sb2("relu_t%d" % i, [128, 512], BF16) for i in range(2)]
        t_relu = P.toks(2, "relu")
        nblk = (NTOK + 511) // 512
        for q in range(4):
            bq = q % 2
            if q + 1 < 4:
                load_quarter(q + 1)
            f1 = FFb[:, bq, 0:8192].rearrange("p (c n) -> p c n", c=8)
            f2 = FFb[:, bq, 8192:16384].rearrange("p (c n) -> p c n", c=8)
            for blk in range(nblk):
                c0 = blk * 512
                ncol = min(512, NTOK - c0)
                tiles = list(range(c0 // 128, (c0 + ncol) // 128))
                for j in range(8):
                    bk = 2 + j % 4
                    for c in range(8):
                        P.op(PE, lambda c=c, j=j, bk=bk, c0=c0, ncol=ncol, f1=f1: nc.tensor.matmul(
                            pbank[bk][:, 0:ncol], lhsT=f1[:, c, j * 128:(j + 1) * 128], rhs=hnT[:, c, c0:c0 + ncol],
                            start=(c == 0), stop=(c == 7)), r=[t_FF[bq]] + [t_hnT[t] for t in tiles], w=[t_pb[bk]])
                    rb = j % 2
                    tr = t_relu[rb]
                    P.op(A, lambda bk=bk, ncol=ncol, rb=rb: nc.scalar.activation(out=relu_t[rb][:, 0:ncol], in_=pbank[bk][:, 0:ncol], func=AF.Relu),
                         r=[t_pb[bk]], w=[tr])
                    P.op(V, lambda j=j, bk=bk, ncol=ncol, rb=rb: nc.vector.tensor_tensor(out=hidT[:, j, 0:ncol], in0=pbank[bk][:, 0:ncol],
                                                                                in1=relu_t[rb][:, 0:ncol], op=ALU.mult),
                         r=[t_pb[bk], tr], w=[t_hid])
                for t in tiles:
                    r0 = t * 128
                    lo = r0 - c0
                    for half in range(2):
                        for j in range(8):
                            P.op(PE, lambda j=j, half=half, lo=lo, f2=f2: nc.tensor.matmul(
                                pbank[half][:, :], lhsT=hidT[:, j, lo:lo + 128], rhs=f2[:, j, half * 512:(half + 1) * 512],
                                start=(j == 0), stop=(j == 7)), r=[t_hid, t_FF[bq]], w=[t_pb[half]])
                        P.op(V, lambda t=t, half=half: nc.vector.tensor_tensor(out=yacc[:, t, half * 512:(half + 1) * 512], in0=pbank[half][:, :],
                                                                                in1=yacc[:, t, half * 512:(half + 1) * 512], op=ALU.add),
                             r=[t_pb[half], t_yacc[t]], w=[t_yacc[t]])
                    if q == 3:
                        P.dma(SP, lambda t=t, r0=r0: nc.sync.dma_start(out=y_all[r0:r0 + 128, :], in_=yacc[:, t, :]), r=[t_yacc[t]])

      except _Stop:
        pass
      info = P.emit()
      build_program.info = info
    return nc


_CACHE = {}


def _consts():
    ident = np.eye(128, dtype=np.float32)
    half = 8
    inv = (500000.0 ** (-np.arange(0, 16, 2, dtype=np.float32) / 16.0)).astype(np.float32)
    rope = np.zeros((128, NT, 16), np.float32)
    for t in range(NT):
        if t < NPT:
            pos = (t * 128 + np.arange(128)).astype(np.float32)
        else:
            pos = (PAST + (np.arange(128) % 32)).astype(np.float32)
        ang = pos[:, None] * inv[None, :]
        rope[:, t, 0:8] = np.cos(ang)
        rope[:, t, 8:16] = np.sin(ang)
    masks = np.zeros((128, 8, 128), np.float32)
    idx = np.arange(128)
    for base, C in ((0, 64), (3, 32)):
        same = (idx[:, None] // C) == (idx[None, :] // C)
        masks[:, base + 0, :] = same & (idx[:, None] > idx[None, :])
        masks[:, base + 1, :] = same & (idx[:, None] < idx[None, :])
        masks[:, base + 2, :] = same & (idx[:, None] <= idx[None, :])
        masks[:, 6 if C == 64 else 7, :] = same
    small = np.zeros((128, 16), np.float32)
    colm = np.zeros((64, 6, 128), np.float32)
    for c in range(2):
        small[c * 64, 0 + c] = 1.0
        small[:, 6 + c] = (idx // 64 == c)
        colm[:, 0 + c, :] = (idx // 64 == c)[None, :]
    for c in range(4):
        small[c * 32, 2 + c] = 1.0
        small[:, 8 + c] = (idx // 32 == c)
        colm[:, 2 + c, :] = (idx // 32 == c)[None, :]
    sels = np.zeros((4, 128), np.float32)
    for s_ in range(4):
        sels[s_, 32 * s_] = 1.0
    return ident, rope, masks, small, colm, sels


def kernel(**inputs):
    f = lambda a: np.ascontiguousarray(np.asarray(a, dtype=np.float32))
    if "nc" not in _CACHE:
        _CACHE["nc"] = build_program()
    nc = _CACHE["nc"]
    ident, rope, masks, small, colm, sels = _consts()
    xp = f(inputs["x_prompt"])
    xs = f(inputs["x_sample"])
    shared = {
        "w_in": f(inputs["w_in"][0]), "w_out": f(inputs["w_out"][0]), "w_ff1": f(inputs["w_ff1"][0]), "w_ff2": f(inputs["w_ff2"][0]),
        "norm_mix": f(inputs["norm_mix"][0]), "norm_ffn": f(inputs["norm_ffn"][0]), "q_gain": f(inputs["q_gain"][0]),
        "k_gain": f(inputs["k_gain"][0]), "kidx_ln_g": f(inputs["kidx_ln_g"][0]), "kidx_ln_b": f(inputs["kidx_ln_b"][0]),
        "mu_shift": f(inputs["mu_shift"][0]), "w0": f(inputs["w0"][0]), "w2": f(inputs["w2"][0]), "a0": f(inputs["a0"][0]),
        "a2": f(inputs["a2"][0]), "g2": f(inputs["g2"][0]), "k_k": f(inputs["k_k"][0]), "k_a": f(inputs["k_a"][0]),
        "r_k": f(inputs["r_k"][0]).reshape(512), "ln_x_g": f(inputs["ln_x_g"][0]), "ln_x_b": f(inputs["ln_x_b"][0]),
        "c_ident": ident, "c_rope": rope, "c_masks": masks, "c_small": small, "c_colm": colm, "c_sels": sels,
    }
    in_maps = []
    for c in range(NCORE):
        m = dict(shared)
        m["x_all"] = np.ascontiguousarray(np.concatenate([xp[c], xs[4 * c:4 * c + 4].reshape(128, D)], axis=0))
        m["ck"] = f(inputs["cache_k"][0, 4 * c:4 * c + 4]).reshape(4, PAST, 256)
        m["cv"] = f(inputs["cache_v"][0, 4 * c:4 * c + 4]).reshape(4, PAST, 256)
        m["cki"] = f(inputs["cache_kidx"][0, 4 * c:4 * c + 4])
        m["swkv"] = f(inputs["state_wkv"][0, 4 * c:4 * c + 4])
        m["ssh"] = f(inputs["state_shift"][0, 4 * c:4 * c + 4, 0])
        in_maps.append(m)
    res = run_bass_kernel_spmd(nc, in_maps, core_ids=list(range(NCORE)))
    R = res.results
    cat = lambda k: np.stack([np.asarray(R[c][k]) for c in range(NCORE)])
    y_all = cat("y_all")
    k_all = cat("k_all")
    v_all = cat("v_all")
    ki_all = cat("ki_all")
    y_p = y_all[:, :SEQ].reshape(8, SEQ, D)
    y_s = y_all[:, SEQ:].reshape(DEC_B, DEC_T, D)
    k_p = k_all[:, :SEQ].reshape(1, 8, SEQ, 4, 64)
    v_p = v_all[:, :SEQ].reshape(1, 8, SEQ, 4, 64)
    ki_p = ki_all[:, :SEQ].reshape(1, 8, SEQ, 64)
    k_s = k_all[:, SEQ:].reshape(1, DEC_B, DEC_T, 4, 64)
    v_s = v_all[:, SEQ:].reshape(1, DEC_B, DEC_T, 4, 64)
    ki_s = ki_all[:, SEQ:].reshape(1, DEC_B, DEC_T, 64)
    wkv_p = cat("wkv_p").reshape(1, 8, 8, 64, 64)
    wkv_s = cat("wkv_s").reshape(1, DEC_B, 8, 64, 64)
    sh_p = cat("sh_p").reshape(1, 8, 1, SW)
    sh_s = cat("sh_s").reshape(1, DEC_B, 1, SW)
    out = (y_p, y_s, k_p, v_p, ki_p, wkv_p, sh_p, k_s, v_s, ki_s, wkv_s, sh_s)
    return tuple(np.ascontiguousarray(o, dtype=np.float32) for o in out)
```

```python
import numpy as np
from contextlib import ExitStack
import concourse.bass as bass
import concourse.mybir as mybir
from concourse.bass_utils import run_bass_kernel_spmd

F32 = mybir.dt.float32
BF16 = mybir.dt.bfloat16
ALU = mybir.AluOpType
AF = mybir.ActivationFunctionType
AX = mybir.AxisListType

D = 1024
NCORE = 8
SEQ = 2048
DEC_B = 32
DEC_T = 32
PAST = 1024
NPT = SEQ // 128
NT = NPT + 1
NTOK = NT * 128
HD = 64
PROJ = 3304
SW = 1696
NA = 1608
DFF = 4096
TOPK = 256
NORM_EPS = 1e-6
GN_EPS = 64e-5


class Tok:
    __slots__ = ("name", "w", "rd", "rd_dma")

    def __init__(self, name):
        self.name = name
        self.w = None
        self.rd = {}
        self.rd_dma = []


class Op:
    __slots__ = ("eng", "fn", "deps", "signal", "dma", "sig", "clock")

    def __init__(self, eng, fn, dma):
        self.eng = eng
        self.fn = fn
        self.deps = set()
        self.signal = False
        self.dma = dma
        self.sig = None
        self.clock = None


class Prog:
    NSLOT = 40

    def __init__(self, nc, es):
        self.nc = nc
        self.E = {"pe": nc.tensor, "dve": nc.vector, "act": nc.scalar, "pool": nc.gpsimd, "sp": nc.sync}
        self.ops = []
        self.sems = {e: es.enter_context(nc.semaphore("sem_" + e)) for e in self.E}
        self.slots = [es.enter_context(nc.semaphore("dsl%d" % i)) for i in range(self.NSLOT)]
        self.dma_ops = []
        self.bar_toks = []
        self.tiny = {}
        self.embed_wait = True

    def tok(self, name="t"):
        return Tok(name)

    def toks(self, n, name="t"):
        return [Tok(name + str(i)) for i in range(n)]

    def _record(self, eng, fn, r, w, dma):
        idx = len(self.ops)
        op = Op(eng, fn, dma)

        def add(d, kind):
            if d is None:
                return
            dop = self.ops[d]
            if (not dma) and (not dop.dma) and dop.eng == eng:
                if eng == "pe":
                    return
            op.deps.add(d)
            dop.signal = True

        for t in r:
            add(t.w, "raw")
        for t in w:
            add(t.w, "waw")
            for d in t.rd.values():
                add(d, "war")
            for d in t.rd_dma:
                add(d, "war")
        for t in w:
            t.w = idx
            t.rd = {}
            t.rd_dma = []
        for t in r:
            if dma:
                t.rd_dma.append(idx)
            else:
                t.rd[eng] = idx
        self.ops.append(op)
        if dma:
            self.dma_ops.append(idx)
            op.signal = True
        return idx

    def op(self, eng, fn, r=(), w=()):
        return self._record(eng, fn, r, w, False)

    def dma(self, q, fn, r=(), w=()):
        return self._record(q, fn, list(r) + self.bar_toks, w, True)

    def barrier(self):
        CE = ["pe", "dve", "act", "pool"]
        bt = {e: Tok("bar_" + e) for e in CE}
        pend = list(self.dma_ops)
        self.dma_ops = []
        for e in CE:
            fn, xr, xw = self.tiny[e]
            i = self.op(e, fn, r=xr, w=[bt[e]] + xw)
            if e == "act":
                for d in pend:
                    self.ops[i].deps.add(d)
        bt2 = {e: Tok("bar2_" + e) for e in CE}
        for e in CE:
            fn, xr, xw = self.tiny[e]
            self.op(e, fn, r=list(bt.values()) + xr, w=[bt2[e]] + xw)
        self.bar_toks = list(bt.values())

    def emit(self):
        clock = {e: {} for e in self.E}
        count = {e: 0 for e in self.E}
        slot_uses = [0] * self.NSLOT
        nslot = 0
        nwait = 0
        for op in self.ops:
            e = op.eng
            eng = self.E[e]
            ck = clock[e]
            deps = sorted(op.deps, reverse=True)
            pend_waits = []
            for d in deps:
                dop = self.ops[d]
                key, val = dop.sig
                if ck.get(key, 0) >= val:
                    continue
                sem = self.sems[key] if isinstance(key, str) else self.slots[key]
                pend_waits.append((sem, val))
                nwait += 1
                for k2, v2 in dop.clock.items():
                    if ck.get(k2, 0) < v2:
                        ck[k2] = v2
            emb = None
            if pend_waits and not op.dma and self.embed_wait:
                emb = pend_waits.pop()
            for (sem, val) in pend_waits:
                eng.wait_ge(sem, val)
            if op.dma:
                j = nslot % self.NSLOT
                nslot += 1
                prev = 16 * slot_uses[j]
                if ck.get(j, 0) < prev:
                    eng.wait_ge(self.slots[j], prev)
                    ck[j] = prev
                ins = op.fn()
                ins.then_inc(self.slots[j], 16)
                slot_uses[j] += 1
                op.sig = (j, 16 * slot_uses[j])
                op.clock = dict(ck)
                op.clock[j] = op.sig[1]
            else:
                ins = op.fn()
                if emb is not None:
                    ins._wait_ge(emb[0], emb[1])
                if op.signal:
                    count[e] += 1
                    ins.then_inc(self.sems[e], 1)
                    op.sig = (e, count[e])
                    op.clock = dict(ck)
                    op.clock[e] = count[e]
        for j in range(self.NSLOT):
            if slot_uses[j]:
                self.nc.sync.wait_ge(self.slots[j], 16 * slot_uses[j])
        return dict(nops=len(self.ops), nwait=nwait, count=count)


class _Stop(Exception):
    pass


def build_program(stage=3, stop=0, bar_mode=3):
    nc = bass.Bass("TRN2", target_bir_lowering=False)

    def din(name, shape):
        return nc.dram_tensor(name, list(shape), F32, kind="ExternalInput").ap()

    def dout(name, shape):
        return nc.dram_tensor(name, list(shape), F32, kind="ExternalOutput").ap()

    x_all = din("x_all", [NTOK, D])
    ck_in = din("ck", [4, PAST, 256])
    cv_in = din("cv", [4, PAST, 256])
    cki_in = din("cki", [4, PAST, 64])
    swkv_in = din("swkv", [4, 8, 64, 64])
    ssh_in = din("ssh", [4, SW])
    w_in = din("w_in", [D, PROJ])
    w_out = din("w_out", [D, D])
    w_ff1 = din("w_ff1", [D, DFF])
    w_ff2 = din("w_ff2", [DFF, D])
    norm_mix = din("norm_mix", [D])
    norm_ffn = din("norm_ffn", [D])
    q_gain = din("q_gain", [64])
    k_gain = din("k_gain", [64])
    kln_g = din("kidx_ln_g", [64])
    kln_b = din("kidx_ln_b", [64])
    mu_in = din("mu_shift", [SW])
    w0_in = din("w0", [512])
    w2_in = din("w2", [32, 512])
    a0_in = din("a0", [512])
    a2_in = din("a2", [32, 512])
    g2_in = din("g2", [96, 512])
    kk_in = din("k_k", [512])
    ka_in = din("k_a", [512])
    rk_in = din("r_k", [512])
    lng_in = din("ln_x_g", [512])
    lnb_in = din("ln_x_b", [512])
    c_ident = din("c_ident", [128, 128])
    c_rope = din("c_rope", [128, NT, 16])
    c_masks = din("c_masks", [128, 8, 128])
    c_small = din("c_small", [128, 16])
    c_colm = din("c_colm", [64, 6, 128])
    c_sels = din("c_sels", [4, 128])

    y_all = dout("y_all", [NTOK, D])
    k_all = dout("k_all", [NTOK, 256])
    v_all = dout("v_all", [NTOK, 256])
    ki_all = dout("ki_all", [NTOK, 64])
    wkv_p = dout("wkv_p", [8, 64, 64])
    wkv_s = dout("wkv_s", [4, 8, 64, 64])
    sh_p = dout("sh_p", [1, SW])
    sh_s = dout("sh_s", [4, SW])
    o_scr = nc.dram_tensor("o_scr", [NT, 128, 8, 128], BF16, kind="Internal").ap()

    es = ExitStack()

    def ckpt(n):
        if stop == n:
            raise _Stop()

    with es:
      P = Prog(nc, es)
      try:

        def sb(name, shape, dt=F32):
            return es.enter_context(nc.sbuf_tensor(name, list(shape), dt))

        def ps(name, shape, dt=F32):
            return es.enter_context(nc.psum_tensor(name, list(shape), dt))

        V, A, PL, PE, SP = "dve", "act", "pool", "pe", "sp"

        V, A, PL, PE, SP = "dve", "act", "pool", "pe", "sp"
        STAGE = stage
        BAR_MODE = bar_mode
        ATT = (bar_mode & 4) == 0

        ident_f = sb("ident_f", [128, 128])
        ident_b = sb("ident_b", [128, 128], BF16)
        rope = sb("rope", [128, NT, 16])
        gmix = sb("gmix", [128, 8])
        gffn = sb("gffn", [128, 8])
        qg_bc = sb("qg_bc", [128, 64])
        kg_bc = sb("kg_bc", [128, 64])
        lg_bc = sb("lg_bc", [128, 64])
        lb_bc = sb("lb_bc", [128, 64])
        zero_b = sb("zero_b", [128, 512], BF16)
        t_const = P.tok("const")

        def bcast_load(dst, src, t=None):
            P.dma(SP, lambda: nc.sync.dma_start(out=dst[:], in_=src.partition_broadcast(128)), w=[t or t_const])

        P.dma(SP, lambda: nc.sync.dma_start(out=ident_f[:], in_=c_ident[:, :]), w=[t_const])
        P.dma(SP, lambda: nc.sync.dma_start(out=rope[:], in_=c_rope[:, :, :]), w=[t_const])
        P.dma(SP, lambda: nc.sync.dma_start(out=gmix[:], in_=norm_mix.rearrange("(c p) -> p c", p=128), allow_slow_non_contiguous=True), w=[t_const])
        P.dma(SP, lambda: nc.sync.dma_start(out=gffn[:], in_=norm_ffn.rearrange("(c p) -> p c", p=128), allow_slow_non_contiguous=True), w=[t_const])
        bcast_load(qg_bc, q_gain)
        bcast_load(kg_bc, k_gain)
        bcast_load(lg_bc, kln_g)
        bcast_load(lb_bc, kln_b)
        t_const2 = P.tok("const2")
        P.op(V, lambda: nc.vector.tensor_copy(out=ident_b[:], in_=ident_f[:]), r=[t_const], w=[t_const2])
        P.op(V, lambda: nc.vector.memset(zero_b[:], 0.0), w=[t_const2])
        eps_c = sb("eps_c", [128, 2])
        P.op(V, lambda: nc.vector.memset(eps_c[:, 0:1], NORM_EPS), w=[t_const2])
        P.op(V, lambda: nc.vector.memset(eps_c[:, 1:2], GN_EPS), w=[t_const2])
        CONST = [t_const, t_const2]
        pbank = [ps("pb%d" % i, [128, 512]) for i in range(8)]
        t_pb = P.toks(8, "pb")

        def pb_bf(i):
            return pbank[i][:].bitcast(BF16)

        ckpt(1)
        bscr_v = sb("bscr_v", [128, 2])
        bscr_a = sb("bscr_a", [128, 2])
        bscr_p = sb("bscr_p", [128, 2])
        t_bsv, t_bsa, t_bsp = P.tok("bsv"), P.tok("bsa"), P.tok("bsp")
        P.tiny = {
            "dve": (lambda: nc.vector.memset(bscr_v[:, 0:1], 0.0), [], [t_bsv]),
            "act": (lambda: nc.scalar.activation(out=bscr_a[:, 0:1], in_=eps_c[:, 0:1], func=AF.Copy), CONST, [t_bsa]),
            "pool": (lambda: nc.gpsimd.memset(bscr_p[:, 0:1], 0.0), [], [t_bsp]),
            "pe": (lambda: nc.tensor.transpose(pb_bf(7)[0:1, 0:1], ident_b[0:1, 0:1], ident_b[0:1, 0:1]), CONST, [t_pb[7]]),
        }
        def norm_tile(ti, xtb, t_x, junk, t_junk, xn, t_xn, stat, xT, t_xT, st_):
            r0 = ti * 128
            P.dma(SP, lambda: nc.sync.dma_start(out=xtb[:], in_=x_all[r0:r0 + 128, :]), w=[t_x])
            P.op(A, lambda: nc.scalar.activation(out=junk[:], in_=xtb[:], func=AF.Square, accum_out=stat[:, 0:1]),
                 r=[t_x], w=[t_junk, st_])
            P.op(A, lambda: nc.scalar.activation(out=stat[:, 1:2], in_=stat[:, 0:1], func=AF.Ln, scale=1.0 / D, bias=eps_c[:, 0:1]), r=[st_] + CONST, w=[st_])
            P.op(A, lambda: nc.scalar.activation(out=stat[:, 2:3], in_=stat[:, 1:2], func=AF.Exp, scale=-0.5), r=[st_], w=[st_])
            P.op(V, lambda: nc.vector.tensor_scalar(out=xn[:], in0=xtb[:], scalar1=stat[:, 2:3], scalar2=None, op0=ALU.mult),
                 r=[t_x, st_], w=[t_xn])
            pbT = pb_bf(7)
            for c in range(8):
                P.op(PE, lambda c=c: nc.tensor.transpose(pbT[:, c * 128:(c + 1) * 128], xn[:, c * 128:(c + 1) * 128], ident_b[:]),
                     r=[t_xn] + CONST, w=[t_pb[7]])
            P.op(V, lambda: nc.vector.tensor_tensor(
                out=xT[:], in0=pbT.rearrange("p (c t) -> p c t", c=8),
                in1=gmix[:].unsqueeze(2).broadcast_to([128, 8, 128]), op=ALU.mult), r=[t_pb[7]] + CONST, w=[t_xT])


        phA = ExitStack()
        es.enter_context(phA)

        def sbA(name, shape, dt=F32):
            return phA.enter_context(nc.sbuf_tensor(name, list(shape), dt))

        WA = sbA("WA", [128, 8, NA], BF16)
        t_W = P.tok("W")
        with ExitStack() as st:
            stg = [st.enter_context(nc.sbuf_tensor("wstg%d" % i, [128, NA], F32)) for i in range(2)]
            t_stg = P.toks(2, "wstg")
            for c in range(8):
                b = c % 2
                P.dma(SP, lambda c=c, b=b: nc.sync.dma_start(out=stg[b][:], in_=w_in[c * 128:(c + 1) * 128, 0:NA]), w=[t_stg[b]])
                P.op(A, lambda c=c, b=b: nc.scalar.activation(out=WA[:, c, 0:800], in_=stg[b][:, 0:800], func=AF.Copy), r=[t_stg[b]], w=[t_W])
                P.op(V, lambda c=c, b=b: nc.vector.tensor_copy(out=WA[:, c, 800:NA], in_=stg[b][:, 800:NA]), r=[t_stg[b]], w=[t_W])
            P.barrier()

        ckpt(2)
        xt = [sbA("xt%d" % i, [128, D]) for i in range(2)]
        t_xt = P.toks(2, "xt")
        junk = sbA("junk", [128, D], BF16)
        t_junk = P.tok("junk")
        xn = sbA("xn", [128, D], BF16)
        t_xn = P.tok("xn")
        xnT = [sbA("xnT%d" % i, [128, 8, 128], BF16) for i in range(2)]
        t_xnT = P.toks(2, "xnT")
        stat = sbA("stat", [128, 64])
        t_statA = P.tok("statA")
        t_st2 = P.tok("st2")
        RB = sbA("RB", [128, 21, 64])
        t_RB = P.tok("RB")
        vf = sbA("vf", [128, 256])
        t_vf = P.tok("vf")
        sq = sbA("sq", [128, 768])
        t_sq = P.tok("sq")
        wi_abs = sbA("wi_abs", [128, 8])
        wi_sgn = sbA("wi_sgn", [128, 8])
        t_wi = P.tok("wi")
        oatt = sbA("oatt", [128, 4, 128], BF16)
        t_oatt = P.tok("oatt")

        LKMAX = 2048
        KT = sbA("KT", [64, 4, LKMAX], BF16)
        KiT = sbA("KiT", [64, LKMAX], BF16)
        Vx = sbA("Vx", [128, 16, 4, 65], BF16)
        t_KT = P.toks(17, "KT")
        t_KiT = P.toks(17, "KiT")
        t_Vx = P.toks(17, "Vx")
        RBb = sbA("RBb", [128, 21, 64], BF16)
        t_RBb = P.tok("RBb")
        QT = sbA("QT", [64, 8, 128], BF16)
        QiT = sbA("QiT", [64, 8, 128], BF16)
        KTn = sbA("KTn", [64, 5, 128], BF16)
        t_QT, t_QiT, t_KTn = P.tok("QT"), P.tok("QiT"), P.tok("KTn")
        score = sbA("score", [128, LKMAX])
        t_score = P.tok("score")
        rl = [sbA("rl%d" % i, [128, 512], BF16) for i in range(2)]
        t_rl = P.toks(2, "rl")
        maskb = sbA("maskb", [128, LKMAX], BF16)
        t_mask = P.tok("mask")
        maskT = sbA("maskT", [128, 16, 128], BF16)
        t_maskT = P.tok("maskT")
        PTall = sbA("PTall", [128, 16, 8, 128], BF16)
        t_PTk = P.toks(16, "PTk")
        PTm = [sbA("PTm%d" % i, [128, 8, 128], BF16) for i in range(2)]
        t_PTm = P.toks(2, "PTm")
        bs = sbA("bs", [128, 64])
        sgn_s = sbA("sgn_s", [32, 8])
        t_sgn_s = P.tok("sgn_s")
        t_bs = P.tok("bs")
        pw2 = sbA("pw2", [128, 20])
        osb = sbA("osb", [128, 512], BF16)
        t_osb = P.tok("osb")
        kstg = sbA("kstg", [128, 8, 256])
        t_kstg = P.tok("kstg")
        kstb = sbA("kstb", [128, 8, 256], BF16)
        t_kstb = P.tok("kstb")
        NBIS = 16
        for j in range(NBIS):
            P.op(V, lambda j=j: nc.vector.memset(pw2[:, j:j + 1], 0.5 ** (j + 1)), w=[t_const2])
        P.op(V, lambda: nc.vector.memset(Vx[:], 1.0), w=t_Vx)

        def attn_unit(nq, qc0, L, kt_toks, ki_toks, v_toks, static_mask, causal_tail, out_col0, sgn, t_sgn):
            nkt = (L + 127) // 128
            tb = t_bs
            def emit_st_exp():
                for kt in range(nkt):
                    nk = min(128, L - kt * 128)
                    for hb in range(2):
                        for gg in range(2):
                            g = hb * 2 + gg
                            P.op(PE, lambda g=g, gg=gg, hb=hb, kt=kt, nk=nk: nc.tensor.matmul(
                                pbank[hb][0:nk, gg * 2 * nq:(gg + 1) * 2 * nq].rearrange("p (a q) -> p a q", a=2),
                                lhsT=KT[:, g, kt * 128:kt * 128 + nk], rhs=QT[:, 2 * g:2 * g + 2, qc0:qc0 + nq], start=True, stop=True),
                                r=[t_QT] + kt_toks, w=[t_pb[hb]])
                        P.op(A, lambda hb=hb, nk=nk, kt=kt: nc.scalar.activation(
                            out=PTall[0:nk, kt, hb * 4:hb * 4 + 4, 0:nq], in_=pbank[hb][0:nk, 0:4 * nq].rearrange("p (a q) -> p a q", a=4),
                            func=AF.Exp, scale=0.125), r=[t_pb[hb]], w=[t_PTk[kt]])

            if not static_mask:
                cnt = 0
                for kc in range(0, L, 512):
                    n = min(512, L - kc)
                    for h in range(8):
                        bk = 4 + cnt % 2
                        rb = cnt % 2
                        cnt += 1
                        P.op(PE, lambda h=h, kc=kc, n=n, bk=bk: nc.tensor.matmul(
                            pbank[bk][0:nq, 0:n], lhsT=QiT[:, h, qc0:qc0 + nq], rhs=KiT[:, kc:kc + n], start=True, stop=True),
                            r=[t_QiT] + ki_toks, w=[t_pb[bk]])
                        P.op(A, lambda n=n, bk=bk, rb=rb: nc.scalar.activation(out=rl[rb][0:nq, 0:n], in_=pbank[bk][0:nq, 0:n], func=AF.Relu),
                             r=[t_pb[bk]], w=[t_rl[rb]])
                        if h == 0:
                            P.op(V, lambda kc=kc, n=n, rb=rb: nc.vector.tensor_scalar(
                                out=score[0:nq, kc:kc + n], in0=rl[rb][0:nq, 0:n], scalar1=sgn[0:nq, 0:1], scalar2=None, op0=ALU.mult),
                                r=[t_rl[rb], t_sgn], w=[t_score])
                        else:
                            P.op(V, lambda h=h, kc=kc, n=n, rb=rb: nc.vector.scalar_tensor_tensor(
                                out=score[0:nq, kc:kc + n], in0=rl[rb][0:nq, 0:n], scalar=sgn[0:nq, h:h + 1],
                                in1=score[0:nq, kc:kc + n], op0=ALU.mult, op1=ALU.add), r=[t_rl[rb], t_sgn, t_score], w=[t_score])
                emit_st_exp()
                P.op(V, lambda: nc.vector.tensor_reduce(out=bs[0:nq, 0:1], in_=score[0:nq, 0:L], axis=AX.X, op=ALU.min), r=[t_score], w=[tb])
                P.op(V, lambda: nc.vector.tensor_reduce(out=bs[0:nq, 1:2], in_=score[0:nq, 0:L], axis=AX.X, op=ALU.max), r=[t_score], w=[tb])
                if causal_tail:
                    P.op(V, lambda: nc.vector.memset(score[0:64, L - 64:L], -1e30), w=[t_score])
                P.op(V, lambda: nc.vector.tensor_tensor(out=bs[0:nq, 2:3], in0=bs[0:nq, 1:2], in1=bs[0:nq, 0:1], op=ALU.subtract), r=[tb], w=[tb])
                P.op(V, lambda: nc.vector.tensor_scalar(out=bs[0:nq, 8:8 + NBIS], in0=pw2[0:nq, 0:NBIS], scalar1=bs[0:nq, 2:3], scalar2=None, op0=ALU.mult),
                     r=[tb] + CONST, w=[tb])
                P.op(V, lambda: nc.vector.tensor_tensor(out=bs[0:nq, 3:4], in0=bs[0:nq, 0:1], in1=bs[0:nq, 8:9], op=ALU.add), r=[tb], w=[tb])
                for j in range(NBIS):
                    P.op(V, lambda: nc.vector.tensor_scalar(out=maskb[0:nq, 0:L], in0=score[0:nq, 0:L], scalar1=bs[0:nq, 3:4], scalar2=None,
                                                            op0=ALU.is_ge, op1=ALU.add, accum_out=bs[0:nq, 4:5]), r=[tb, t_score], w=[t_mask, tb])
                    if j < NBIS - 1:
                        P.op(V, lambda j=j: nc.vector.tensor_scalar(out=bs[0:nq, 5:6], in0=bs[0:nq, 4:5], scalar1=TOPK - 0.5, scalar2=bs[0:nq, 8 + j:9 + j],
                                                                    op0=ALU.is_ge, op1=ALU.mult), r=[tb], w=[tb])
                        P.op(V, lambda j=j: nc.vector.scalar_tensor_tensor(out=bs[0:nq, 3:4], in0=bs[0:nq, 5:6], scalar=bs[0:nq, 9 + j:10 + j], in1=bs[0:nq, 3:4],
                                                                           op0=ALU.subtract, op1=ALU.add), r=[tb], w=[tb])
                P.op(V, lambda: nc.vector.tensor_tensor(out=bs[0:nq, 0:1], in0=bs[0:nq, 3:4], in1=bs[0:nq, 8 + NBIS - 1:8 + NBIS], op=ALU.subtract), r=[tb], w=[tb])
                P.op(V, lambda: nc.vector.tensor_scalar(out=maskb[0:nq, 0:L], in0=score[0:nq, 0:L], scalar1=bs[0:nq, 0:1], scalar2=None, op0=ALU.is_ge),
                     r=[tb, t_score], w=[t_mask])
            else:
                emit_st_exp()
                P.op(V, lambda: nc.vector.memset(maskb[0:nq, 0:L], 1.0), w=[t_mask])
                if causal_tail:
                    P.op(V, lambda: nc.vector.memset(maskb[0:64, L - 64:L], 0.0), w=[t_mask])
            for g0 in range(0, nkt, 8):
                pbm = pb_bf(6)
                ks = list(range(g0, min(nkt, g0 + 8)))
                for kt in ks:
                    nk = min(128, L - kt * 128)
                    P.op(PE, lambda kt=kt, nk=nk, g0=g0, pbm=pbm: nc.tensor.transpose(
                        pbm[0:nk, (kt - g0) * 128:(kt - g0) * 128 + nq], maskb[0:nq, kt * 128:kt * 128 + nk], ident_b[0:nq, 0:nq]),
                        r=[t_mask] + CONST, w=[t_pb[6]])
                nfull = len([kt for kt in ks if L - kt * 128 >= 128])
                if nfull:
                    P.op(A, lambda g0=g0, nfull=nfull, pbm=pbm: nc.scalar.activation(
                        out=maskT[:, g0:g0 + nfull, 0:nq], in_=pbm[:, 0:nfull * 128].rearrange("p (k q) -> p k q", q=128)[:, :, 0:nq], func=AF.Copy),
                        r=[t_pb[6]], w=[t_maskT])
                if nfull < len(ks):
                    kt = ks[-1]
                    nk = L - kt * 128
                    P.op(A, lambda kt=kt, nk=nk, g0=g0, pbm=pbm: nc.scalar.activation(
                        out=maskT[0:nk, kt, 0:nq], in_=pbm[0:nk, (kt - g0) * 128:(kt - g0) * 128 + nq], func=AF.Copy), r=[t_pb[6]], w=[t_maskT])
            for ob in (2, 3):
                P.op(PE, lambda ob=ob: nc.tensor.matmul(pbank[ob][0:nq, 0:260], lhsT=zero_b[0:1, 0:nq], rhs=zero_b[0:1, 0:260], start=True, stop=False),
                     r=CONST, w=[t_pb[ob]])
            for kt in range(nkt):
                nk = min(128, L - kt * 128)
                pb_ = kt % 2
                eng, en = (nc.vector, V)
                P.op(en, lambda eng=eng, nk=nk, pb_=pb_, kt=kt: eng.tensor_tensor(
                    out=PTm[pb_][0:nk, :, 0:nq], in0=PTall[0:nk, kt, :, 0:nq],
                    in1=maskT[0:nk, kt, 0:nq].unsqueeze(1).broadcast_to([nk, 8, nq]), op=ALU.mult), r=[t_PTk[kt], t_maskT], w=[t_PTm[pb_]])
                for h in range(8):
                    ob = 2 + h // 4
                    P.op(PE, lambda h=h, ob=ob, kt=kt, nk=nk, pb_=pb_: nc.tensor.matmul(
                        pbank[ob][0:nq, (h % 4) * 65:(h % 4) * 65 + 65], lhsT=PTm[pb_][0:nk, h, 0:nq], rhs=Vx[0:nk, kt, h // 2, :],
                        start=False, stop=(kt == nkt - 1 and h % 4 == 3)), r=[t_PTm[pb_]] + v_toks, w=[t_pb[ob]])
            to = t_bs
            for ob in range(2):
                ov = pbank[2 + ob][0:nq, 0:260].rearrange("p (h d) -> p h d", d=65)
                P.op(V, lambda ov=ov, ob=ob: nc.vector.reciprocal(out=bs[0:nq, 40 + 4 * ob:44 + 4 * ob], in_=ov[:, :, 64]), r=[t_pb[2 + ob]], w=[to])
                P.op(V, lambda ov=ov, ob=ob: nc.vector.tensor_tensor(
                    out=osb[0:nq, ob * 256:(ob + 1) * 256].rearrange("p (h d) -> p h d", d=64), in0=ov[:, :, 0:64],
                    in1=bs[0:nq, 40 + 4 * ob:44 + 4 * ob].unsqueeze(2).broadcast_to([nq, 4, 64]), op=ALU.mult), r=[t_pb[2 + ob], to], w=[t_osb])
            pbo = pb_bf(7)
            for c in range(4):
                P.op(PE, lambda c=c, pbo=pbo: nc.tensor.transpose(pbo[:, c * 128:c * 128 + nq], osb[0:nq, c * 128:(c + 1) * 128], ident_b[0:nq, 0:nq]),
                     r=[t_osb] + CONST, w=[t_pb[7]])
            P.op(A, lambda pbo=pbo: nc.scalar.activation(out=oatt[:, :, out_col0:out_col0 + nq],
                                                         in_=pbo[:, 0:512].rearrange("p (c q) -> p c q", c=4)[:, :, 0:nq], func=AF.Copy),
                 r=[t_pb[7]], w=[t_oatt])

        def attention_tile(ti):
            sample = ti == NPT
            P.op(V, lambda: nc.vector.tensor_copy(out=RBb[:, 0:12, :], in_=RB[:, 0:12, :]), r=[t_RB], w=[t_RBb])
            P.op(V, lambda: nc.vector.tensor_copy(out=RBb[:, 20, :], in_=RB[:, 20, :]), r=[t_RB], w=[t_RBb])
            P.op(V, lambda: nc.vector.tensor_tensor(out=RBb[:, 12:20, :], in0=RB[:, 12:20, :],
                                                     in1=wi_abs[:].unsqueeze(2).broadcast_to([128, 8, 64]), op=ALU.mult), r=[t_RB, t_wi], w=[t_RBb])
            pq, pqi, pk = pb_bf(7), pb_bf(6), pb_bf(5)
            for h in range(8):
                P.op(PE, lambda h=h: nc.tensor.transpose(pq[0:64, h * 128:(h + 1) * 128], RBb[:, h, :], ident_b[:]), r=[t_RBb] + CONST, w=[t_pb[7]])
            P.op(A, lambda: nc.scalar.activation(out=QT[:], in_=pq[0:64, :].rearrange("p (h t) -> p h t", h=8), func=AF.Copy), r=[t_pb[7]], w=[t_QT])
            for h in range(8):
                P.op(PE, lambda h=h: nc.tensor.transpose(pqi[0:64, h * 128:(h + 1) * 128], RBb[:, 12 + h, :], ident_b[:]), r=[t_RBb] + CONST, w=[t_pb[6]])
            P.op(V, lambda: nc.vector.tensor_copy(out=QiT[:], in_=pqi[0:64, :].rearrange("p (h t) -> p h t", h=8)), r=[t_pb[6]], w=[t_QiT])
            for g in range(5):
                src = 8 + g if g < 4 else 20
                P.op(PE, lambda g=g, src=src: nc.tensor.transpose(pk[0:64, g * 128:(g + 1) * 128], RBb[:, src, :], ident_b[:]), r=[t_RBb] + CONST, w=[t_pb[5]])
            if not sample:
                r0 = ti * 128
                P.op(A, lambda r0=r0: nc.scalar.activation(out=KT[:, :, r0:r0 + 128], in_=pk[0:64, 0:512].rearrange("p (g t) -> p g t", g=4), func=AF.Copy),
                     r=[t_pb[5]], w=[t_KT[ti]])
                P.op(A, lambda r0=r0: nc.scalar.activation(out=KiT[:, r0:r0 + 128], in_=pk[0:64, 512:640], func=AF.Copy), r=[t_pb[5]], w=[t_KiT[ti]])
                P.op(V, lambda ti=ti: nc.vector.tensor_copy(out=Vx[:, ti, :, 0:64], in_=vf[:].rearrange("p (g d) -> p g d", g=4)), r=[t_vf], w=[t_Vx[ti]])
                L = (ti + 1) * 128
                attn_unit(128, 0, L, t_KT[0:ti + 1], t_KiT[0:ti + 1], t_Vx[0:ti + 1], static_mask=(L <= TOPK), causal_tail=True, out_col0=0, sgn=wi_sgn, t_sgn=t_wi)
            else:
                P.op(A, lambda: nc.scalar.activation(out=KTn[:], in_=pk[0:64, 0:640].rearrange("p (g t) -> p g t", g=5), func=AF.Copy), r=[t_pb[5]], w=[t_KTn])
                for s in range(4):
                    allk = t_KT + t_KiT + t_Vx
                    P.dma(SP, lambda s=s: nc.sync.dma_start(out=kstg[:], in_=ck_in[s].rearrange("(k p) c -> p k c", p=128)), w=[t_kstg])
                    P.op(V, lambda: nc.vector.tensor_copy(out=kstb[:], in_=kstg[:]), r=[t_kstg], w=[t_kstb])
                    for g in range(4):
                        pkc = pb_bf(4 + g % 2)
                        for kt in range(8):
                            P.op(PE, lambda g=g, kt=kt, pkc=pkc: nc.tensor.transpose(pkc[0:64, kt * 128:(kt + 1) * 128], kstb[:, kt, g * 64:(g + 1) * 64], ident_b[:]),
                                 r=[t_kstb] + CONST, w=[t_pb[4 + g % 2]])
                        P.op(A, lambda g=g, pkc=pkc: nc.scalar.activation(out=KT[:, g, 0:1024], in_=pkc[0:64, :], func=AF.Copy), r=[t_pb[4 + g % 2]], w=allk)
                    P.op(V, lambda s=s: nc.vector.tensor_copy(out=KT[:, :, 1024:1056], in_=KTn[:, 0:4, 32 * s:32 * s + 32]), r=[t_KTn], w=allk)
                    P.dma(SP, lambda s=s: nc.sync.dma_start(out=kstg[:, :, 0:64], in_=cki_in[s].rearrange("(k p) c -> p k c", p=128)), w=[t_kstg])
                    P.op(V, lambda: nc.vector.tensor_copy(out=kstb[:, :, 0:64], in_=kstg[:, :, 0:64]), r=[t_kstg], w=[t_kstb])
                    pkc = pb_bf(4)
                    for kt in range(8):
                        P.op(PE, lambda kt=kt, pkc=pkc: nc.tensor.transpose(pkc[0:64, kt * 128:(kt + 1) * 128], kstb[:, kt, 0:64], ident_b[:]),
                             r=[t_kstb] + CONST, w=[t_pb[4]])
                    P.op(A, lambda pkc=pkc: nc.scalar.activation(out=KiT[:, 0:1024], in_=pkc[0:64, :], func=AF.Copy), r=[t_pb[4]], w=allk)
                    P.op(V, lambda s=s: nc.vector.tensor_copy(out=KiT[:, 1024:1056], in_=KTn[:, 4, 32 * s:32 * s + 32]), r=[t_KTn], w=allk)
                    P.dma(SP, lambda s=s: nc.sync.dma_start(out=kstg[:], in_=cv_in[s].rearrange("(k p) c -> p k c", p=128)), w=[t_kstg])
                    P.op(PL, lambda: nc.gpsimd.tensor_copy(out=Vx[:, 0:8, :, 0:64], in_=kstg[:].rearrange("p k (g d) -> p k g d", g=4)), r=[t_kstg], w=allk)
                    P.dma(SP, lambda s=s: nc.sync.dma_start(out=kstg[0:32, 0, :], in_=vf[32 * s:32 * s + 32, :]), r=[t_vf], w=[t_kstg])
                    P.op(PL, lambda: nc.gpsimd.tensor_copy(out=Vx[0:32, 8, :, 0:64], in_=kstg[0:32, 0, :].rearrange("p (g d) -> p g d", g=4)), r=[t_kstg], w=allk)
                    P.dma(SP, lambda s=s: nc.sync.dma_start(out=sgn_s[0:32, :], in_=wi_sgn[32 * s:32 * s + 32, :]), r=[t_wi], w=[t_sgn_s])
                    attn_unit(32, 32 * s, PAST + 32, allk, allk, allk, static_mask=False, causal_tail=False, out_col0=32 * s, sgn=sgn_s, t_sgn=t_sgn_s)


        def normA(tj):
            bj = tj % 2
            norm_tile(tj, xt[bj], t_xt[bj], junk, t_junk, xn, t_xn, stat, xnT[bj], t_xnT[bj], t_statA)

        normA(0)
        for ti in range(NT):
            sample = ti == NPT
            b = ti % 2
            xT, t_xT = xnT[b], t_xnT[b]
            r0 = ti * 128
            groupsA = [(0, 512, 0), (512, 512, 1), (1024, 512, 2), (1536, 72, 3)]
            for (c0, n, bk) in groupsA:
                for c in range(8):
                    P.op(PE, lambda c=c, c0=c0, n=n, bk=bk, xT=xT: nc.tensor.matmul(
                        pbank[bk][:, 0:n], lhsT=xT[:, c, :], rhs=WA[:, c, c0:c0 + n], start=(c == 0), stop=(c == 7)),
                        r=[t_xT, t_W], w=[t_pb[bk]])
            if ti + 1 < NT:
                normA(ti + 1)
            st2 = t_st2
            P.op(A, lambda: nc.scalar.activation(out=sq[:, 0:512], in_=pbank[0][:, 0:512], func=AF.Square), r=[t_pb[0]], w=[t_sq])
            P.op(A, lambda: nc.scalar.activation(out=sq[:, 512:768], in_=pbank[1][:, 0:256], func=AF.Square), r=[t_pb[1]], w=[t_sq])
            P.op(V, lambda: nc.vector.tensor_reduce(out=stat[:, 8:20], in_=sq[:].rearrange("p (h d) -> p h d", d=64), axis=AX.X, op=ALU.add),
                 r=[t_sq], w=[st2])
            P.op(V, lambda: nc.vector.tensor_reduce(out=stat[:, 24:25], in_=pbank[3][:, 0:64], axis=AX.X, op=ALU.add), r=[t_pb[3]], w=[st2])
            P.op(A, lambda: nc.scalar.activation(out=junk[:, 0:64], in_=pbank[3][:, 0:64], func=AF.Square, accum_out=stat[:, 25:26]),
                 r=[t_pb[3]], w=[t_junk, st2])
            P.op(A, lambda: nc.scalar.activation(out=stat[:, 8:20], in_=stat[:, 8:20], func=AF.Ln, scale=1.0 / 64, bias=eps_c[:, 0:1]), r=[st2] + CONST, w=[st2])
            P.op(A, lambda: nc.scalar.activation(out=stat[:, 8:20], in_=stat[:, 8:20], func=AF.Exp, scale=-0.5), r=[st2], w=[st2])
            P.op(V, lambda: nc.vector.tensor_scalar(out=stat[:, 26:27], in0=stat[:, 24:25], scalar1=1.0 / 64, scalar2=None, op0=ALU.mult), r=[st2], w=[st2])
            P.op(V, lambda: nc.vector.tensor_tensor(out=stat[:, 27:28], in0=stat[:, 26:27], in1=stat[:, 26:27], op=ALU.mult), r=[st2], w=[st2])
            P.op(V, lambda: nc.vector.scalar_tensor_tensor(out=stat[:, 28:29], in0=stat[:, 25:26], scalar=1.0 / 64, in1=stat[:, 27:28],
                                                          op0=ALU.mult, op1=ALU.subtract), r=[st2], w=[st2])
            P.op(A, lambda: nc.scalar.activation(out=stat[:, 28:29], in_=stat[:, 28:29], func=AF.Ln, bias=eps_c[:, 0:1]), r=[st2] + CONST, w=[st2])
            P.op(A, lambda: nc.scalar.activation(out=stat[:, 28:29], in_=stat[:, 28:29], func=AF.Exp, scale=-0.5), r=[st2], w=[st2])
            P.op(V, lambda: nc.vector.tensor_tensor(out=RB[:, 0:8, :], in0=pbank[0][:, 0:512].rearrange("p (h d) -> p h d", d=64),
                                                    in1=stat[:, 8:16].unsqueeze(2).broadcast_to([128, 8, 64]), op=ALU.mult),
                 r=[t_pb[0], st2], w=[t_RB])
            P.op(V, lambda: nc.vector.tensor_tensor(out=RB[:, 8:12, :], in0=pbank[1][:, 0:256].rearrange("p (h d) -> p h d", d=64),
                                                    in1=stat[:, 16:20].unsqueeze(2).broadcast_to([128, 4, 64]), op=ALU.mult),
                 r=[t_pb[1], st2], w=[t_RB])
            P.op(V, lambda: nc.vector.tensor_tensor(out=RB[:, 0:8, :], in0=RB[:, 0:8, :],
                                                     in1=qg_bc[:].unsqueeze(1).broadcast_to([128, 8, 64]), op=ALU.mult), r=[t_RB] + CONST, w=[t_RB])
            P.op(V, lambda: nc.vector.tensor_tensor(out=RB[:, 8:12, :], in0=RB[:, 8:12, :],
                                                     in1=kg_bc[:].unsqueeze(1).broadcast_to([128, 4, 64]), op=ALU.mult), r=[t_RB] + CONST, w=[t_RB])
            P.op(A, lambda: nc.scalar.activation(out=vf[:], in_=pbank[1][:, 256:512], func=AF.Copy), r=[t_pb[1]], w=[t_vf])
            P.op(A, lambda: nc.scalar.activation(out=RB[:, 12:20, :], in_=pbank[2][:, 0:512].rearrange("p (h d) -> p h d", d=64), func=AF.Copy),
                 r=[t_pb[2]], w=[t_RB])
            P.op(V, lambda: nc.vector.tensor_scalar(out=RB[:, 20, :], in0=pbank[3][:, 0:64], scalar1=stat[:, 26:27], scalar2=stat[:, 28:29],
                                                    op0=ALU.subtract, op1=ALU.mult), r=[t_pb[3], st2], w=[t_RB])
            P.op(V, lambda: nc.vector.tensor_tensor(out=RB[:, 20, :], in0=RB[:, 20, :], in1=lg_bc[:], op=ALU.mult), r=[t_RB] + CONST, w=[t_RB])
            P.op(V, lambda: nc.vector.tensor_tensor(out=RB[:, 20, :], in0=RB[:, 20, :], in1=lb_bc[:], op=ALU.add), r=[t_RB] + CONST, w=[t_RB])
            P.op(A, lambda: nc.scalar.activation(out=wi_abs[:], in_=pbank[3][:, 64:72], func=AF.Abs), r=[t_pb[3]], w=[t_wi])
            P.op(A, lambda: nc.scalar.activation(out=wi_sgn[:], in_=pbank[3][:, 64:72], func=AF.Sign), r=[t_pb[3]], w=[t_wi])
            cosb = rope[:, ti, 0:8].unsqueeze(1).broadcast_to([128, 21, 8])
            sinb = rope[:, ti, 8:16].unsqueeze(1).broadcast_to([128, 21, 8])
            rt = sq[:, 0:21 * 32].rearrange("p (h f) -> p h f", f=32)
            P.op(V, lambda rt=rt, cosb=cosb: nc.vector.tensor_tensor(out=rt[:, :, 0:8], in0=RB[:, :, 0:8], in1=cosb, op=ALU.mult), r=[t_RB] + CONST, w=[t_sq])
            P.op(V, lambda rt=rt, sinb=sinb: nc.vector.tensor_tensor(out=rt[:, :, 8:16], in0=RB[:, :, 8:16], in1=sinb, op=ALU.mult), r=[t_RB] + CONST, w=[t_sq])
            P.op(V, lambda rt=rt, cosb=cosb: nc.vector.tensor_tensor(out=rt[:, :, 16:24], in0=RB[:, :, 8:16], in1=cosb, op=ALU.mult), r=[t_RB] + CONST, w=[t_sq])
            P.op(V, lambda rt=rt, sinb=sinb: nc.vector.tensor_tensor(out=rt[:, :, 24:32], in0=RB[:, :, 0:8], in1=sinb, op=ALU.mult), r=[t_RB] + CONST, w=[t_sq])
            P.op(V, lambda rt=rt: nc.vector.tensor_tensor(out=RB[:, :, 0:8], in0=rt[:, :, 0:8], in1=rt[:, :, 8:16], op=ALU.subtract), r=[t_sq], w=[t_RB])
            P.op(V, lambda rt=rt: nc.vector.tensor_tensor(out=RB[:, :, 8:16], in0=rt[:, :, 16:24], in1=rt[:, :, 24:32], op=ALU.add), r=[t_sq], w=[t_RB])
            P.dma(SP, lambda r0=r0: nc.sync.dma_start(out=k_all[r0:r0 + 128, :], in_=RB[:, 8:12, :].rearrange("p h d -> p (h d)")), r=[t_RB])
            P.dma(SP, lambda r0=r0: nc.sync.dma_start(out=v_all[r0:r0 + 128, :], in_=vf[:]), r=[t_vf])
            P.dma(SP, lambda r0=r0: nc.sync.dma_start(out=ki_all[r0:r0 + 128, :], in_=RB[:, 20, :]), r=[t_RB])
            if ti == 0:
                ckpt(3)
            if STAGE >= 2 and ATT:
                attention_tile(ti)
            else:
                P.op(V, lambda: nc.vector.memset(oatt[:], 0.0), w=[t_oatt])
            P.dma(SP, lambda ti=ti: nc.sync.dma_start(out=o_scr[ti, :, 0:4, :], in_=oatt[:]), r=[t_oatt])
            if ti == 0:
                ckpt(31)
            if ti == 1:
                ckpt(32)
            if ti == 8:
                ckpt(33)
            if ti == 15:
                ckpt(34)

        ckpt(4)
        P.barrier()
        phA.close()

        phB = ExitStack()
        es.enter_context(phB)

        def sbB(name, shape, dt=F32):
            return phB.enter_context(nc.sbuf_tensor(name, list(shape), dt))

        W1 = sbB("W1", [128, 8, SW], BF16)
        W2 = sbB("W2", [128, 8, SW], BF16)
        t_W12 = P.tok("W12")
        sh0mu = sbB("sh0mu", [4, SW], BF16)
        sels_b = sbB("sels_b", [4, 128], BF16)
        t_rc = P.tok("rconst")
        lwb = sbB("lwb", [96, 3, 512], BF16)
        mask_f = sbB("mask_f", [128, 4, 128])
        mask_b = sbB("mask_b", [128, 8, 128], BF16)
        small_c = sbB("small_c", [128, 16])
        colm = sbB("colm", [64, 6, 128], BF16)
        bcbuf = [sbB("bcbuf%d" % i, [128, 512]) for i in range(2)]
        t_bcbuf = P.toks(2, "bcbuf")
        bc_src = {"w0": w0_in, "a0": a0_in, "kk": kk_in, "ka": ka_in, "rk": rk_in, "lng": lng_in, "lnb": lnb_in}
        bc_n = [0]

        def bc(nm):
            i = bc_n[0] % 2
            bc_n[0] += 1
            src = bc_src[nm]
            P.dma(SP, lambda: nc.sync.dma_start(out=bcbuf[i][:], in_=src.partition_broadcast(128)), w=[t_bcbuf[i]])
            return bcbuf[i], t_bcbuf[i]
        tiny_c = sbB("tiny_c", [128, 1])
        with ExitStack() as st:
            mu_bc = st.enter_context(nc.sbuf_tensor("mu_bc", [128, SW], F32))
            omm_bc = st.enter_context(nc.sbuf_tensor("omm_bc", [128, SW], F32))
            ssh_t = st.enter_context(nc.sbuf_tensor("ssh_t", [4, SW], F32))
            sels_f = st.enter_context(nc.sbuf_tensor("sels_f", [4, 128], F32))
            lw_f = st.enter_context(nc.sbuf_tensor("lw_f", [96, 3, 512], F32))
            colm_f = st.enter_context(nc.sbuf_tensor("colm_f", [64, 6, 128], F32))
            mask_t = st.enter_context(nc.sbuf_tensor("mask_t", [128, 8, 128], F32))
            t_mu = P.tok("mu")
            bcast_load(mu_bc, mu_in, t_mu)
            P.dma(SP, lambda: nc.sync.dma_start(out=ssh_t[:], in_=ssh_in[:, :]), w=[t_mu])
            P.dma(SP, lambda: nc.sync.dma_start(out=sels_f[:], in_=c_sels[:, :]), w=[t_mu])
            P.op(V, lambda: nc.vector.tensor_scalar(out=omm_bc[:], in0=mu_bc[:], scalar1=-1.0, scalar2=1.0,
                                                    op0=ALU.mult, op1=ALU.add), r=[t_mu], w=[t_mu])
            P.op(V, lambda: nc.vector.tensor_tensor(out=sh0mu[:], in0=ssh_t[:], in1=mu_bc[0:4, :], op=ALU.mult), r=[t_mu], w=[t_rc])
            P.op(V, lambda: nc.vector.tensor_copy(out=sels_b[:], in_=sels_f[:]), r=[t_mu], w=[t_rc])
            stg_B = [st.enter_context(nc.sbuf_tensor("wstgB%d" % i, [128, SW], F32)) for i in range(2)]
            t_stg_B = P.toks(2, "wstgB")
            for c in range(8):
                b = c % 2
                P.dma(SP, lambda c=c, b=b: nc.sync.dma_start(out=stg_B[b][:], in_=w_in[c * 128:(c + 1) * 128, NA:PROJ]), w=[t_stg_B[b]])
                P.op(V, lambda c=c, b=b: nc.vector.tensor_tensor(out=W2[:, c, :], in0=stg_B[b][:], in1=mu_bc[:], op=ALU.mult),
                     r=[t_stg_B[b], t_mu], w=[t_W12])
                P.op(PL, lambda c=c, b=b: nc.gpsimd.tensor_tensor(out=W1[:, c, :], in0=stg_B[b][:], in1=omm_bc[:], op=ALU.mult),
                     r=[t_stg_B[b], t_mu], w=[t_W12])
            P.dma(SP, lambda: nc.sync.dma_start(out=lw_f[0:32, 0, :], in_=w2_in[:, :]), w=[t_mu])
            P.dma(SP, lambda: nc.sync.dma_start(out=lw_f[0:32, 1, :], in_=a2_in[:, :]), w=[t_mu])
            P.dma(SP, lambda: nc.sync.dma_start(out=lw_f[0:96, 2, :], in_=g2_in[:, :]), w=[t_mu])
            P.op(V, lambda: nc.vector.tensor_copy(out=lwb[0:32, 0:2, :], in_=lw_f[0:32, 0:2, :]), r=[t_mu], w=[t_rc])
            P.op(V, lambda: nc.vector.tensor_copy(out=lwb[0:96, 2, :], in_=lw_f[0:96, 2, :]), r=[t_mu], w=[t_rc])
            P.dma(SP, lambda: nc.sync.dma_start(out=mask_t[:], in_=c_masks[:, :, :]), w=[t_mu])
            for di, si in enumerate([2, 6, 5, 7]):
                P.dma(SP, lambda di=di, si=si: nc.sync.dma_start(out=mask_f[:, di, :], in_=c_masks[:, si, :]), w=[t_rc])
            P.dma(SP, lambda: nc.sync.dma_start(out=small_c[:], in_=c_small[:, :]), w=[t_rc])
            P.dma(SP, lambda: nc.sync.dma_start(out=colm_f[:], in_=c_colm[:, :, :]), w=[t_mu])
            P.op(V, lambda: nc.vector.tensor_copy(out=mask_b[:], in_=mask_t[:]), r=[t_mu], w=[t_rc])
            P.op(V, lambda: nc.vector.tensor_copy(out=colm[:], in_=colm_f[:]), r=[t_mu], w=[t_rc])
            P.op(V, lambda: nc.vector.memset(tiny_c[:], 1e-24), w=[t_rc])
            P.barrier()
        RC = [t_rc]
        ckpt(5)

        def BT(name, shape, dt=F32):
            return sbB(name, shape, dt), P.tok(name)

        xt_B = [sbB("xtB0", [128, D])] * 2
        t_xt_B = [P.tok("xtB")] * 2
        xn_B, t_xn_B = BT("xnB", [128, D], BF16)
        xnT_B = [sbB("xnTB%d" % i, [128, 8, 128], BF16) for i in range(2)]
        t_xnT_B = P.toks(2, "xnTB")
        xsh, t_xsh = BT("xsh", [128, 8, 128], BF16)
        stat_B, t_statB = BT("statB", [128, 64])
        shrow, t_shrow = BT("shrow", [4, 512])
        orw, t_orw = BT("orw", [128, 4, 128], BF16)
        rS, t_rS = BT("rS", [128, 512])
        kS, t_kS = BT("kS", [128, 512])
        vS, t_vS = BT("vS", [128, 512])
        gS, t_gS = BT("gS", [128, 512])
        vb, t_vb = BT("vb", [128, 512], BF16)
        tl, t_tl = BT("tl", [128, 160])
        li, t_li = BT("li", [128, 160], BF16)
        loraT, t_loraT = BT("loraT", [96, 384], BF16)
        XW, t_XW = BT("XW", [128, 1024])
        EG, t_EG = BT("EG", [128, 512])
        ENG, t_ENG = BT("ENG", [128, 512])
        EGM, t_EGM = BT("EGM", [128, 512])
        EEND, t_EEND = BT("EEND", [128, 512])
        EGE, t_EGE = BT("EGE", [128, 512])
        KK, t_KK = BT("KK", [128, 512])
        lwt, t_lwt = KK, t_KK
        KP, t_KP = BT("KP", [128, 512])
        GendS, t_GendS = KP, t_KP
        BB, t_BB = BT("BB", [128, 512])
        T1, t_T1 = BT("T1", [128, 512])
        GX, t_GX = T1, t_T1
        st8, t_st8 = BT("st8", [128, 64])
        rt_, t_rt = BT("rt_", [128, 512], BF16)
        at_, t_at = BT("at_", [128, 512], BF16)
        kt_, t_kt = BT("kt_", [128, 512], BF16)
        bt_, t_bt = BT("bt_", [128, 512], BF16)
        kh_, t_kh = BT("kh_", [128, 512], BF16)
        bh_, t_bh = BT("bh_", [128, 512], BF16)
        Bm, t_Bm = BT("Bm", [128, 512], BF16)
        Km, t_Km = BT("Km", [128, 512], BF16)
        rT, t_rT = BT("rT", [64, 8, 128], BF16)
        aT, t_aT = BT("aT", [64, 8, 128], BF16)
        kT, t_kT = BT("kT", [64, 8, 128], BF16)
        bT, t_bT = BT("bT", [64, 8, 128], BF16)
        Nm = [BT("Nm%d" % i, [128, 8, 128], BF16) for i in range(2)]
        Mm = [BT("Mm%d" % i, [128, 8, 128], BF16) for i in range(2)]
        ArbT, t_ArbT = BT("ArbT", [128, 8, 128], BF16)
        AakT, t_AakT = Mm[1]
        ArkT, t_ArkT = BT("ArkT", [128, 8, 128], BF16)
        Xf, t_Xf = BT("Xf", [128, 8, 128])
        Xb, t_Xb = BT("Xb", [128, 8, 128], BF16)
        junk_B, t_junk_B = Xb[:].rearrange("p h t -> p (h t)"), t_Xb
        GT, t_GT = BT("GT", [64, 512], BF16)
        RT2, t_RT2 = BT("RT2", [64, 8, 128], BF16)
        RTm, t_RTm = BT("RTm", [64, 8, 128], BF16)
        GAM, t_GAM = BT("GAM", [64, 8, 4])
        Hf, t_Hf = BT("Hf", [64, 8, 64])
        Hb, t_Hb = BT("Hb", [64, 8, 64], BF16)
        Tg, t_Tg = BT("Tg", [64, 8, 64])
        S0, t_S0 = BT("S0", [64, 8, 64])
        So, t_So = Tg, t_Tg
        YS, t_YS = EG, t_EG
        YN, t_YN = ENG, t_ENG
        ob_, t_ob = BT("ob_", [128, 512], BF16)
        P.op(V, lambda: nc.vector.memset(Hf[:], 0.0), w=[t_Hf])
        P.op(V, lambda: nc.vector.memset(Hb[:], 0.0), w=[t_Hb])
        NEG_E = -0.6065306597126334
        ckpt(60)

        def h3(ap, d=64):
            return ap.rearrange("p (h d) -> p h d", d=d)

        def rwkv_tile(ti, xT, t_xT, part):
            sample = ti == NPT
            C = 32 if sample else 64
            NCH = 128 // C
            mi = 3 if sample else 0
            MSL, MSU, MU, BLK = mi, mi + 1, mi + 2, (7 if sample else 6)
            selc = small_c[:, 2:6] if sample else small_c[:, 0:2]
            chi = small_c[:, 8:12] if sample else small_c[:, 6:8]
            cm0 = 2 if sample else 0
            nlev = 5 if sample else 6
            for (c0, n, bk) in ([(1536, 160, 3), (0, 512, 0), (512, 512, 1), (1024, 512, 2)] if part == 1 else []):
                for c in range(8):
                    P.op(PE, lambda c=c, c0=c0, n=n, bk=bk: nc.tensor.matmul(pbank[bk][:, 0:n], lhsT=xT[:, c, :], rhs=W1[:, c, c0:c0 + n],
                                                                           start=(c == 0), stop=False), r=[t_xT, t_W12], w=[t_pb[bk]])
                for c in range(8):
                    P.op(PE, lambda c=c, c0=c0, n=n, bk=bk: nc.tensor.matmul(pbank[bk][:, 0:n], lhsT=xsh[:, c, :], rhs=W2[:, c, c0:c0 + n],
                                                                           start=False, stop=(c == 7 and not sample)), r=[t_xsh, t_W12], w=[t_pb[bk]])
                if sample:
                    P.op(PE, lambda c0=c0, n=n, bk=bk: nc.tensor.matmul(pbank[bk][:, 0:n], lhsT=sels_b[0:4, :], rhs=sh0mu[0:4, c0:c0 + n],
                                                                      start=False, stop=True), r=RC, w=[t_pb[bk]])
            if part == 1:
                return
            P.op(A, lambda: nc.scalar.activation(out=rS[:], in_=pbank[0][:, :], func=AF.Copy), r=[t_pb[0]], w=[t_rS])
            P.op(A, lambda: nc.scalar.activation(out=kS[:], in_=pbank[1][:, :], func=AF.Copy), r=[t_pb[1]], w=[t_kS])
            P.op(A, lambda: nc.scalar.activation(out=vS[:], in_=pbank[2][:, :], func=AF.Copy), r=[t_pb[2]], w=[t_vS])
            P.op(A, lambda: nc.scalar.activation(out=vb[:], in_=vS[:], func=AF.Copy), r=[t_vS], w=[t_vb])
            if ti == 0:
                ckpt(62)
            P.op(A, lambda: nc.scalar.activation(out=tl[:, 0:32], in_=pbank[3][:, 0:32], func=AF.Exp, scale=2.0), r=[t_pb[3]], w=[t_tl])
            P.op(A, lambda: nc.scalar.activation(out=tl[:, 64:160], in_=pbank[3][:, 64:160], func=AF.Exp, scale=-1.0), r=[t_pb[3]], w=[t_tl])
            P.op(A, lambda: nc.scalar.activation(out=li[:, 32:64], in_=pbank[3][:, 32:64], func=AF.Copy), r=[t_pb[3]], w=[t_li])
            P.op(V, lambda: nc.vector.tensor_scalar(out=tl[:, 0:32], in0=tl[:, 0:32], scalar1=1.0, scalar2=None, op0=ALU.add), r=[t_tl], w=[t_tl])
            P.op(V, lambda: nc.vector.tensor_scalar(out=tl[:, 64:160], in0=tl[:, 64:160], scalar1=1.0, scalar2=None, op0=ALU.add), r=[t_tl], w=[t_tl])
            P.op(V, lambda: nc.vector.reciprocal(out=tl[:, 0:32], in_=tl[:, 0:32]), r=[t_tl], w=[t_tl])
            P.op(V, lambda: nc.vector.reciprocal(out=tl[:, 64:160], in_=tl[:, 64:160]), r=[t_tl], w=[t_tl])
            P.op(V, lambda: nc.vector.tensor_scalar(out=li[:, 0:32], in0=tl[:, 0:32], scalar1=-2.0, scalar2=1.0, op0=ALU.mult, op1=ALU.add), r=[t_tl], w=[t_li])
            P.op(V, lambda: nc.vector.tensor_copy(out=li[:, 64:160], in_=tl[:, 64:160]), r=[t_tl], w=[t_li])
            p7 = pb_bf(7)
            P.op(PE, lambda: nc.tensor.transpose(p7[0:32, 0:128], li[:, 0:32], ident_b[:]), r=[t_li] + CONST, w=[t_pb[7]])
            P.op(PE, lambda: nc.tensor.transpose(p7[0:32, 128:256], li[:, 32:64], ident_b[:]), r=[t_li] + CONST, w=[t_pb[7]])
            P.op(PE, lambda: nc.tensor.transpose(p7[0:96, 256:384], li[:, 64:160], ident_b[:]), r=[t_li] + CONST, w=[t_pb[7]])
            P.op(A, lambda: nc.scalar.activation(out=loraT[0:32, 0:256], in_=p7[0:32, 0:256], func=AF.Copy), r=[t_pb[7]], w=[t_loraT])
            P.op(A, lambda: nc.scalar.activation(out=loraT[0:96, 256:384], in_=p7[0:96, 256:384], func=AF.Copy), r=[t_pb[7]], w=[t_loraT])
            P.op(PE, lambda: nc.tensor.matmul(pbank[4][:, :], lhsT=loraT[0:32, 0:128], rhs=lwb[0:32, 0, :], start=True, stop=True), r=[t_loraT] + RC, w=[t_pb[4]])
            P.op(PE, lambda: nc.tensor.matmul(pbank[5][:, :], lhsT=loraT[0:32, 128:256], rhs=lwb[0:32, 1, :], start=True, stop=True), r=[t_loraT] + RC, w=[t_pb[5]])
            P.op(PE, lambda: nc.tensor.matmul(pbank[6][:, :], lhsT=loraT[0:96, 256:384], rhs=lwb[0:96, 2, :], start=True, stop=True), r=[t_loraT] + RC, w=[t_pb[6]])
            b1, tb1 = bc("w0")
            P.op(V, lambda: nc.vector.tensor_tensor(out=XW[:, 0:512], in0=pbank[4][:, :], in1=b1[:], op=ALU.add), r=[t_pb[4], tb1], w=[t_XW])
            b2, tb2 = bc("a0")
            P.op(V, lambda: nc.vector.tensor_tensor(out=XW[:, 512:1024], in0=pbank[5][:, :], in1=b2[:], op=ALU.add), r=[t_pb[5], tb2], w=[t_XW])
            P.op(A, lambda: nc.scalar.activation(out=XW[:], in_=XW[:], func=AF.Exp, scale=-1.0), r=[t_XW], w=[t_XW])
            P.op(V, lambda: nc.vector.tensor_scalar(out=XW[:], in0=XW[:], scalar1=1.0, scalar2=None, op0=ALU.add), r=[t_XW], w=[t_XW])
            P.op(V, lambda: nc.vector.reciprocal(out=XW[:], in_=XW[:]), r=[t_XW], w=[t_XW])
            P.op(A, lambda: nc.scalar.activation(out=gS[:], in_=pbank[6][:, :], func=AF.Copy), r=[t_pb[6]], w=[t_gS])
            AA = XW[:, 512:1024]
            if ti == 0:
                ckpt(64)
            P.op(A, lambda: nc.scalar.activation(out=lwt[:], in_=XW[:, 0:512], func=AF.Copy, scale=NEG_E), r=[t_XW], w=[t_lwt])
            P.op(PE, lambda: nc.tensor.matmul(pbank[3][:, :], lhsT=mask_f[:, (2 if sample else 0), :], rhs=lwt[:], start=True, stop=True), r=[t_lwt] + RC, w=[t_pb[3]])
            P.op(PE, lambda: nc.tensor.matmul(pbank[4][:, :], lhsT=mask_f[:, (3 if sample else 1), :], rhs=lwt[:], start=True, stop=True), r=[t_lwt] + RC, w=[t_pb[4]])
            GinS = XW[:, 0:512]
            P.op(A, lambda: nc.scalar.activation(out=GinS, in_=pbank[3][:, :], func=AF.Copy), r=[t_pb[3], t_lwt], w=[t_XW])
            P.op(A, lambda: nc.scalar.activation(out=GendS[:], in_=pbank[4][:, :], func=AF.Copy), r=[t_pb[4]], w=[t_GendS])
            P.op(A, lambda: nc.scalar.activation(out=EG[:], in_=GinS, func=AF.Exp), r=[t_XW], w=[t_EG])
            P.op(A, lambda: nc.scalar.activation(out=ENG[:], in_=GinS, func=AF.Exp, scale=-1.0), r=[t_XW], w=[t_ENG])
            P.op(V, lambda: nc.vector.tensor_tensor(out=GX[:], in0=GinS, in1=lwt[:], op=ALU.subtract), r=[t_XW, t_lwt], w=[t_GX])
            P.op(A, lambda: nc.scalar.activation(out=EGM[:], in_=GX[:], func=AF.Exp), r=[t_GX], w=[t_EGM])
            P.op(V, lambda: nc.vector.tensor_tensor(out=GX[:], in0=GendS[:], in1=GinS, op=ALU.subtract), r=[t_XW, t_GendS, t_EGM], w=[t_GX])
            P.op(A, lambda: nc.scalar.activation(out=EEND[:], in_=GX[:], func=AF.Exp), r=[t_GX], w=[t_EEND])
            P.op(A, lambda: nc.scalar.activation(out=EGE[:], in_=GendS[:], func=AF.Exp), r=[t_GendS], w=[t_EGE])
            if ti == 0:
                ckpt(65)
            b3, tb3 = bc("kk")
            P.op(V, lambda: nc.vector.tensor_tensor(out=KK[:], in0=kS[:], in1=b3[:], op=ALU.mult), r=[t_kS, tb3], w=[t_KK])
            P.op(A, lambda: nc.scalar.activation(out=T1[:], in_=KK[:], func=AF.Square), r=[t_KK], w=[t_T1])
            P.op(V, lambda: nc.vector.tensor_reduce(out=st8[:, 0:8], in_=h3(T1[:]), axis=AX.X, op=ALU.add), r=[t_T1], w=[t_st8])
            P.op(A, lambda: nc.scalar.activation(out=st8[:, 0:8], in_=st8[:, 0:8], func=AF.Ln, bias=tiny_c[:, 0:1]), r=[t_st8] + RC, w=[t_st8])
            P.op(A, lambda: nc.scalar.activation(out=st8[:, 0:8], in_=st8[:, 0:8], func=AF.Exp, scale=-0.5), r=[t_st8], w=[t_st8])
            P.op(V, lambda: nc.vector.tensor_tensor(out=h3(KK[:]), in0=h3(KK[:]), in1=st8[:, 0:8].unsqueeze(2).broadcast_to([128, 8, 64]), op=ALU.mult),
                 r=[t_KK, t_st8], w=[t_KK])
            b4, tb4 = bc("ka")
            P.op(V, lambda: nc.vector.scalar_tensor_tensor(out=KP[:], in0=AA, scalar=-1.0, in1=b4[:], op0=ALU.add, op1=ALU.mult), r=[t_XW, tb4], w=[t_KP])
            P.op(V, lambda: nc.vector.scalar_tensor_tensor(out=KP[:], in0=KP[:], scalar=1.0, in1=kS[:], op0=ALU.add, op1=ALU.mult), r=[t_KP, t_kS], w=[t_KP])
            P.op(V, lambda: nc.vector.tensor_tensor(out=BB[:], in0=KK[:], in1=AA, op=ALU.mult), r=[t_KK, t_XW], w=[t_BB])
            P.op(PL, lambda: nc.gpsimd.tensor_tensor(out=T1[:], in0=rS[:], in1=KP[:], op=ALU.mult), r=[t_rS, t_KP, t_st8], w=[t_T1])
            b5, tb5 = bc("rk")
            P.op(PL, lambda: nc.gpsimd.tensor_tensor(out=T1[:], in0=T1[:], in1=b5[:], op=ALU.mult), r=[t_T1, tb5], w=[t_T1])
            P.op(V, lambda: nc.vector.tensor_reduce(out=st8[:, 8:16], in_=h3(T1[:]), axis=AX.X, op=ALU.add), r=[t_T1], w=[t_st8])
            P.op(V, lambda: nc.vector.tensor_tensor(out=rt_[:], in0=rS[:], in1=EG[:], op=ALU.mult), r=[t_rS, t_EG], w=[t_rt])
            P.op(V, lambda: nc.vector.tensor_tensor(out=at_[:], in0=KK[:], in1=EGM[:], op=ALU.mult), r=[t_KK, t_EGM], w=[t_at])
            P.op(V, lambda: nc.vector.tensor_tensor(out=kt_[:], in0=KP[:], in1=ENG[:], op=ALU.mult), r=[t_KP, t_ENG], w=[t_kt])
            P.op(V, lambda: nc.vector.scalar_tensor_tensor(out=bt_[:], in0=BB[:], scalar=-1.0, in1=ENG[:], op0=ALU.mult, op1=ALU.mult), r=[t_BB, t_ENG], w=[t_bt])
            P.op(V, lambda: nc.vector.tensor_tensor(out=kh_[:], in0=KP[:], in1=EEND[:], op=ALU.mult), r=[t_KP, t_EEND], w=[t_kh])
            P.op(V, lambda: nc.vector.scalar_tensor_tensor(out=bh_[:], in0=BB[:], scalar=-1.0, in1=EEND[:], op0=ALU.mult, op1=ALU.mult), r=[t_BB, t_EEND], w=[t_bh])
            if ti == 0:
                ckpt(67)
            for (src, t_src, dst, t_dst, bk, eng) in [(rt_, t_rt, rT, t_rT, 0, A), (at_, t_at, aT, t_aT, 1, V), (kt_, t_kt, kT, t_kT, 2, A), (bt_, t_bt, bT, t_bT, 5, V)]:
                pv = pb_bf(bk)
                for h in range(8):
                    P.op(PE, lambda h=h, pv=pv, src=src: nc.tensor.transpose(pv[0:64, h * 128:(h + 1) * 128], src[:, h * 64:(h + 1) * 64], ident_b[:]),
                         r=[t_src] + CONST, w=[t_pb[bk]])
                if eng == A:
                    P.op(A, lambda pv=pv, dst=dst: nc.scalar.activation(out=dst[:], in_=pv[0:64, :].rearrange("p (h t) -> p h t", h=8), func=AF.Copy), r=[t_pb[bk]], w=[t_dst])
                else:
                    P.op(V, lambda pv=pv, dst=dst: nc.vector.tensor_copy(out=dst[:], in_=pv[0:64, :].rearrange("p (h t) -> p h t", h=8)), r=[t_pb[bk]], w=[t_dst])
            if ti == 0:
                ckpt(68)
            N0, t_N0 = Nm[0]
            M0, t_M0 = Mm[0]
            specs = [(aT, t_aT, bT, t_bT, N0, t_N0, MSL, 0), (bT, t_bT, aT, t_aT, M0, t_M0, MSU, 1), (bT, t_bT, rT, t_rT, ArbT, t_ArbT, MU, 2),
                     (kT, t_kT, aT, t_aT, AakT, t_AakT, MSU, 5), (kT, t_kT, rT, t_rT, ArkT, t_ArkT, MU, 6)]
            for hg in range(2):
                for (L_, tL, R_, tR, dst, t_dst, mk, bk) in specs:
                    for j in range(4):
                        h = hg * 4 + j
                        P.op(PE, lambda h=h, j=j, L_=L_, R_=R_, bk=bk: nc.tensor.matmul(pbank[bk][:, j * 128:(j + 1) * 128], lhsT=L_[:, h, :], rhs=R_[:, h, :],
                                                                                     start=True, stop=True), r=[tL, tR], w=[t_pb[bk]])
                    P.op(V, lambda hg=hg, dst=dst, mk=mk, bk=bk: nc.vector.tensor_tensor(
                        out=dst[:, hg * 4:hg * 4 + 4, :], in0=pbank[bk][:, :].rearrange("p (h t) -> p h t", h=4),
                        in1=mask_b[:, mk, :].unsqueeze(1).broadcast_to([128, 4, 128]), op=ALU.mult), r=[t_pb[bk]] + RC, w=[t_dst])
            if ti == 0:
                ckpt(69)
            for h in range(8):
                P.op(PE, lambda h=h: nc.tensor.matmul(pbank[7][:, h * 64:(h + 1) * 64], lhsT=AakT[:, h, :], rhs=vb[:, h * 64:(h + 1) * 64], start=True, stop=True),
                     r=[t_AakT, t_vb], w=[t_pb[7]])
            P.op(A, lambda: nc.scalar.activation(out=Xf[:, :, 64:128], in_=h3(pbank[7][:, :]), func=AF.Copy), r=[t_pb[7]], w=[t_Xf])
            P.op(A, lambda: nc.scalar.activation(out=Xf[:, :, 0:64], in_=h3(at_[:]), func=AF.Copy), r=[t_at], w=[t_Xf])
            P.op(A, lambda: nc.scalar.activation(out=Xb[:], in_=Xf[:], func=AF.Copy), r=[t_Xf], w=[t_Xb])
            if ti == 0:
                ckpt(70)
            cur = 0
            for lev in range(nlev):
                Mc, t_Mc = Mm[cur]
                Nc, t_Nc = Nm[cur]
                for h in range(8):
                    bk = 3 + h // 4
                    P.op(PE, lambda h=h, bk=bk, Mc=Mc: nc.tensor.matmul(pbank[bk][:, (h % 4) * 128:(h % 4 + 1) * 128], lhsT=Mc[:, h, :], rhs=Xb[:, h, :], start=True, stop=True),
                         r=[t_Mc, t_Xb], w=[t_pb[bk]])
                for hg in range(2):
                    P.op(V, lambda hg=hg: nc.vector.tensor_tensor(out=Xf[:, hg * 4:hg * 4 + 4, :], in0=pbank[3 + hg][:, :].rearrange("p (h t) -> p h t", h=4),
                                                                  in1=Xf[:, hg * 4:hg * 4 + 4, :], op=ALU.add), r=[t_pb[3 + hg], t_Xf], w=[t_Xf])
                P.op(A, lambda: nc.scalar.activation(out=Xb[:], in_=Xf[:], func=AF.Copy), r=[t_Xf], w=[t_Xb])
                if lev < nlev - 1:
                    Mn, t_Mn = Mm[1 - cur]
                    Nn, t_Nn = Nm[1 - cur]
                    for hg in range(2):
                        for j in range(4):
                            h = hg * 4 + j
                            P.op(PE, lambda h=h, j=j, hg=hg, Mc=Mc, Nc=Nc: nc.tensor.matmul(pbank[0 + hg][:, j * 128:(j + 1) * 128], lhsT=Nc[:, h, :], rhs=Mc[:, h, :], start=True, stop=True),
                                 r=[t_Mc, t_Nc], w=[t_pb[0 + hg]])
                        P.op(A, lambda hg=hg, Mn=Mn: nc.scalar.activation(out=Mn[:, hg * 4:hg * 4 + 4, :], in_=pbank[0 + hg][:, :].rearrange("p (h t) -> p h t", h=4), func=AF.Copy),
                             r=[t_pb[0 + hg]], w=[t_Mn])
                        if lev < nlev - 2:
                            bkn = 2 if hg == 0 else 5
                            for j in range(4):
                                h = hg * 4 + j
                                P.op(PE, lambda h=h, j=j, bkn=bkn, Mc=Mc, Nc=Nc: nc.tensor.matmul(pbank[bkn][:, j * 128:(j + 1) * 128], lhsT=Mc[:, h, :], rhs=Nc[:, h, :], start=True, stop=True),
                                     r=[t_Mc, t_Nc], w=[t_pb[bkn]])
                            P.op(A, lambda hg=hg, bkn=bkn, Nn=Nn: nc.scalar.activation(out=Nn[:, hg * 4:hg * 4 + 4, :], in_=pbank[bkn][:, :].rearrange("p (h t) -> p h t", h=4), func=AF.Copy),
                                 r=[t_pb[bkn]], w=[t_Nn])
                    cur = 1 - cur
            if ti == 0:
                ckpt(71)
            for h in range(8):
                bk = h // 4
                P.op(PE, lambda h=h, bk=bk: nc.tensor.matmul(pbank[bk][0:64, (h % 4) * 128:(h % 4 + 1) * 128], lhsT=Xb[:, h, 0:64], rhs=ArbT[:, h, :], start=True, stop=True),
                     r=[t_Xb, t_ArbT], w=[t_pb[bk]])
            for hg in range(2):
                P.op(V, lambda hg=hg: nc.vector.tensor_tensor(out=RT2[:, hg * 4:hg * 4 + 4, :], in0=pbank[hg][0:64, :].rearrange("p (h t) -> p h t", h=4),
                                                              in1=rT[:, hg * 4:hg * 4 + 4, :], op=ALU.add), r=[t_pb[hg], t_rT], w=[t_RT2])
            for h in range(8):
                P.op(PE, lambda h=h: nc.tensor.matmul(pbank[7][0:64, h * NCH:(h + 1) * NCH], lhsT=EGE[:, h * 64:(h + 1) * 64], rhs=selc, start=True, stop=True),
                     r=[t_EGE] + RC, w=[t_pb[7]])
            P.op(A, lambda: nc.scalar.activation(out=GAM[:, :, 0:NCH], in_=pbank[7][0:64, 0:8 * NCH].rearrange("p (h c) -> p h c", c=NCH), func=AF.Copy), r=[t_pb[7]], w=[t_GAM])
            if ti == 0:
                ckpt(72)
            P.op(PE, lambda: nc.tensor.matmul(pbank[2][:, :], lhsT=zero_b[0:1, 0:128], rhs=zero_b[0:1, 0:512], start=True, stop=False), r=CONST, w=[t_pb[2]])
            for h in range(8):
                P.op(PE, lambda h=h: nc.tensor.matmul(pbank[2][:, h * 64:(h + 1) * 64], lhsT=ArbT[:, h, :], rhs=Xb[:, h, 64:128], start=False, stop=False),
                     r=[t_ArbT, t_Xb], w=[t_pb[2]])
                P.op(PE, lambda h=h: nc.tensor.matmul(pbank[2][:, h * 64:(h + 1) * 64], lhsT=ArkT[:, h, :], rhs=vb[:, h * 64:(h + 1) * 64], start=False, stop=False),
                     r=[t_ArkT, t_vb], w=[t_pb[2]])
            if ti == 0:
                ckpt(73)
            for c in range(NCH):
                if sample:
                    P.dma(SP, lambda c=c: nc.sync.dma_start(out=S0[:], in_=swkv_in[c].rearrange("h v k -> v h k")), w=[t_S0])
                    for h in range(8):
                        P.op(PE, lambda h=h: nc.tensor.matmul(pbank[5][0:64, h * 64:(h + 1) * 64], lhsT=S0[:, h, :], rhs=ident_f[0:64, 0:64], start=True, stop=True),
                             r=[t_S0] + CONST, w=[t_pb[5]])
                    P.op(V, lambda: nc.vector.tensor_copy(out=Hf[:], in_=h3(pbank[5][0:64, :])), r=[t_pb[5]], w=[t_Hf])
                    P.op(A, lambda: nc.scalar.activation(out=Hb[:], in_=Hf[:], func=AF.Copy), r=[t_Hf], w=[t_Hb])
                P.op(V, lambda c=c: nc.vector.tensor_scalar(out=Bm[:], in0=bh_[:], scalar1=chi[:, c:c + 1], scalar2=None, op0=ALU.mult), r=[t_bh] + RC, w=[t_Bm])
                P.op(V, lambda c=c: nc.vector.tensor_scalar(out=Km[:], in0=kh_[:], scalar1=chi[:, c:c + 1], scalar2=None, op0=ALU.mult), r=[t_kh] + RC, w=[t_Km])
                P.op(V, lambda c=c: nc.vector.tensor_tensor(out=RTm[:], in0=RT2[:], in1=colm[:, cm0 + c, :].unsqueeze(1).broadcast_to([64, 8, 128]), op=ALU.mult),
                     r=[t_RT2] + RC, w=[t_RTm])
                for h in range(8):
                    P.op(PE, lambda h=h: nc.tensor.matmul(pbank[6][0:64, h * 64:(h + 1) * 64], lhsT=Xb[:, h, 0:64], rhs=Bm[:, h * 64:(h + 1) * 64], start=True, stop=True),
                         r=[t_Xb, t_Bm], w=[t_pb[6]])
                P.op(A, lambda: nc.scalar.activation(out=GT[:], in_=pbank[6][0:64, :], func=AF.Copy), r=[t_pb[6]], w=[t_GT])
                for h in range(8):
                    P.op(PE, lambda c=c, h=h: nc.tensor.matmul(pbank[2][:, h * 64:(h + 1) * 64], lhsT=RTm[:, h, :], rhs=Hb[:, h, :], start=False, stop=(c == NCH - 1 and h == 7)),
                         r=[t_RTm, t_Hb], w=[t_pb[2]])
                P.op(PE, lambda: nc.tensor.matmul(pbank[5][0:64, :], lhsT=zero_b[0:1, 0:64], rhs=zero_b[0:1, 0:512], start=True, stop=False), r=CONST, w=[t_pb[5]])
                for h in range(8):
                    P.op(PE, lambda c=c, h=h: nc.tensor.matmul(pbank[5][0:64, h * 64:(h + 1) * 64], lhsT=Bm[:, h * 64:(h + 1) * 64], rhs=Xb[:, h, 64:128], start=False, stop=False),
                         r=[t_Bm, t_Xb], w=[t_pb[5]])
                    P.op(PE, lambda c=c, h=h: nc.tensor.matmul(pbank[5][0:64, h * 64:(h + 1) * 64], lhsT=Km[:, h * 64:(h + 1) * 64], rhs=vb[:, h * 64:(h + 1) * 64], start=False, stop=False),
                         r=[t_Km, t_vb], w=[t_pb[5]])
                    P.op(PE, lambda c=c, h=h: nc.tensor.matmul(pbank[5][0:64, h * 64:(h + 1) * 64], lhsT=GT[:, h * 64:(h + 1) * 64], rhs=Hb[:, h, :], start=False, stop=(h == 7)),
                         r=[t_GT, t_Hb], w=[t_pb[5]])
                P.op(V, lambda c=c: nc.vector.tensor_tensor(out=Tg[:], in0=Hf[:], in1=GAM[:, :, c:c + 1].broadcast_to([64, 8, 64]), op=ALU.mult), r=[t_Hf, t_GAM], w=[t_Tg])
                P.op(V, lambda: nc.vector.tensor_tensor(out=Hf[:], in0=Tg[:], in1=h3(pbank[5][0:64, :]), op=ALU.add), r=[t_Tg, t_pb[5]], w=[t_Hf])
                P.op(A, lambda: nc.scalar.activation(out=Hb[:], in_=Hf[:], func=AF.Copy), r=[t_Hf], w=[t_Hb])
                if sample or (ti == NPT - 1 and c == NCH - 1):
                    for h in range(8):
                        P.op(PE, lambda h=h: nc.tensor.matmul(pbank[7][0:64, h * 64:(h + 1) * 64], lhsT=Hf[:, h, :], rhs=ident_f[0:64, 0:64], start=True, stop=True),
                             r=[t_Hf] + CONST, w=[t_pb[7]])
                    P.op(V, lambda: nc.vector.tensor_copy(out=So[:], in_=h3(pbank[7][0:64, :])), r=[t_pb[7]], w=[t_So])
                    dstw = wkv_s[c] if sample else wkv_p
                    P.dma(SP, lambda dstw=dstw: nc.sync.dma_start(out=dstw.rearrange("h v k -> v h k"), in_=So[:]), r=[t_So])
            if ti == 0:
                ckpt(74)
            P.op(A, lambda: nc.scalar.activation(out=YS[:], in_=pbank[2][:, :], func=AF.Copy), r=[t_pb[2]], w=[t_YS])
            P.op(V, lambda: nc.vector.tensor_reduce(out=st8[:, 16:24], in_=h3(YS[:]), axis=AX.X, op=ALU.add), r=[t_YS], w=[t_st8])
            P.op(A, lambda: nc.scalar.activation(out=T1[:], in_=YS[:], func=AF.Square), r=[t_YS, t_st8], w=[t_T1])
            P.op(V, lambda: nc.vector.tensor_reduce(out=st8[:, 24:32], in_=h3(T1[:]), axis=AX.X, op=ALU.add), r=[t_T1], w=[t_st8])
            P.op(V, lambda: nc.vector.tensor_scalar(out=st8[:, 16:24], in0=st8[:, 16:24], scalar1=1.0 / 64, scalar2=None, op0=ALU.mult), r=[t_st8], w=[t_st8])
            P.op(V, lambda: nc.vector.tensor_tensor(out=st8[:, 32:40], in0=st8[:, 16:24], in1=st8[:, 16:24], op=ALU.mult), r=[t_st8], w=[t_st8])
            P.op(V, lambda: nc.vector.scalar_tensor_tensor(out=st8[:, 24:32], in0=st8[:, 24:32], scalar=1.0 / 64, in1=st8[:, 32:40], op0=ALU.mult, op1=ALU.subtract),
                 r=[t_st8], w=[t_st8])
            P.op(A, lambda: nc.scalar.activation(out=st8[:, 24:32], in_=st8[:, 24:32], func=AF.Ln, bias=eps_c[:, 1:2]), r=[t_st8] + CONST, w=[t_st8])
            P.op(A, lambda: nc.scalar.activation(out=st8[:, 24:32], in_=st8[:, 24:32], func=AF.Exp, scale=-0.5), r=[t_st8], w=[t_st8])
            P.op(V, lambda: nc.vector.tensor_tensor(out=h3(YN[:]), in0=h3(YS[:]), in1=st8[:, 16:24].unsqueeze(2).broadcast_to([128, 8, 64]), op=ALU.subtract),
                 r=[t_YS, t_st8], w=[t_YN])
            P.op(V, lambda: nc.vector.tensor_tensor(out=h3(YN[:]), in0=h3(YN[:]), in1=st8[:, 24:32].unsqueeze(2).broadcast_to([128, 8, 64]), op=ALU.mult),
                 r=[t_YN, t_st8], w=[t_YN])
            b6, tb6 = bc("lng")
            P.op(V, lambda: nc.vector.tensor_tensor(out=YN[:], in0=YN[:], in1=b6[:], op=ALU.mult), r=[t_YN, tb6], w=[t_YN])
            b7, tb7 = bc("lnb")
            P.op(V, lambda: nc.vector.tensor_tensor(out=YN[:], in0=YN[:], in1=b7[:], op=ALU.add), r=[t_YN, tb7], w=[t_YN])
            P.op(V, lambda: nc.vector.tensor_tensor(out=h3(T1[:]), in0=h3(vS[:]), in1=st8[:, 8:16].unsqueeze(2).broadcast_to([128, 8, 64]), op=ALU.mult),
                 r=[t_vS, t_st8], w=[t_T1])
            P.op(V, lambda: nc.vector.tensor_tensor(out=YN[:], in0=YN[:], in1=T1[:], op=ALU.add), r=[t_YN, t_T1], w=[t_YN])
            P.op(V, lambda: nc.vector.tensor_tensor(out=ob_[:], in0=YN[:], in1=gS[:], op=ALU.mult), r=[t_YN, t_gS], w=[t_ob])
            p7b = pb_bf(7)
            for c in range(4):
                P.op(PE, lambda c=c: nc.tensor.transpose(p7b[:, c * 128:(c + 1) * 128], ob_[:, c * 128:(c + 1) * 128], ident_b[:]), r=[t_ob] + CONST, w=[t_pb[7]])
            P.op(A, lambda: nc.scalar.activation(out=orw[:], in_=p7b[:, 0:512].rearrange("p (c t) -> p c t", c=4), func=AF.Copy), r=[t_pb[7]], w=[t_orw])


        t_shdummy = P.tok("shd")

        def normB(tj):
            bj = tj % 2
            xTj, t_xTj = xnT_B[bj], t_xnT_B[bj]
            norm_tile(tj, xt_B[bj], t_xt_B[bj], junk_B, t_junk_B, xn_B, t_xn_B, stat_B, xTj, t_xTj, t_statB)
            if tj == NPT:
                P.op(V, lambda: nc.vector.tensor_copy(
                    out=xsh[:].rearrange("p c (s t) -> p c s t", s=4)[:, :, :, 1:32],
                    in_=xTj[:].rearrange("p c (s t) -> p c s t", s=4)[:, :, :, 0:31]), r=[t_xTj], w=[t_xsh])
                P.op(V, lambda: nc.vector.memset(xsh[:].rearrange("p c (s t) -> p c s t", s=4)[:, :, :, 0:1], 0.0), w=[t_xsh])
            else:
                P.op(V, lambda: nc.vector.tensor_copy(out=xsh[:, :, 1:128], in_=xTj[:, :, 0:127]), r=[t_xTj], w=[t_xsh])
                if tj == 0:
                    P.op(V, lambda: nc.vector.memset(xsh[:, :, 0:1], 0.0), w=[t_xsh])
                else:
                    xTp, t_xTp = xnT_B[1 - bj], t_xnT_B[1 - bj]
                    P.op(V, lambda: nc.vector.tensor_copy(out=xsh[:, :, 0:1], in_=xTp[:, :, 127:128]), r=[t_xTp], w=[t_xsh])

        normB(0)
        for ti in range(NT):
            sample = ti == NPT
            b = ti % 2
            xT, t_xT = xnT_B[b], t_xnT_B[b]
            if STAGE >= 3:
                rwkv_tile(ti, xT, t_xT, 1)
            if ti + 1 < NT:
                normB(ti + 1)
            if STAGE >= 3:
                rwkv_tile(ti, xT, t_xT, 2)
            else:
                P.op(V, lambda: nc.vector.memset(orw[:], 0.0), w=[t_orw])
            P.dma(SP, lambda ti=ti: nc.sync.dma_start(out=o_scr[ti, :, 4:8, :], in_=orw[:]), r=[t_orw])
            if ti == 1:
                ckpt(51)
            if ti == NPT - 1 or sample:
                nrow = 4 if sample else 1
                if sample:
                    lastc = sbB("lastc", [128, 8, 4], BF16)
                    t_lastc = P.tok("lastc")
                    for s_ in range(4):
                        P.op(V, lambda s_=s_, xT=xT: nc.vector.tensor_copy(out=lastc[:, :, s_:s_ + 1], in_=xT[:, :, 32 * s_ + 31:32 * s_ + 32]), r=[t_xT], w=[t_lastc])
                for gi, (c0, n) in enumerate([(0, 512), (512, 512), (1024, 512), (1536, 160)]):
                    bk = 4 + gi % 2
                    for c in range(8):
                        if sample:
                            lh, tl_ = lastc[:, c, :], t_lastc
                        else:
                            lh, tl_ = xT[:, c, 127:128], t_xT
                        P.op(PE, lambda c=c, c0=c0, n=n, bk=bk, lh=lh, nrow=nrow: nc.tensor.matmul(
                            pbank[bk][0:nrow, 0:n], lhsT=lh, rhs=W1[:, c, c0:c0 + n], start=(c == 0), stop=False), r=[tl_, t_W12], w=[t_pb[bk]])
                        P.op(PE, lambda c=c, c0=c0, n=n, bk=bk, lh=lh, nrow=nrow: nc.tensor.matmul(
                            pbank[bk][0:nrow, 0:n], lhsT=lh, rhs=W2[:, c, c0:c0 + n], start=False, stop=(c == 7)), r=[tl_, t_W12], w=[t_pb[bk]])
                    P.op(A, lambda c0=c0, n=n, bk=bk, nrow=nrow: nc.scalar.activation(out=shrow[0:nrow, 0:n], in_=pbank[bk][0:nrow, 0:n], func=AF.Copy),
                         r=[t_pb[bk]], w=[t_shrow])
                    dst = sh_s if sample else sh_p
                    P.dma(SP, lambda dst=dst, nrow=nrow, c0=c0, n=n: nc.sync.dma_start(out=dst[0:nrow, c0:c0 + n], in_=shrow[0:nrow, 0:n]), r=[t_shrow], w=[t_shdummy])

        ckpt(6)
        P.barrier()
        phB.close()

        ph2 = ExitStack()
        es.enter_context(ph2)

        def sb2(name, shape, dt=F32):
            return ph2.enter_context(nc.sbuf_tensor(name, list(shape), dt))

        hnT = sb2("hnT", [128, 8, NTOK], BF16)
        t_hnT = P.toks(NT, "hnT")
        yacc = sb2("yacc", [128, NT, D])
        t_yacc = P.toks(NT, "yacc")
        FFb = sb2("FFb", [128, 2, 16384], BF16)
        t_FF = P.toks(2, "FF")
        stg2 = [sb2("stg2_%d" % i, [128, 1024]) for i in range(2)]
        t_stg2 = P.toks(2, "stg2")
        ndma = [0]
        cvt_engs = [(A, lambda o, i: nc.scalar.activation(out=o, in_=i, func=AF.Copy)),
                    (V, lambda o, i: nc.vector.tensor_copy(out=o, in_=i)),
                    (PL, lambda o, i: nc.gpsimd.tensor_copy(out=o, in_=i))]

        def load_cvt(src_ap, dst_ap, t_dst):
            b = ndma[0] % 2
            ndma[0] += 1
            P.dma(SP, lambda: nc.sync.dma_start(out=stg2[b][:], in_=src_ap), w=[t_stg2[b]])
            for k in range(2):
                en, f = cvt_engs[(2 * ndma[0] + k) % 3]
                P.op(en, lambda k=k, f=f: f(dst_ap[:, k * 512:(k + 1) * 512], stg2[b][:, k * 512:(k + 1) * 512]),
                     r=[t_stg2[b]], w=[t_dst])

        WO = FFb[:, 1, 0:8192].rearrange("p (c n) -> p c n", c=8)
        for c in range(8):
            load_cvt(w_out[c * 128:(c + 1) * 128, :], WO[:, c, :], t_FF[1])

        def load_quarter(q):
            bq = q % 2
            f1 = FFb[:, bq, 0:8192].rearrange("p (c n) -> p c n", c=8)
            f2 = FFb[:, bq, 8192:16384].rearrange("p (c n) -> p c n", c=8)
            for c in range(8):
                load_cvt(w_ff1[c * 128:(c + 1) * 128, q * 1024:(q + 1) * 1024], f1[:, c, :], t_FF[bq])
            for j in range(8):
                load_cvt(w_ff2[(q * 8 + j) * 128:(q * 8 + j + 1) * 128, :], f2[:, j, :], t_FF[bq])

        load_quarter(0)
        ot = [sb2("ot%d" % i, [128, 8, 128], BF16) for i in range(2)]
        t_ot = P.toks(2, "ot")
        xt2 = [sb2("xt2_%d" % i, [128, D]) for i in range(2)]
        t_xt2 = P.toks(2, "xt2")
        hn = sb2("hn", [128, D], BF16)
        t_hn = P.tok("hn")
        stat2 = sb2("stat2", [128, 8])
        t_stat2p = P.tok("stat2p")
        def p2a_front(ti):
            b = ti % 2
            r0 = ti * 128
            P.dma(SP, lambda: nc.sync.dma_start(out=xt2[b][:], in_=x_all[r0:r0 + 128, :]), w=[t_xt2[b]])
            P.dma(SP, lambda: nc.sync.dma_start(out=ot[b][:], in_=o_scr[ti, :, :, :]), w=[t_ot[b]])
            for half in range(2):
                for c in range(8):
                    P.op(PE, lambda c=c, half=half: nc.tensor.matmul(
                        pbank[half][:, :], lhsT=ot[b][:, c, :], rhs=WO[:, c, half * 512:(half + 1) * 512],
                        start=(c == 0), stop=(c == 7)), r=[t_ot[b], t_FF[1]], w=[t_pb[half]])
                P.op(V, lambda half=half: nc.vector.tensor_tensor(out=yacc[:, ti, half * 512:(half + 1) * 512], in0=pbank[half][:, :],
                                                                  in1=xt2[b][:, half * 512:(half + 1) * 512], op=ALU.add),
                     r=[t_pb[half], t_xt2[b]], w=[t_yacc[ti]])

        def p2a_back(ti):
            r0 = ti * 128
            st_ = t_stat2p
            P.op(A, lambda: nc.scalar.activation(out=hn[:], in_=yacc[:, ti, :], func=AF.Square, accum_out=stat2[:, 0:1]), r=[t_yacc[ti]], w=[t_hn, st_])
            P.op(A, lambda: nc.scalar.activation(out=stat2[:, 1:2], in_=stat2[:, 0:1], func=AF.Ln, scale=1.0 / D, bias=eps_c[:, 0:1]), r=[st_] + CONST, w=[st_])
            P.op(A, lambda: nc.scalar.activation(out=stat2[:, 2:3], in_=stat2[:, 1:2], func=AF.Exp, scale=-0.5), r=[st_], w=[st_])
            P.op(V, lambda: nc.vector.tensor_scalar(out=hn[:], in0=yacc[:, ti, :], scalar1=stat2[:, 2:3], scalar2=None, op0=ALU.mult),
                 r=[t_yacc[ti], st_], w=[t_hn])
            pbT = pb_bf(7)
            for c in range(8):
                P.op(PE, lambda c=c: nc.tensor.transpose(pbT[:, c * 128:(c + 1) * 128], hn[:, c * 128:(c + 1) * 128], ident_b[:]),
                     r=[t_hn] + CONST, w=[t_pb[7]])
            P.op(V, lambda: nc.vector.tensor_tensor(
                out=hnT[:, :, r0:r0 + 128], in0=pbT.rearrange("p (c t) -> p c t", c=8),
                in1=gffn[:].unsqueeze(2).broadcast_to([128, 8, 128]), op=ALU.mult), r=[t_pb[7]] + CONST, w=[t_hnT[ti]])

        p2a_front(0)
        for ti in range(NT):
            if ti + 1 < NT:
                p2a_front(ti + 1)
            p2a_back(ti)

        ckpt(7)
        hidT = sb2("hidT", [128, 8, 512], BF16)
        t_hid = P.tok("hid")
        relu_t = [sb2("relu_t%d" % i, [128, 512], BF16) for i in range(2)]
        t_relu = P.toks(2, "relu")
        nblk = (NTOK + 511) // 512
        for q in range(4):
            bq = q % 2
            if q + 1 < 4:
                load_quarter(q + 1)
            f1 = FFb[:, bq, 0:8192].rearrange("p (c n) -> p c n", c=8)
            f2 = FFb[:, bq, 8192:16384].rearrange("p (c n) -> p c n", c=8)
            for blk in range(nblk):
                c0 = blk * 512
                ncol = min(512, NTOK - c0)
                tiles = list(range(c0 // 128, (c0 + ncol) // 128))
                for j in range(8):
                    bk = 2 + j % 4
                    for c in range(8):
                        P.op(PE, lambda c=c, j=j, bk=bk, c0=c0, ncol=ncol, f1=f1: nc.tensor.matmul(
                            pbank[bk][:, 0:ncol], lhsT=f1[:, c, j * 128:(j + 1) * 128], rhs=hnT[:, c, c0:c0 + ncol],
                            start=(c == 0), stop=(c == 7)), r=[t_FF[bq]] + [t_hnT[t] for t in tiles], w=[t_pb[bk]])
                    rb = j % 2
                    tr = t_relu[rb]
                    P.op(A, lambda bk=bk, ncol=ncol, rb=rb: nc.scalar.activation(out=relu_t[rb][:, 0:ncol], in_=pbank[bk][:, 0:ncol], func=AF.Relu),
                         r=[t_pb[bk]], w=[tr])
                    P.op(V, lambda j=j, bk=bk, ncol=ncol, rb=rb: nc.vector.tensor_tensor(out=hidT[:, j, 0:ncol], in0=pbank[bk][:, 0:ncol],
                                                                                in1=relu_t[rb][:, 0:ncol], op=ALU.mult),
                         r=[t_pb[bk], tr], w=[t_hid])
                for t in tiles:
                    r0 = t * 128
                    lo = r0 - c0
                    for half in range(2):
                        for j in range(8):
                            P.op(PE, lambda j=j, half=half, lo=lo, f2=f2: nc.tensor.matmul(
                                pbank[half][:, :], lhsT=hidT[:, j, lo:lo + 128], rhs=f2[:, j, half * 512:(half + 1) * 512],
                                start=(j == 0), stop=(j == 7)), r=[t_hid, t_FF[bq]], w=[t_pb[half]])
                        P.op(V, lambda t=t, half=half: nc.vector.tensor_tensor(out=yacc[:, t, half * 512:(half + 1) * 512], in0=pbank[half][:, :],
                                                                                in1=yacc[:, t, half * 512:(half + 1) * 512], op=ALU.add),
                             r=[t_pb[half], t_yacc[t]], w=[t_yacc[t]])
                    if q == 3:
                        P.dma(SP, lambda t=t, r0=r0: nc.sync.dma_start(out=y_all[r0:r0 + 128, :], in_=yacc[:, t, :]), r=[t_yacc[t]])

      except _Stop:
        pass
      info = P.emit()
      build_program.info = info
    return nc


_CACHE = {}


def _consts():
    ident = np.eye(128, dtype=np.float32)
    half = 8
    inv = (500000.0 ** (-np.arange(0, 16, 2, dtype=np.float32) / 16.0)).astype(np.float32)
    rope = np.zeros((128, NT, 16), np.float32)
    for t in range(NT):
        if t < NPT:
            pos = (t * 128 + np.arange(128)).astype(np.float32)
        else:
            pos = (PAST + (np.arange(128) % 32)).astype(np.float32)
        ang = pos[:, None] * inv[None, :]
        rope[:, t, 0:8] = np.cos(ang)
        rope[:, t, 8:16] = np.sin(ang)
    masks = np.zeros((128, 8, 128), np.float32)
    idx = np.arange(128)
    for base, C in ((0, 64), (3, 32)):
        same = (idx[:, None] // C) == (idx[None, :] // C)
        masks[:, base + 0, :] = same & (idx[:, None] > idx[None, :])
        masks[:, base + 1, :] = same & (idx[:, None] < idx[None, :])
        masks[:, base + 2, :] = same & (idx[:, None] <= idx[None, :])
        masks[:, 6 if C == 64 else 7, :] = same
    small = np.zeros((128, 16), np.float32)
    colm = np.zeros((64, 6, 128), np.float32)
    for c in range(2):
        small[c * 64, 0 + c] = 1.0
        small[:, 6 + c] = (idx // 64 == c)
        colm[:, 0 + c, :] = (idx // 64 == c)[None, :]
    for c in range(4):
        small[c * 32, 2 + c] = 1.0
        small[:, 8 + c] = (idx // 32 == c)
        colm[:, 2 + c, :] = (idx // 32 == c)[None, :]
    sels = np.zeros((4, 128), np.float32)
    for s_ in range(4):
        sels[s_, 32 * s_] = 1.0
    return ident, rope, masks, small, colm, sels


def kernel(**inputs):
    f = lambda a: np.ascontiguousarray(np.asarray(a, dtype=np.float32))
    if "nc" not in _CACHE:
        _CACHE["nc"] = build_program()
    nc = _CACHE["nc"]
    ident, rope, masks, small, colm, sels = _consts()
    xp = f(inputs["x_prompt"])
    xs = f(inputs["x_sample"])
    shared = {
        "w_in": f(inputs["w_in"][0]), "w_out": f(inputs["w_out"][0]), "w_ff1": f(inputs["w_ff1"][0]), "w_ff2": f(inputs["w_ff2"][0]),
        "norm_mix": f(inputs["norm_mix"][0]), "norm_ffn": f(inputs["norm_ffn"][0]), "q_gain": f(inputs["q_gain"][0]),
        "k_gain": f(inputs["k_gain"][0]), "kidx_ln_g": f(inputs["kidx_ln_g"][0]), "kidx_ln_b": f(inputs["kidx_ln_b"][0]),
        "mu_shift": f(inputs["mu_shift"][0]), "w0": f(inputs["w0"][0]), "w2": f(inputs["w2"][0]), "a0": f(inputs["a0"][0]),
        "a2": f(inputs["a2"][0]), "g2": f(inputs["g2"][0]), "k_k": f(inputs["k_k"][0]), "k_a": f(inputs["k_a"][0]),
        "r_k": f(inputs["r_k"][0]).reshape(512), "ln_x_g": f(inputs["ln_x_g"][0]), "ln_x_b": f(inputs["ln_x_b"][0]),
        "c_ident": ident, "c_rope": rope, "c_masks": masks, "c_small": small, "c_colm": colm, "c_sels": sels,
    }
    in_maps = []
    for c in range(NCORE):
        m = dict(shared)
        m["x_all"] = np.ascontiguousarray(np.concatenate([xp[c], xs[4 * c:4 * c + 4].reshape(128, D)], axis=0))
        m["ck"] = f(inputs["cache_k"][0, 4 * c:4 * c + 4]).reshape(4, PAST, 256)
        m["cv"] = f(inputs["cache_v"][0, 4 * c:4 * c + 4]).reshape(4, PAST, 256)
        m["cki"] = f(inputs["cache_kidx"][0, 4 * c:4 * c + 4])
        m["swkv"] = f(inputs["state_wkv"][0, 4 * c:4 * c + 4])
        m["ssh"] = f(inputs["state_shift"][0, 4 * c:4 * c + 4, 0])
        in_maps.append(m)
    res = run_bass_kernel_spmd(nc, in_maps, core_ids=list(range(NCORE)))
    R = res.results
    cat = lambda k: np.stack([np.asarray(R[c][k]) for c in range(NCORE)])
    y_all = cat("y_all")
    k_all = cat("k_all")
    v_all = cat("v_all")
    ki_all = cat("ki_all")
    y_p = y_all[:, :SEQ].reshape(8, SEQ, D)
    y_s = y_all[:, SEQ:].reshape(DEC_B, DEC_T, D)
    k_p = k_all[:, :SEQ].reshape(1, 8, SEQ, 4, 64)
    v_p = v_all[:, :SEQ].reshape(1, 8, SEQ, 4, 64)
    ki_p = ki_all[:, :SEQ].reshape(1, 8, SEQ, 64)
    k_s = k_all[:, SEQ:].reshape(1, DEC_B, DEC_T, 4, 64)
    v_s = v_all[:, SEQ:].reshape(1, DEC_B, DEC_T, 4, 64)
    ki_s = ki_all[:, SEQ:].reshape(1, DEC_B, DEC_T, 64)
    wkv_p = cat("wkv_p").reshape(1, 8, 8, 64, 64)
    wkv_s = cat("wkv_s").reshape(1, DEC_B, 8, 64, 64)
    sh_p = cat("sh_p").reshape(1, 8, 1, SW)
    sh_s = cat("sh_s").reshape(1, DEC_B, 1, SW)
    out = (y_p, y_s, k_p, v_p, ki_p, wkv_p, sh_p, k_s, v_s, ki_s, wkv_s, sh_s)
    return tuple(np.ascontiguousarray(o, dtype=np.float32) for o in out)
```

```python
import numpy as np
from contextlib import ExitStack
import concourse.bass as bass
import concourse.mybir as mybir
from concourse.bass_utils import run_bass_kernel_spmd

F32 = mybir.dt.float32
BF16 = mybir.dt.bfloat16
ALU = mybir.AluOpType
AF = mybir.ActivationFunctionType
AX = mybir.AxisListType

D = 1024
NCORE = 8
SEQ = 2048
DEC_B = 32
DEC_T = 32
PAST = 1024
NPT = SEQ // 128
NT = NPT + 1
NTOK = NT * 128
HD = 64
PROJ = 3304
SW = 1696
NA = 1608
DFF = 4096
TOPK = 256
NORM_EPS = 1e-6
GN_EPS = 64e-5


class Tok:
    __slots__ = ("name", "w", "rd", "rd_dma")

    def __init__(self, name):
        self.name = name
        self.w = None
        self.rd = {}
        self.rd_dma = []


class Op:
    __slots__ = ("eng", "fn", "deps", "signal", "dma", "sig", "clock")

    def __init__(self, eng, fn, dma):
        self.eng = eng
        self.fn = fn
        self.deps = set()
        self.signal = False
        self.dma = dma
        self.sig = None
        self.clock = None


class Prog:
    NSLOT = 40

    def __init__(self, nc, es):
        self.nc = nc
        self.E = {"pe": nc.tensor, "dve": nc.vector, "act": nc.scalar, "pool": nc.gpsimd, "sp": nc.sync}
        self.ops = []
        self.sems = {e: es.enter_context(nc.semaphore("sem_" + e)) for e in self.E}
        self.slots = [es.enter_context(nc.semaphore("dsl%d" % i)) for i in range(self.NSLOT)]
        self.dma_ops = []
        self.bar_toks = []
        self.tiny = {}
        self.embed_wait = True

    def tok(self, name="t"):
        return Tok(name)

    def toks(self, n, name="t"):
        return [Tok(name + str(i)) for i in range(n)]

    def _record(self, eng, fn, r, w, dma):
        idx = len(self.ops)
        op = Op(eng, fn, dma)

        def add(d, kind):
            if d is None:
                return
            dop = self.ops[d]
            if (not dma) and (not dop.dma) and dop.eng == eng:
                if eng == "pe":
                    return
            op.deps.add(d)
            dop.signal = True

        for t in r:
            add(t.w, "raw")
        for t in w:
            add(t.w, "waw")
            for d in t.rd.values():
                add(d, "war")
            for d in t.rd_dma:
                add(d, "war")
        for t in w:
            t.w = idx
            t.rd = {}
            t.rd_dma = []
        for t in r:
            if dma:
                t.rd_dma.append(idx)
            else:
                t.rd[eng] = idx
        self.ops.append(op)
        if dma:
            self.dma_ops.append(idx)
            op.signal = True
        return idx

    def op(self, eng, fn, r=(), w=()):
        return self._record(eng, fn, r, w, False)

    def dma(self, q, fn, r=(), w=()):
        return self._record(q, fn, list(r) + self.bar_toks, w, True)

    def barrier(self):
        CE = ["pe", "dve", "act", "pool"]
        bt = {e: Tok("bar_" + e) for e in CE}
        pend = list(self.dma_ops)
        self.dma_ops = []
        for e in CE:
            fn, xr, xw = self.tiny[e]
            i = self.op(e, fn, r=xr, w=[bt[e]] + xw)
            if e == "act":
                for d in pend:
                    self.ops[i].deps.add(d)
        bt2 = {e: Tok("bar2_" + e) for e in CE}
        for e in CE:
            fn, xr, xw = self.tiny[e]
            self.op(e, fn, r=list(bt.values()) + xr, w=[bt2[e]] + xw)
        self.bar_toks = list(bt.values())

    def emit(self):
        clock = {e: {} for e in self.E}
        count = {e: 0 for e in self.E}
        slot_uses = [0] * self.NSLOT
        nslot = 0
        nwait = 0
        for op in self.ops:
            e = op.eng
            eng = self.E[e]
            ck = clock[e]
            deps = sorted(op.deps, reverse=True)
            pend_waits = []
            for d in deps:
                dop = self.ops[d]
                key, val = dop.sig
                if ck.get(key, 0) >= val:
                    continue
                sem = self.sems[key] if isinstance(key, str) else self.slots[key]
                pend_waits.append((sem, val))
                nwait += 1
                for k2, v2 in dop.clock.items():
                    if ck.get(k2, 0) < v2:
                        ck[k2] = v2
            emb = None
            if pend_waits and not op.dma and self.embed_wait:
                emb = pend_waits.pop()
            for (sem, val) in pend_waits:
                eng.wait_ge(sem, val)
            if op.dma:
                j = nslot % self.NSLOT
                nslot += 1
                prev = 16 * slot_uses[j]
                if ck.get(j, 0) < prev:
                    eng.wait_ge(self.slots[j], prev)
                    ck[j] = prev
                ins = op.fn()
                ins.then_inc(self.slots[j], 16)
                slot_uses[j] += 1
                op.sig = (j, 16 * slot_uses[j])
                op.clock = dict(ck)
                op.clock[j] = op.sig[1]
            else:
                ins = op.fn()
                if emb is not None:
                    ins._wait_ge(emb[0], emb[1])
                if op.signal:
                    count[e] += 1
                    ins.then_inc(self.sems[e], 1)
                    op.sig = (e, count[e])
                    op.clock = dict(ck)
                    op.clock[e] = count[e]
        for j in range(self.NSLOT):
            if slot_uses[j]:
                self.nc.sync.wait_ge(self.slots[j], 16 * slot_uses[j])
        return dict(nops=len(self.ops), nwait=nwait, count=count)


class _Stop(Exception):
    pass


def build_program(stage=3, stop=0, bar_mode=3):
    nc = bass.Bass("TRN2", target_bir_lowering=False)

    def din(name, shape):
        return nc.dram_tensor(name, list(shape), F32, kind="ExternalInput").ap()

    def dout(name, shape):
        return nc.dram_tensor(name, list(shape), F32, kind="ExternalOutput").ap()

    x_all = din("x_all", [NTOK, D])
    ck_in = din("ck", [4, PAST, 256])
    cv_in = din("cv", [4, PAST, 256])
    cki_in = din("cki", [4, PAST, 64])
    swkv_in = din("swkv", [4, 8, 64, 64])
    ssh_in = din("ssh", [4, SW])
    w_in = din("w_in", [D, PROJ])
    w_out = din("w_out", [D, D])
    w_ff1 = din("w_ff1", [D, DFF])
    w_ff2 = din("w_ff2", [DFF, D])
    norm_mix = din("norm_mix", [D])
    norm_ffn = din("norm_ffn", [D])
    q_gain = din("q_gain", [64])
    k_gain = din("k_gain", [64])
    kln_g = din("kidx_ln_g", [64])
    kln_b = din("kidx_ln_b", [64])
    mu_in = din("mu_shift", [SW])
    w0_in = din("w0", [512])
    w2_in = din("w2", [32, 512])
    a0_in = din("a0", [512])
    a2_in = din("a2", [32, 512])
    g2_in = din("g2", [96, 512])
    kk_in = din("k_k", [512])
    ka_in = din("k_a", [512])
    rk_in = din("r_k", [512])
    lng_in = din("ln_x_g", [512])
    lnb_in = din("ln_x_b", [512])
    c_ident = din("c_ident", [128, 128])
    c_rope = din("c_rope", [128, NT, 16])
    c_masks = din("c_masks", [128, 8, 128])
    c_small = din("c_small", [128, 16])
    c_colm = din("c_colm", [64, 6, 128])
    c_sels = din("c_sels", [4, 128])

    y_all = dout("y_all", [NTOK, D])
    k_all = dout("k_all", [NTOK, 256])
    v_all = dout("v_all", [NTOK, 256])
    ki_all = dout("ki_all", [NTOK, 64])
    wkv_p = dout("wkv_p", [8, 64, 64])
    wkv_s = dout("wkv_s", [4, 8, 64, 64])
    sh_p = dout("sh_p", [1, SW])
    sh_s = dout("sh_s", [4, SW])
    o_scr = nc.dram_tensor("o_scr", [NT, 128, 8, 128], BF16, kind="Internal").ap()

    es = ExitStack()

    def ckpt(n):
        if stop == n:
            raise _Stop()

    with es:
      P = Prog(nc, es)
      try:

        def sb(name, shape, dt=F32):
            return es.enter_context(nc.sbuf_tensor(name, list(shape), dt))

        def ps(name, shape, dt=F32):
            return es.enter_context(nc.psum_tensor(name, list(shape), dt))

        V, A, PL, PE, SP = "dve", "act", "pool", "pe", "sp"

        V, A, PL, PE, SP = "dve", "act", "pool", "pe", "sp"
        STAGE = stage
        BAR_MODE = bar_mode
        ATT = (bar_mode & 4) == 0

        ident_f = sb("ident_f", [128, 128])
        ident_b = sb("ident_b", [128, 128], BF16)
        rope = sb("rope", [128, NT, 16])
        gmix = sb("gmix", [128, 8])
        gffn = sb("gffn", [128, 8])
        qg_bc = sb("qg_bc", [128, 64])
        kg_bc = sb("kg_bc", [128, 64])
        lg_bc = sb("lg_bc", [128, 64])
        lb_bc = sb("lb_bc", [128, 64])
        zero_b = sb("zero_b", [128, 512], BF16)
        t_const = P.tok("const")

        def bcast_load(dst, src, t=None):
            P.dma(SP, lambda: nc.sync.dma_start(out=dst[:], in_=src.partition_broadcast(128)), w=[t or t_const])

        P.dma(SP, lambda: nc.sync.dma_start(out=ident_f[:], in_=c_ident[:, :]), w=[t_const])
        P.dma(SP, lambda: nc.sync.dma_start(out=rope[:], in_=c_rope[:, :, :]), w=[t_const])
        P.dma(SP, lambda: nc.sync.dma_start(out=gmix[:], in_=norm_mix.rearrange("(c p) -> p c", p=128), allow_slow_non_contiguous=True), w=[t_const])
        P.dma(SP, lambda: nc.sync.dma_start(out=gffn[:], in_=norm_ffn.rearrange("(c p) -> p c", p=128), allow_slow_non_contiguous=True), w=[t_const])
        bcast_load(qg_bc, q_gain)
        bcast_load(kg_bc, k_gain)
        bcast_load(lg_bc, kln_g)
        bcast_load(lb_bc, kln_b)
        t_const2 = P.tok("const2")
        P.op(V, lambda: nc.vector.tensor_copy(out=ident_b[:], in_=ident_f[:]), r=[t_const], w=[t_const2])
        P.op(V, lambda: nc.vector.memset(zero_b[:], 0.0), w=[t_const2])
        eps_c = sb("eps_c", [128, 2])
        P.op(V, lambda: nc.vector.memset(eps_c[:, 0:1], NORM_EPS), w=[t_const2])
        P.op(V, lambda: nc.vector.memset(eps_c[:, 1:2], GN_EPS), w=[t_const2])
        CONST = [t_const, t_const2]
        pbank = [ps("pb%d" % i, [128, 512]) for i in range(8)]
        t_pb = P.toks(8, "pb")

        def pb_bf(i):
            return pbank[i][:].bitcast(BF16)

        ckpt(1)
        bscr_v = sb("bscr_v", [128, 2])
        bscr_a = sb("bscr_a", [128, 2])
        bscr_p = sb("bscr_p", [128, 2])
        t_bsv, t_bsa, t_bsp = P.tok("bsv"), P.tok("bsa"), P.tok("bsp")
        P.tiny = {
            "dve": (lambda: nc.vector.memset(bscr_v[:, 0:1], 0.0), [], [t_bsv]),
            "act": (lambda: nc.scalar.activation(out=bscr_a[:, 0:1], in_=eps_c[:, 0:1], func=AF.Copy), CONST, [t_bsa]),
            "pool": (lambda: nc.gpsimd.memset(bscr_p[:, 0:1], 0.0), [], [t_bsp]),
            "pe": (lambda: nc.tensor.transpose(pb_bf(7)[0:1, 0:1], ident_b[0:1, 0:1], ident_b[0:1, 0:1]), CONST, [t_pb[7]]),
        }
        def norm_tile(ti, xtb, t_x, junk, t_junk, xn, t_xn, stat, xT, t_xT, st_):
            r0 = ti * 128
            P.dma(SP, lambda: nc.sync.dma_start(out=xtb[:], in_=x_all[r0:r0 + 128, :]), w=[t_x])
            P.op(A, lambda: nc.scalar.activation(out=junk[:], in_=xtb[:], func=AF.Square, accum_out=stat[:, 0:1]),
                 r=[t_x], w=[t_junk, st_])
            P.op(A, lambda: nc.scalar.activation(out=stat[:, 1:2], in_=stat[:, 0:1], func=AF.Ln, scale=1.0 / D, bias=eps_c[:, 0:1]), r=[st_] + CONST, w=[st_])
            P.op(A, lambda: nc.scalar.activation(out=stat[:, 2:3], in_=stat[:, 1:2], func=AF.Exp, scale=-0.5), r=[st_], w=[st_])
            P.op(V, lambda: nc.vector.tensor_scalar(out=xn[:], in0=xtb[:], scalar1=stat[:, 2:3], scalar2=None, op0=ALU.mult),
                 r=[t_x, st_], w=[t_xn])
            pbT = pb_bf(7)
            for c in range(8):
                P.op(PE, lambda c=c: nc.tensor.transpose(pbT[:, c * 128:(c + 1) * 128], xn[:, c * 128:(c + 1) * 128], ident_b[:]),
                     r=[t_xn] + CONST, w=[t_pb[7]])
            P.op(V, lambda: nc.vector.tensor_tensor(
                out=xT[:], in0=pbT.rearrange("p (c t) -> p c t", c=8),
                in1=gmix[:].unsqueeze(2).broadcast_to([128, 8, 128]), op=ALU.mult), r=[t_pb[7]] + CONST, w=[t_xT])


        phA = ExitStack()
        es.enter_context(phA)

        def sbA(name, shape, dt=F32):
            return phA.enter_context(nc.sbuf_tensor(name, list(shape), dt))

        WA = sbA("WA", [128, 8, NA], BF16)
        t_W = P.tok("W")
        with ExitStack() as st:
            stg = [st.enter_context(nc.sbuf_tensor("wstg%d" % i, [128, NA], F32)) for i in range(2)]
            t_stg = P.toks(2, "wstg")
            for c in range(8):
                b = c % 2
                P.dma(SP, lambda c=c, b=b: nc.sync.dma_start(out=stg[b][:], in_=w_in[c * 128:(c + 1) * 128, 0:NA]), w=[t_stg[b]])
                P.op(A, lambda c=c, b=b: nc.scalar.activation(out=WA[:, c, 0:800], in_=stg[b][:, 0:800], func=AF.Copy), r=[t_stg[b]], w=[t_W])
                P.op(V, lambda c=c, b=b: nc.vector.tensor_copy(out=WA[:, c, 800:NA], in_=stg[b][:, 800:NA]), r=[t_stg[b]], w=[t_W])
            P.barrier()

        ckpt(2)
        xt = [sbA("xt%d" % i, [128, D]) for i in range(2)]
        t_xt = P.toks(2, "xt")
        junk = sbA("junk", [128, D], BF16)
        t_junk = P.tok("junk")
        xn = sbA("xn", [128, D], BF16)
        t_xn = P.tok("xn")
        xnT = [sbA("xnT%d" % i, [128, 8, 128], BF16) for i in range(2)]
        t_xnT = P.toks(2, "xnT")
        stat = sbA("stat", [128, 64])
        t_statA = P.tok("statA")
        t_st2 = P.tok("st2")
        RB = sbA("RB", [128, 21, 64])
        t_RB = P.tok("RB")
        vf = sbA("vf", [128, 256])
        t_vf = P.tok("vf")
        sq = sbA("sq", [128, 768])
        t_sq = P.tok("sq")
        wi_abs = sbA("wi_abs", [128, 8])
        wi_sgn = sbA("wi_sgn", [128, 8])
        t_wi = P.tok("wi")
        oatt = sbA("oatt", [128, 4, 128], BF16)
        t_oatt = P.tok("oatt")

        LKMAX = 2048
        KT = sbA("KT", [64, 4, LKMAX], BF16)
        KiT = sbA("KiT", [64, LKMAX], BF16)
        Vx = sbA("Vx", [128, 16, 4, 65], BF16)
        t_KT = P.toks(17, "KT")
        t_KiT = P.toks(17, "KiT")
        t_Vx = P.toks(17, "Vx")
        RBb = sbA("RBb", [128, 21, 64], BF16)
        t_RBb = P.tok("RBb")
        QT = sbA("QT", [64, 8, 128], BF16)
        QiT = sbA("QiT", [64, 8, 128], BF16)
        KTn = sbA("KTn", [64, 5, 128], BF16)
        t_QT, t_QiT, t_KTn = P.tok("QT"), P.tok("QiT"), P.tok("KTn")
        score = sbA("score", [128, LKMAX])
        t_score = P.tok("score")
        rl = [sbA("rl%d" % i, [128, 512], BF16) for i in range(2)]
        t_rl = P.toks(2, "rl")
        maskb = sbA("maskb", [128, LKMAX], BF16)
        t_mask = P.tok("mask")
        maskT = sbA("maskT", [128, 16, 128], BF16)
        t_maskT = P.tok("maskT")
        PTall = sbA("PTall", [128, 16, 8, 128], BF16)
        t_PTk = P.toks(16, "PTk")
        PTm = [sbA("PTm%d" % i, [128, 8, 128], BF16) for i in range(2)]
        t_PTm = P.toks(2, "PTm")
        bs = sbA("bs", [128, 64])
        sgn_s = sbA("sgn_s", [32, 8])
        t_sgn_s = P.tok("sgn_s")
        t_bs = P.tok("bs")
        pw2 = sbA("pw2", [128, 20])
        osb = sbA("osb", [128, 512], BF16)
        t_osb = P.tok("osb")
        kstg = sbA("kstg", [128, 8, 256])
        t_kstg = P.tok("kstg")
        kstb = sbA("kstb", [128, 8, 256], BF16)
        t_kstb = P.tok("kstb")
        NBIS = 16
        for j in range(NBIS):
            P.op(V, lambda j=j: nc.vector.memset(pw2[:, j:j + 1], 0.5 ** (j + 1)), w=[t_const2])
        P.op(V, lambda: nc.vector.memset(Vx[:], 1.0), w=t_Vx)

        def attn_unit(nq, qc0, L, kt_toks, ki_toks, v_toks, static_mask, causal_tail, out_col0, sgn, t_sgn):
            nkt = (L + 127) // 128
            tb = t_bs
            def emit_st_exp():
                for kt in range(nkt):
                    nk = min(128, L - kt * 128)
                    for hb in range(2):
                        for gg in range(2):
                            g = hb * 2 + gg
                            P.op(PE, lambda g=g, gg=gg, hb=hb, kt=kt, nk=nk: nc.tensor.matmul(
                                pbank[hb][0:nk, gg * 2 * nq:(gg + 1) * 2 * nq].rearrange("p (a q) -> p a q", a=2),
                                lhsT=KT[:, g, kt * 128:kt * 128 + nk], rhs=QT[:, 2 * g:2 * g + 2, qc0:qc0 + nq], start=True, stop=True),
                                r=[t_QT] + kt_toks, w=[t_pb[hb]])
                        P.op(A, lambda hb=hb, nk=nk, kt=kt: nc.scalar.activation(
                            out=PTall[0:nk, kt, hb * 4:hb * 4 + 4, 0:nq], in_=pbank[hb][0:nk, 0:4 * nq].rearrange("p (a q) -> p a q", a=4),
                            func=AF.Exp, scale=0.125), r=[t_pb[hb]], w=[t_PTk[kt]])

            if not static_mask:
                cnt = 0
                for kc in range(0, L, 512):
                    n = min(512, L - kc)
                    for h in range(8):
                        bk = 4 + cnt % 2
                        rb = cnt % 2
                        cnt += 1
                        P.op(PE, lambda h=h, kc=kc, n=n, bk=bk: nc.tensor.matmul(
                            pbank[bk][0:nq, 0:n], lhsT=QiT[:, h, qc0:qc0 + nq], rhs=KiT[:, kc:kc + n], start=True, stop=True),
                            r=[t_QiT] + ki_toks, w=[t_pb[bk]])
                        P.op(A, lambda n=n, bk=bk, rb=rb: nc.scalar.activation(out=rl[rb][0:nq, 0:n], in_=pbank[bk][0:nq, 0:n], func=AF.Relu),
                             r=[t_pb[bk]], w=[t_rl[rb]])
                        if h == 0:
                            P.op(V, lambda kc=kc, n=n, rb=rb: nc.vector.tensor_scalar(
                                out=score[0:nq, kc:kc + n], in0=rl[rb][0:nq, 0:n], scalar1=sgn[0:nq, 0:1], scalar2=None, op0=ALU.mult),
                                r=[t_rl[rb], t_sgn], w=[t_score])
                        else:
                            P.op(V, lambda h=h, kc=kc, n=n, rb=rb: nc.vector.scalar_tensor_tensor(
                                out=score[0:nq, kc:kc + n], in0=rl[rb][0:nq, 0:n], scalar=sgn[0:nq, h:h + 1],
                                in1=score[0:nq, kc:kc + n], op0=ALU.mult, op1=ALU.add), r=[t_rl[rb], t_sgn, t_score], w=[t_score])
                emit_st_exp()
                P.op(V, lambda: nc.vector.tensor_reduce(out=bs[0:nq, 0:1], in_=score[0:nq, 0:L], axis=AX.X, op=ALU.min), r=[t_score], w=[tb])
                P.op(V, lambda: nc.vector.tensor_reduce(out=bs[0:nq, 1:2], in_=score[0:nq, 0:L], axis=AX.X, op=ALU.max), r=[t_score], w=[tb])
                if causal_tail:
                    P.op(V, lambda: nc.vector.memset(score[0:64, L - 64:L], -1e30), w=[t_score])
                P.op(V, lambda: nc.vector.tensor_tensor(out=bs[0:nq, 2:3], in0=bs[0:nq, 1:2], in1=bs[0:nq, 0:1], op=ALU.subtract), r=[tb], w=[tb])
                P.op(V, lambda: nc.vector.tensor_scalar(out=bs[0:nq, 8:8 + NBIS], in0=pw2[0:nq, 0:NBIS], scalar1=bs[0:nq, 2:3], scalar2=None, op0=ALU.mult),
                     r=[tb] + CONST, w=[tb])
                P.op(V, lambda: nc.vector.tensor_tensor(out=bs[0:nq, 3:4], in0=bs[0:nq, 0:1], in1=bs[0:nq, 8:9], op=ALU.add), r=[tb], w=[tb])
                for j in range(NBIS):
                    P.op(V, lambda: nc.vector.tensor_scalar(out=maskb[0:nq, 0:L], in0=score[0:nq, 0:L], scalar1=bs[0:nq, 3:4], scalar2=None,
                                                            op0=ALU.is_ge, op1=ALU.add, accum_out=bs[0:nq, 4:5]), r=[tb, t_score], w=[t_mask, tb])
                    if j < NBIS - 1:
                        P.op(V, lambda j=j: nc.vector.tensor_scalar(out=bs[0:nq, 5:6], in0=bs[0:nq, 4:5], scalar1=TOPK - 0.5, scalar2=bs[0:nq, 8 + j:9 + j],
                                                                    op0=ALU.is_ge, op1=ALU.mult), r=[tb], w=[tb])
                        P.op(V, lambda j=j: nc.vector.scalar_tensor_tensor(out=bs[0:nq, 3:4], in0=bs[0:nq, 5:6], scalar=bs[0:nq, 9 + j:10 + j], in1=bs[0:nq, 3:4],
                                                                           op0=ALU.subtract, op1=ALU.add), r=[tb], w=[tb])
                P.op(V, lambda: nc.vector.tensor_tensor(out=bs[0:nq, 0:1], in0=bs[0:nq, 3:4], in1=bs[0:nq, 8 + NBIS - 1:8 + NBIS], op=ALU.subtract), r=[tb], w=[tb])
                P.op(V, lambda: nc.vector.tensor_scalar(out=maskb[0:nq, 0:L], in0=score[0:nq, 0:L], scalar1=bs[0:nq, 0:1], scalar2=None, op0=ALU.is_ge),
                     r=[tb, t_score], w=[t_mask])
            else:
                emit_st_exp()
                P.op(V, lambda: nc.vector.memset(maskb[0:nq, 0:L], 1.0), w=[t_mask])
                if causal_tail:
                    P.op(V, lambda: nc.vector.memset(maskb[0:64, L - 64:L], 0.0), w=[t_mask])
            for g0 in range(0, nkt, 8):
                pbm = pb_bf(6)
                ks = list(range(g0, min(nkt, g0 + 8)))
                for kt in ks:
                    nk = min(128, L - kt * 128)
                    P.op(PE, lambda kt=kt, nk=nk, g0=g0, pbm=pbm: nc.tensor.transpose(
                        pbm[0:nk, (kt - g0) * 128:(kt - g0) * 128 + nq], maskb[0:nq, kt * 128:kt * 128 + nk], ident_b[0:nq, 0:nq]),
                        r=[t_mask] + CONST, w=[t_pb[6]])
                nfull = len([kt for kt in ks if L - kt * 128 >= 128])
                if nfull:
                    P.op(A, lambda g0=g0, nfull=nfull, pbm=pbm: nc.scalar.activation(
                        out=maskT[:, g0:g0 + nfull, 0:nq], in_=pbm[:, 0:nfull * 128].rearrange("p (k q) -> p k q", q=128)[:, :, 0:nq], func=AF.Copy),
                        r=[t_pb[6]], w=[t_maskT])
                if nfull < len(ks):
                    kt = ks[-1]
                    nk = L - kt * 128
                    P.op(A, lambda kt=kt, nk=nk, g0=g0, pbm=pbm: nc.scalar.activation(
                        out=maskT[0:nk, kt, 0:nq], in_=pbm[0:nk, (kt - g0) * 128:(kt - g0) * 128 + nq], func=AF.Copy), r=[t_pb[6]], w=[t_maskT])
            for ob in (2, 3):
                P.op(PE, lambda ob=ob: nc.tensor.matmul(pbank[ob][0:nq, 0:260], lhsT=zero_b[0:1, 0:nq], rhs=zero_b[0:1, 0:260], start=True, stop=False),
                     r=CONST, w=[t_pb[ob]])
            for kt in range(nkt):
                nk = min(128, L - kt * 128)
                pb_ = kt % 2
                eng, en = (nc.vector, V)
                P.op(en, lambda eng=eng, nk=nk, pb_=pb_, kt=kt: eng.tensor_tensor(
                    out=PTm[pb_][0:nk, :, 0:nq], in0=PTall[0:nk, kt, :, 0:nq],
                    in1=maskT[0:nk, kt, 0:nq].unsqueeze(1).broadcast_to([nk, 8, nq]), op=ALU.mult), r=[t_PTk[kt], t_maskT], w=[t_PTm[pb_]])
                for h in range(8):
                    ob = 2 + h // 4
                    P.op(PE, lambda h=h, ob=ob, kt=kt, nk=nk, pb_=pb_: nc.tensor.matmul(
                        pbank[ob][0:nq, (h % 4) * 65:(h % 4) * 65 + 65], lhsT=PTm[pb_][0:nk, h, 0:nq], rhs=Vx[0:nk, kt, h // 2, :],
                        start=False, stop=(kt == nkt - 1 and h % 4 == 3)), r=[t_PTm[pb_]] + v_toks, w=[t_pb[ob]])
            to = t_bs
            for ob in range(2):
                ov = pbank[2 + ob][0:nq, 0:260].rearrange("p (h d) -> p h d", d=65)
                P.op(V, lambda ov=ov, ob=ob: nc.vector.reciprocal(out=bs[0:nq, 40 + 4 * ob:44 + 4 * ob], in_=ov[:, :, 64]), r=[t_pb[2 + ob]], w=[to])
                P.op(V, lambda ov=ov, ob=ob: nc.vector.tensor_tensor(
                    out=osb[0:nq, ob * 256:(ob + 1) * 256].rearrange("p (h d) -> p h d", d=64), in0=ov[:, :, 0:64],
                    in1=bs[0:nq, 40 + 4 * ob:44 + 4 * ob].unsqueeze(2).broadcast_to([nq, 4, 64]), op=ALU.mult), r=[t_pb[2 + ob], to], w=[t_osb])
            pbo = pb_bf(7)
            for c in range(4):
                P.op(PE, lambda c=c, pbo=pbo: nc.tensor.transpose(pbo[:, c * 128:c * 128 + nq], osb[0:nq, c * 128:(c + 1) * 128], ident_b[0:nq, 0:nq]),
                     r=[t_osb] + CONST, w=[t_pb[7]])
            P.op(A, lambda pbo=pbo: nc.scalar.activation(out=oatt[:, :, out_col0:out_col0 + nq],
                                                         in_=pbo[:, 0:512].rearrange("p (c q) -> p c q", c=4)[:, :, 0:nq], func=AF.Copy),
                 r=[t_pb[7]], w=[t_oatt])

        def attention_tile(ti):
            sample = ti == NPT
            P.op(V, lambda: nc.vector.tensor_copy(out=RBb[:, 0:12, :], in_=RB[:, 0:12, :]), r=[t_RB], w=[t_RBb])
            P.op(V, lambda: nc.vector.tensor_copy(out=RBb[:, 20, :], in_=RB[:, 20, :]), r=[t_RB], w=[t_RBb])
            P.op(V, lambda: nc.vector.tensor_tensor(out=RBb[:, 12:20, :], in0=RB[:, 12:20, :],
                                                     in1=wi_abs[:].unsqueeze(2).broadcast_to([128, 8, 64]), op=ALU.mult), r=[t_RB, t_wi], w=[t_RBb])
            pq, pqi, pk = pb_bf(7), pb_bf(6), pb_bf(5)
            for h in range(8):
                P.op(PE, lambda h=h: nc.tensor.transpose(pq[0:64, h * 128:(h + 1) * 128], RBb[:, h, :], ident_b[:]), r=[t_RBb] + CONST, w=[t_pb[7]])
            P.op(A, lambda: nc.scalar.activation(out=QT[:], in_=pq[0:64, :].rearrange("p (h t) -> p h t", h=8), func=AF.Copy), r=[t_pb[7]], w=[t_QT])
            for h in range(8):
                P.op(PE, lambda h=h: nc.tensor.transpose(pqi[0:64, h * 128:(h + 1) * 128], RBb[:, 12 + h, :], ident_b[:]), r=[t_RBb] + CONST, w=[t_pb[6]])
            P.op(V, lambda: nc.vector.tensor_copy(out=QiT[:], in_=pqi[0:64, :].rearrange("p (h t) -> p h t", h=8)), r=[t_pb[6]], w=[t_QiT])
            for g in range(5):
                src = 8 + g if g < 4 else 20
                P.op(PE, lambda g=g, src=src: nc.tensor.transpose(pk[0:64, g * 128:(g + 1) * 128], RBb[:, src, :], ident_b[:]), r=[t_RBb] + CONST, w=[t_pb[5]])
            if not sample:
                r0 = ti * 128
                P.op(A, lambda r0=r0: nc.scalar.activation(out=KT[:, :, r0:r0 + 128], in_=pk[0:64, 0:512].rearrange("p (g t) -> p g t", g=4), func=AF.Copy),
                     r=[t_pb[5]], w=[t_KT[ti]])
                P.op(A, lambda r0=r0: nc.scalar.activation(out=KiT[:, r0:r0 + 128], in_=pk[0:64, 512:640], func=AF.Copy), r=[t_pb[5]], w=[t_KiT[ti]])
                P.op(V, lambda ti=ti: nc.vector.tensor_copy(out=Vx[:, ti, :, 0:64], in_=vf[:].rearrange("p (g d) -> p g d", g=4)), r=[t_vf], w=[t_Vx[ti]])
                L = (ti + 1) * 128
                attn_unit(128, 0, L, t_KT[0:ti + 1], t_KiT[0:ti + 1], t_Vx[0:ti + 1], static_mask=(L <= TOPK), causal_tail=True, out_col0=0, sgn=wi_sgn, t_sgn=t_wi)
            else:
                P.op(A, lambda: nc.scalar.activation(out=KTn[:], in_=pk[0:64, 0:640].rearrange("p (g t) -> p g t", g=5), func=AF.Copy), r=[t_pb[5]], w=[t_KTn])
                for s in range(4):
                    allk = t_KT + t_KiT + t_Vx
                    P.dma(SP, lambda s=s: nc.sync.dma_start(out=kstg[:], in_=ck_in[s].rearrange("(k p) c -> p k c", p=128)), w=[t_kstg])
                    P.op(V, lambda: nc.vector.tensor_copy(out=kstb[:], in_=kstg[:]), r=[t_kstg], w=[t_kstb])
                    for g in range(4):
                        pkc = pb_bf(4 + g % 2)
                        for kt in range(8):
                            P.op(PE, lambda g=g, kt=kt, pkc=pkc: nc.tensor.transpose(pkc[0:64, kt * 128:(kt + 1) * 128], kstb[:, kt, g * 64:(g + 1) * 64], ident_b[:]),
                                 r=[t_kstb] + CONST, w=[t_pb[4 + g % 2]])
                        P.op(A, lambda g=g, pkc=pkc: nc.scalar.activation(out=KT[:, g, 0:1024], in_=pkc[0:64, :], func=AF.Copy), r=[t_pb[4 + g % 2]], w=allk)
                    P.op(V, lambda s=s: nc.vector.tensor_copy(out=KT[:, :, 1024:1056], in_=KTn[:, 0:4, 32 * s:32 * s + 32]), r=[t_KTn], w=allk)
                    P.dma(SP, lambda s=s: nc.sync.dma_start(out=kstg[:, :, 0:64], in_=cki_in[s].rearrange("(k p) c -> p k c", p=128)), w=[t_kstg])
                    P.op(V, lambda: nc.vector.tensor_copy(out=kstb[:, :, 0:64], in_=kstg[:, :, 0:64]), r=[t_kstg], w=[t_kstb])
                    pkc = pb_bf(4)
                    for kt in range(8):
                        P.op(PE, lambda kt=kt, pkc=pkc: nc.tensor.transpose(pkc[0:64, kt * 128:(kt + 1) * 128], kstb[:, kt, 0:64], ident_b[:]),
                             r=[t_kstb] + CONST, w=[t_pb[4]])
                    P.op(A, lambda pkc=pkc: nc.scalar.activation(out=KiT[:, 0:1024], in_=pkc[0:64, :], func=AF.Copy), r=[t_pb[4]], w=allk)
                    P.op(V, lambda s=s: nc.vector.tensor_copy(out=KiT[:, 1024:1056], in_=KTn[:, 4, 32 * s:32 * s + 32]), r=[t_KTn], w=allk)
                    P.dma(SP, lambda s=s: nc.sync.dma_start(out=kstg[:], in_=cv_in[s].rearrange("(k p) c -> p k c", p=128)), w=[t_kstg])
                    P.op(PL, lambda: nc.gpsimd.tensor_copy(out=Vx[:, 0:8, :, 0:64], in_=kstg[:].rearrange("p k (g d) -> p k g d", g=4)), r=[t_kstg], w=allk)
                    P.dma(SP, lambda s=s: nc.sync.dma_start(out=kstg[0:32, 0, :], in_=vf[32 * s:32 * s + 32, :]), r=[t_vf], w=[t_kstg])
                    P.op(PL, lambda: nc.gpsimd.tensor_copy(out=Vx[0:32, 8, :, 0:64], in_=kstg[0:32, 0, :].rearrange("p (g d) -> p g d", g=4)), r=[t_kstg], w=allk)
                    P.dma(SP, lambda s=s: nc.sync.dma_start(out=sgn_s[0:32, :], in_=wi_sgn[32 * s:32 * s + 32, :]), r=[t_wi], w=[t_sgn_s])
                    attn_unit(32, 32 * s, PAST + 32, allk, allk, allk, static_mask=False, causal_tail=False, out_col0=32 * s, sgn=sgn_s, t_sgn=t_sgn_s)


        def normA(tj):
            bj = tj % 2
            norm_tile(tj, xt[bj], t_xt[bj], junk, t_junk, xn, t_xn, stat, xnT[bj], t_xnT[bj], t_statA)

        normA(0)
        for ti in range(NT):
            sample = ti == NPT
            b = ti % 2
            xT, t_xT = xnT[b], t_xnT[b]
            r0 = ti * 128
            groupsA = [(0, 512, 0), (512, 512, 1), (1024, 512, 2), (1536, 72, 3)]
            for (c0, n, bk) in groupsA:
                for c in range(8):
                    P.op(PE, lambda c=c, c0=c0, n=n, bk=bk, xT=xT: nc.tensor.matmul(
                        pbank[bk][:, 0:n], lhsT=xT[:, c, :], rhs=WA[:, c, c0:c0 + n], start=(c == 0), stop=(c == 7)),
                        r=[t_xT, t_W], w=[t_pb[bk]])
            if ti + 1 < NT:
                normA(ti + 1)
            st2 = t_st2
            P.op(A, lambda: nc.scalar.activation(out=sq[:, 0:512], in_=pbank[0][:, 0:512], func=AF.Square), r=[t_pb[0]], w=[t_sq])
            P.op(A, lambda: nc.scalar.activation(out=sq[:, 512:768], in_=pbank[1][:, 0:256], func=AF.Square), r=[t_pb[1]], w=[t_sq])
            P.op(V, lambda: nc.vector.tensor_reduce(out=stat[:, 8:20], in_=sq[:].rearrange("p (h d) -> p h d", d=64), axis=AX.X, op=ALU.add),
                 r=[t_sq], w=[st2])
            P.op(V, lambda: nc.vector.tensor_reduce(out=stat[:, 24:25], in_=pbank[3][:, 0:64], axis=AX.X, op=ALU.add), r=[t_pb[3]], w=[st2])
            P.op(A, lambda: nc.scalar.activation(out=junk[:, 0:64], in_=pbank[3][:, 0:64], func=AF.Square, accum_out=stat[:, 25:26]),
                 r=[t_pb[3]], w=[t_junk, st2])
            P.op(A, lambda: nc.scalar.activation(out=stat[:, 8:20], in_=stat[:, 8:20], func=AF.Ln, scale=1.0 / 64, bias=eps_c[:, 0:1]), r=[st2] + CONST, w=[st2])
            P.op(A, lambda: nc.scalar.activation(out=stat[:, 8:20], in_=stat[:, 8:20], func=AF.Exp, scale=-0.5), r=[st2], w=[st2])
            P.op(V, lambda: nc.vector.tensor_scalar(out=stat[:, 26:27], in0=stat[:, 24:25], scalar1=1.0 / 64, scalar2=None, op0=ALU.mult), r=[st2], w=[st2])
            P.op(V, lambda: nc.vector.tensor_tensor(out=stat[:, 27:28], in0=stat[:, 26:27], in1=stat[:, 26:27], op=ALU.mult), r=[st2], w=[st2])
            P.op(V, lambda: nc.vector.scalar_tensor_tensor(out=stat[:, 28:29], in0=stat[:, 25:26], scalar=1.0 / 64, in1=stat[:, 27:28],
                                                          op0=ALU.mult, op1=ALU.subtract), r=[st2], w=[st2])
            P.op(A, lambda: nc.scalar.activation(out=stat[:, 28:29], in_=stat[:, 28:29], func=AF.Ln, bias=eps_c[:, 0:1]), r=[st2] + CONST, w=[st2])
            P.op(A, lambda: nc.scalar.activation(out=stat[:, 28:29], in_=stat[:, 28:29], func=AF.Exp, scale=-0.5), r=[st2], w=[st2])
            P.op(V, lambda: nc.vector.tensor_tensor(out=RB[:, 0:8, :], in0=pbank[0][:, 0:512].rearrange("p (h d) -> p h d", d=64),
                                                    in1=stat[:, 8:16].unsqueeze(2).broadcast_to([128, 8, 64]), op=ALU.mult),
                 r=[t_pb[0], st2], w=[t_RB])
            P.op(V, lambda: nc.vector.tensor_tensor(out=RB[:, 8:12, :], in0=pbank[1][:, 0:256].rearrange("p (h d) -> p h d", d=64),
                                                    in1=stat[:, 16:20].unsqueeze(2).broadcast_to([128, 4, 64]), op=ALU.mult),
                 r=[t_pb[1], st2], w=[t_RB])
            P.op(V, lambda: nc.vector.tensor_tensor(out=RB[:, 0:8, :], in0=RB[:, 0:8, :],
                                                     in1=qg_bc[:].unsqueeze(1).broadcast_to([128, 8, 64]), op=ALU.mult), r=[t_RB] + CONST, w=[t_RB])
            P.op(V, lambda: nc.vector.tensor_tensor(out=RB[:, 8:12, :], in0=RB[:, 8:12, :],
                                                     in1=kg_bc[:].unsqueeze(1).broadcast_to([128, 4, 64]), op=ALU.mult), r=[t_RB] + CONST, w=[t_RB])
            P.op(A, lambda: nc.scalar.activation(out=vf[:], in_=pbank[1][:, 256:512], func=AF.Copy), r=[t_pb[1]], w=[t_vf])
            P.op(A, lambda: nc.scalar.activation(out=RB[:, 12:20, :], in_=pbank[2][:, 0:512].rearrange("p (h d) -> p h d", d=64), func=AF.Copy),
                 r=[t_pb[2]], w=[t_RB])
            P.op(V, lambda: nc.vector.tensor_scalar(out=RB[:, 20, :], in0=pbank[3][:, 0:64], scalar1=stat[:, 26:27], scalar2=stat[:, 28:29],
                                                    op0=ALU.subtract, op1=ALU.mult), r=[t_pb[3], st2], w=[t_RB])
            P.op(V, lambda: nc.vector.tensor_tensor(out=RB[:, 20, :], in0=RB[:, 20, :], in1=lg_bc[:], op=ALU.mult), r=[t_RB] + CONST, w=[t_RB])
            P.op(V, lambda: nc.vector.tensor_tensor(out=RB[:, 20, :], in0=RB[:, 20, :], in1=lb_bc[:], op=ALU.add), r=[t_RB] + CONST, w=[t_RB])
            P.op(A, lambda: nc.scalar.activation(out=wi_abs[:], in_=pbank[3][:, 64:72], func=AF.Abs), r=[t_pb[3]], w=[t_wi])
            P.op(A, lambda: nc.scalar.activation(out=wi_sgn[:], in_=pbank[3][:, 64:72], func=AF.Sign), r=[t_pb[3]], w=[t_wi])
            cosb = rope[:, ti, 0:8].unsqueeze(1).broadcast_to([128, 21, 8])
            sinb = rope[:, ti, 8:16].unsqueeze(1).broadcast_to([128, 21, 8])
            rt = sq[:, 0:21 * 32].rearrange("p (h f) -> p h f", f=32)
            P.op(V, lambda rt=rt, cosb=cosb: nc.vector.tensor_tensor(out=rt[:, :, 0:8], in0=RB[:, :, 0:8], in1=cosb, op=ALU.mult), r=[t_RB] + CONST, w=[t_sq])
            P.op(V, lambda rt=rt, sinb=sinb: nc.vector.tensor_tensor(out=rt[:, :, 8:16], in0=RB[:, :, 8:16], in1=sinb, op=ALU.mult), r=[t_RB] + CONST, w=[t_sq])
            P.op(V, lambda rt=rt, cosb=cosb: nc.vector.tensor_tensor(out=rt[:, :, 16:24], in0=RB[:, :, 8:16], in1=cosb, op=ALU.mult), r=[t_RB] + CONST, w=[t_sq])
            P.op(V, lambda rt=rt, sinb=sinb: nc.vector.tensor_tensor(out=rt[:, :, 24:32], in0=RB[:, :, 0:8], in1=sinb, op=ALU.mult), r=[t_RB] + CONST, w=[t_sq])
            P.op(V, lambda rt=rt: nc.vector.tensor_tensor(out=RB[:, :, 0:8], in0=rt[:, :, 0:8], in1=rt[:, :, 8:16], op=ALU.subtract), r=[t_sq], w=[t_RB])
            P.op(V, lambda rt=rt: nc.vector.tensor_tensor(out=RB[:, :, 8:16], in0=rt[:, :, 16:24], in1=rt[:, :, 24:32], op=ALU.add), r=[t_sq], w=[t_RB])
            P.dma(SP, lambda r0=r0: nc.sync.dma_start(out=k_all[r0:r0 + 128, :], in_=RB[:, 8:12, :].rearrange("p h d -> p (h d)")), r=[t_RB])
            P.dma(SP, lambda r0=r0: nc.sync.dma_start(out=v_all[r0:r0 + 128, :], in_=vf[:]), r=[t_vf])
            P.dma(SP, lambda r0=r0: nc.sync.dma_start(out=ki_all[r0:r0 + 128, :], in_=RB[:, 20, :]), r=[t_RB])
            if ti == 0:
                ckpt(3)
            if STAGE >= 2 and ATT:
                attention_tile(ti)
            else:
                P.op(V, lambda: nc.vector.memset(oatt[:], 0.0), w=[t_oatt])
            P.dma(A, lambda ti=ti: nc.scalar.dma_start(out=o_scr[ti, :, 0:4, :], in_=oatt[:]), r=[t_oatt])
            if ti == 0:
                ckpt(31)
            if ti == 1:
                ckpt(32)
            if ti == 8:
                ckpt(33)
            if ti == 15:
                ckpt(34)

        ckpt(4)
        P.barrier()
        phA.close()

        phB = ExitStack()
        es.enter_context(phB)

        def sbB(name, shape, dt=F32):
            return phB.enter_context(nc.sbuf_tensor(name, list(shape), dt))

        W1 = sbB("W1", [128, 8, SW], BF16)
        W2 = sbB("W2", [128, 8, SW], BF16)
        t_W12 = P.tok("W12")
        sh0mu = sbB("sh0mu", [4, SW], BF16)
        sels_b = sbB("sels_b", [4, 128], BF16)
        t_rc = P.tok("rconst")
        lwb = sbB("lwb", [96, 3, 512], BF16)
        mask_f = sbB("mask_f", [128, 4, 128])
        mask_b = sbB("mask_b", [128, 8, 128], BF16)
        small_c = sbB("small_c", [128, 16])
        colm = sbB("colm", [64, 6, 128], BF16)
        bcbuf = [sbB("bcbuf%d" % i, [128, 512]) for i in range(2)]
        t_bcbuf = P.toks(2, "bcbuf")
        bc_src = {"w0": w0_in, "a0": a0_in, "kk": kk_in, "ka": ka_in, "rk": rk_in, "lng": lng_in, "lnb": lnb_in}
        bc_n = [0]

        def bc(nm):
            i = bc_n[0] % 2
            bc_n[0] += 1
            src = bc_src[nm]
            P.dma(SP, lambda: nc.sync.dma_start(out=bcbuf[i][:], in_=src.partition_broadcast(128)), w=[t_bcbuf[i]])
            return bcbuf[i], t_bcbuf[i]
        tiny_c = sbB("tiny_c", [128, 1])
        with ExitStack() as st:
            mu_bc = st.enter_context(nc.sbuf_tensor("mu_bc", [128, SW], F32))
            omm_bc = st.enter_context(nc.sbuf_tensor("omm_bc", [128, SW], F32))
            ssh_t = st.enter_context(nc.sbuf_tensor("ssh_t", [4, SW], F32))
            sels_f = st.enter_context(nc.sbuf_tensor("sels_f", [4, 128], F32))
            lw_f = st.enter_context(nc.sbuf_tensor("lw_f", [96, 3, 512], F32))
            colm_f = st.enter_context(nc.sbuf_tensor("colm_f", [64, 6, 128], F32))
            mask_t = st.enter_context(nc.sbuf_tensor("mask_t", [128, 8, 128], F32))
            t_mu = P.tok("mu")
            bcast_load(mu_bc, mu_in, t_mu)
            P.dma(SP, lambda: nc.sync.dma_start(out=ssh_t[:], in_=ssh_in[:, :]), w=[t_mu])
            P.dma(SP, lambda: nc.sync.dma_start(out=sels_f[:], in_=c_sels[:, :]), w=[t_mu])
            P.op(V, lambda: nc.vector.tensor_scalar(out=omm_bc[:], in0=mu_bc[:], scalar1=-1.0, scalar2=1.0,
                                                    op0=ALU.mult, op1=ALU.add), r=[t_mu], w=[t_mu])
            P.op(V, lambda: nc.vector.tensor_tensor(out=sh0mu[:], in0=ssh_t[:], in1=mu_bc[0:4, :], op=ALU.mult), r=[t_mu], w=[t_rc])
            P.op(V, lambda: nc.vector.tensor_copy(out=sels_b[:], in_=sels_f[:]), r=[t_mu], w=[t_rc])
            stg_B = [st.enter_context(nc.sbuf_tensor("wstgB%d" % i, [128, SW], F32)) for i in range(2)]
            t_stg_B = P.toks(2, "wstgB")
            for c in range(8):
                b = c % 2
                P.dma(SP, lambda c=c, b=b: nc.sync.dma_start(out=stg_B[b][:], in_=w_in[c * 128:(c + 1) * 128, NA:PROJ]), w=[t_stg_B[b]])
                P.op(V, lambda c=c, b=b: nc.vector.tensor_tensor(out=W2[:, c, :], in0=stg_B[b][:], in1=mu_bc[:], op=ALU.mult),
                     r=[t_stg_B[b], t_mu], w=[t_W12])
                P.op(PL, lambda c=c, b=b: nc.gpsimd.tensor_tensor(out=W1[:, c, :], in0=stg_B[b][:], in1=omm_bc[:], op=ALU.mult),
                     r=[t_stg_B[b], t_mu], w=[t_W12])
            P.dma(SP, lambda: nc.sync.dma_start(out=lw_f[0:32, 0, :], in_=w2_in[:, :]), w=[t_mu])
            P.dma(SP, lambda: nc.sync.dma_start(out=lw_f[0:32, 1, :], in_=a2_in[:, :]), w=[t_mu])
            P.dma(SP, lambda: nc.sync.dma_start(out=lw_f[0:96, 2, :], in_=g2_in[:, :]), w=[t_mu])
            P.op(V, lambda: nc.vector.tensor_copy(out=lwb[0:32, 0:2, :], in_=lw_f[0:32, 0:2, :]), r=[t_mu], w=[t_rc])
            P.op(V, lambda: nc.vector.tensor_copy(out=lwb[0:96, 2, :], in_=lw_f[0:96, 2, :]), r=[t_mu], w=[t_rc])
            P.dma(SP, lambda: nc.sync.dma_start(out=mask_t[:], in_=c_masks[:, :, :]), w=[t_mu])
            for di, si in enumerate([2, 6, 5, 7]):
                P.dma(SP, lambda di=di, si=si: nc.sync.dma_start(out=mask_f[:, di, :], in_=c_masks[:, si, :]), w=[t_rc])
            P.dma(SP, lambda: nc.sync.dma_start(out=small_c[:], in_=c_small[:, :]), w=[t_rc])
            P.dma(SP, lambda: nc.sync.dma_start(out=colm_f[:], in_=c_colm[:, :, :]), w=[t_mu])
            P.op(V, lambda: nc.vector.tensor_copy(out=mask_b[:], in_=mask_t[:]), r=[t_mu], w=[t_rc])
            P.op(V, lambda: nc.vector.tensor_copy(out=colm[:], in_=colm_f[:]), r=[t_mu], w=[t_rc])
            P.op(V, lambda: nc.vector.memset(tiny_c[:], 1e-24), w=[t_rc])
            P.barrier()
        RC = [t_rc]
        ckpt(5)

        def BT(name, shape, dt=F32):
            return sbB(name, shape, dt), P.tok(name)

        xt_B = [sbB("xtB0", [128, D])] * 2
        t_xt_B = [P.tok("xtB")] * 2
        xn_B, t_xn_B = BT("xnB", [128, D], BF16)
        xnT_B = [sbB("xnTB%d" % i, [128, 8, 128], BF16) for i in range(2)]
        t_xnT_B = P.toks(2, "xnTB")
        xsh, t_xsh = BT("xsh", [128, 8, 128], BF16)
        stat_B, t_statB = BT("statB", [128, 64])
        shrow, t_shrow = BT("shrow", [4, 512])
        orw, t_orw = BT("orw", [128, 4, 128], BF16)
        rS, t_rS = BT("rS", [128, 512])
        kS, t_kS = BT("kS", [128, 512])
        vS, t_vS = BT("vS", [128, 512])
        gS, t_gS = BT("gS", [128, 512])
        vb, t_vb = BT("vb", [128, 512], BF16)
        tl, t_tl = BT("tl", [128, 160])
        li, t_li = BT("li", [128, 160], BF16)
        loraT, t_loraT = BT("loraT", [96, 384], BF16)
        XW, t_XW = BT("XW", [128, 1024])
        EG, t_EG = BT("EG", [128, 512])
        ENG, t_ENG = BT("ENG", [128, 512])
        EGM, t_EGM = BT("EGM", [128, 512])
        EEND, t_EEND = BT("EEND", [128, 512])
        EGE, t_EGE = BT("EGE", [128, 512])
        KK, t_KK = BT("KK", [128, 512])
        lwt, t_lwt = KK, t_KK
        KP, t_KP = BT("KP", [128, 512])
        GendS, t_GendS = KP, t_KP
        BB, t_BB = BT("BB", [128, 512])
        T1, t_T1 = BT("T1", [128, 512])
        GX, t_GX = T1, t_T1
        st8, t_st8 = BT("st8", [128, 64])
        rt_, t_rt = BT("rt_", [128, 512], BF16)
        at_, t_at = BT("at_", [128, 512], BF16)
        kt_, t_kt = BT("kt_", [128, 512], BF16)
        bt_, t_bt = BT("bt_", [128, 512], BF16)
        kh_, t_kh = BT("kh_", [128, 512], BF16)
        bh_, t_bh = BT("bh_", [128, 512], BF16)
        Bm, t_Bm = BT("Bm", [128, 512], BF16)
        Km, t_Km = BT("Km", [128, 512], BF16)
        rT, t_rT = BT("rT", [64, 8, 128], BF16)
        aT, t_aT = BT("aT", [64, 8, 128], BF16)
        kT, t_kT = BT("kT", [64, 8, 128], BF16)
        bT, t_bT = BT("bT", [64, 8, 128], BF16)
        Nm = [BT("Nm%d" % i, [128, 8, 128], BF16) for i in range(2)]
        Mm = [BT("Mm%d" % i, [128, 8, 128], BF16) for i in range(2)]
        ArbT, t_ArbT = BT("ArbT", [128, 8, 128], BF16)
        AakT, t_AakT = Mm[1]
        ArkT, t_ArkT = BT("ArkT", [128, 8, 128], BF16)
        Xf, t_Xf = BT("Xf", [128, 8, 128])
        Xb, t_Xb = BT("Xb", [128, 8, 128], BF16)
        junk_B, t_junk_B = Xb[:].rearrange("p h t -> p (h t)"), t_Xb
        GT, t_GT = BT("GT", [64, 512], BF16)
        RT2, t_RT2 = BT("RT2", [64, 8, 128], BF16)
        RTm, t_RTm = BT("RTm", [64, 8, 128], BF16)
        GAM, t_GAM = BT("GAM", [64, 8, 4])
        Hf, t_Hf = BT("Hf", [64, 8, 64])
        Hb, t_Hb = BT("Hb", [64, 8, 64], BF16)
        Tg, t_Tg = BT("Tg", [64, 8, 64])
        S0, t_S0 = BT("S0", [64, 8, 64])
        So, t_So = Tg, t_Tg
        YS, t_YS = EG, t_EG
        YN, t_YN = ENG, t_ENG
        ob_, t_ob = BT("ob_", [128, 512], BF16)
        P.op(V, lambda: nc.vector.memset(Hf[:], 0.0), w=[t_Hf])
        P.op(V, lambda: nc.vector.memset(Hb[:], 0.0), w=[t_Hb])
        NEG_E = -0.6065306597126334
        ckpt(60)

        def h3(ap, d=64):
            return ap.rearrange("p (h d) -> p h d", d=d)

        def rwkv_tile(ti, xT, t_xT, part):
            sample = ti == NPT
            C = 32 if sample else 64
            NCH = 128 // C
            mi = 3 if sample else 0
            MSL, MSU, MU, BLK = mi, mi + 1, mi + 2, (7 if sample else 6)
            selc = small_c[:, 2:6] if sample else small_c[:, 0:2]
            chi = small_c[:, 8:12] if sample else small_c[:, 6:8]
            cm0 = 2 if sample else 0
            nlev = 5 if sample else 6
            for (c0, n, bk) in ([(1536, 160, 3), (0, 512, 0), (512, 512, 1), (1024, 512, 2)] if part == 1 else []):
                for c in range(8):
                    P.op(PE, lambda c=c, c0=c0, n=n, bk=bk: nc.tensor.matmul(pbank[bk][:, 0:n], lhsT=xT[:, c, :], rhs=W1[:, c, c0:c0 + n],
                                                                           start=(c == 0), stop=False), r=[t_xT, t_W12], w=[t_pb[bk]])
                for c in range(8):
                    P.op(PE, lambda c=c, c0=c0, n=n, bk=bk: nc.tensor.matmul(pbank[bk][:, 0:n], lhsT=xsh[:, c, :], rhs=W2[:, c, c0:c0 + n],
                                                                           start=False, stop=(c == 7 and not sample)), r=[t_xsh, t_W12], w=[t_pb[bk]])
                if sample:
                    P.op(PE, lambda c0=c0, n=n, bk=bk: nc.tensor.matmul(pbank[bk][:, 0:n], lhsT=sels_b[0:4, :], rhs=sh0mu[0:4, c0:c0 + n],
                                                                      start=False, stop=True), r=RC, w=[t_pb[bk]])
            if part == 1:
                return
            P.op(A, lambda: nc.scalar.activation(out=rS[:], in_=pbank[0][:, :], func=AF.Copy), r=[t_pb[0]], w=[t_rS])
            P.op(A, lambda: nc.scalar.activation(out=kS[:], in_=pbank[1][:, :], func=AF.Copy), r=[t_pb[1]], w=[t_kS])
            P.op(A, lambda: nc.scalar.activation(out=vS[:], in_=pbank[2][:, :], func=AF.Copy), r=[t_pb[2]], w=[t_vS])
            P.op(A, lambda: nc.scalar.activation(out=vb[:], in_=vS[:], func=AF.Copy), r=[t_vS], w=[t_vb])
            if ti == 0:
                ckpt(62)
            P.op(A, lambda: nc.scalar.activation(out=tl[:, 0:32], in_=pbank[3][:, 0:32], func=AF.Exp, scale=2.0), r=[t_pb[3]], w=[t_tl])
            P.op(A, lambda: nc.scalar.activation(out=tl[:, 64:160], in_=pbank[3][:, 64:160], func=AF.Exp, scale=-1.0), r=[t_pb[3]], w=[t_tl])
            P.op(A, lambda: nc.scalar.activation(out=li[:, 32:64], in_=pbank[3][:, 32:64], func=AF.Copy), r=[t_pb[3]], w=[t_li])
            P.op(V, lambda: nc.vector.tensor_scalar(out=tl[:, 0:32], in0=tl[:, 0:32], scalar1=1.0, scalar2=None, op0=ALU.add), r=[t_tl], w=[t_tl])
            P.op(V, lambda: nc.vector.tensor_scalar(out=tl[:, 64:160], in0=tl[:, 64:160], scalar1=1.0, scalar2=None, op0=ALU.add), r=[t_tl], w=[t_tl])
            P.op(V, lambda: nc.vector.reciprocal(out=tl[:, 0:32], in_=tl[:, 0:32]), r=[t_tl], w=[t_tl])
            P.op(V, lambda: nc.vector.reciprocal(out=tl[:, 64:160], in_=tl[:, 64:160]), r=[t_tl], w=[t_tl])
            P.op(V, lambda: nc.vector.tensor_scalar(out=li[:, 0:32], in0=tl[:, 0:32], scalar1=-2.0, scalar2=1.0, op0=ALU.mult, op1=ALU.add), r=[t_tl], w=[t_li])
            P.op(V, lambda: nc.vector.tensor_copy(out=li[:, 64:160], in_=tl[:, 64:160]), r=[t_tl], w=[t_li])
            p7 = pb_bf(7)
            P.op(PE, lambda: nc.tensor.transpose(p7[0:32, 0:128], li[:, 0:32], ident_b[:]), r=[t_li] + CONST, w=[t_pb[7]])
            P.op(PE, lambda: nc.tensor.transpose(p7[0:32, 128:256], li[:, 32:64], ident_b[:]), r=[t_li] + CONST, w=[t_pb[7]])
            P.op(PE, lambda: nc.tensor.transpose(p7[0:96, 256:384], li[:, 64:160], ident_b[:]), r=[t_li] + CONST, w=[t_pb[7]])
            P.op(A, lambda: nc.scalar.activation(out=loraT[0:32, 0:256], in_=p7[0:32, 0:256], func=AF.Copy), r=[t_pb[7]], w=[t_loraT])
            P.op(A, lambda: nc.scalar.activation(out=loraT[0:96, 256:384], in_=p7[0:96, 256:384], func=AF.Copy), r=[t_pb[7]], w=[t_loraT])
            P.op(PE, lambda: nc.tensor.matmul(pbank[4][:, :], lhsT=loraT[0:32, 0:128], rhs=lwb[0:32, 0, :], start=True, stop=True), r=[t_loraT] + RC, w=[t_pb[4]])
            P.op(PE, lambda: nc.tensor.matmul(pbank[5][:, :], lhsT=loraT[0:32, 128:256], rhs=lwb[0:32, 1, :], start=True, stop=True), r=[t_loraT] + RC, w=[t_pb[5]])
            P.op(PE, lambda: nc.tensor.matmul(pbank[6][:, :], lhsT=loraT[0:96, 256:384], rhs=lwb[0:96, 2, :], start=True, stop=True), r=[t_loraT] + RC, w=[t_pb[6]])
            b1, tb1 = bc("w0")
            P.op(V, lambda: nc.vector.tensor_tensor(out=XW[:, 0:512], in0=pbank[4][:, :], in1=b1[:], op=ALU.add), r=[t_pb[4], tb1], w=[t_XW])
            b2, tb2 = bc("a0")
            P.op(V, lambda: nc.vector.tensor_tensor(out=XW[:, 512:1024], in0=pbank[5][:, :], in1=b2[:], op=ALU.add), r=[t_pb[5], tb2], w=[t_XW])
            P.op(A, lambda: nc.scalar.activation(out=XW[:], in_=XW[:], func=AF.Exp, scale=-1.0), r=[t_XW], w=[t_XW])
            P.op(V, lambda: nc.vector.tensor_scalar(out=XW[:], in0=XW[:], scalar1=1.0, scalar2=None, op0=ALU.add), r=[t_XW], w=[t_XW])
            P.op(V, lambda: nc.vector.reciprocal(out=XW[:], in_=XW[:]), r=[t_XW], w=[t_XW])
            P.op(A, lambda: nc.scalar.activation(out=gS[:], in_=pbank[6][:, :], func=AF.Copy), r=[t_pb[6]], w=[t_gS])
            AA = XW[:, 512:1024]
            if ti == 0:
                ckpt(64)
            P.op(A, lambda: nc.scalar.activation(out=lwt[:], in_=XW[:, 0:512], func=AF.Copy, scale=NEG_E), r=[t_XW], w=[t_lwt])
            P.op(PE, lambda: nc.tensor.matmul(pbank[3][:, :], lhsT=mask_f[:, (2 if sample else 0), :], rhs=lwt[:], start=True, stop=True), r=[t_lwt] + RC, w=[t_pb[3]])
            P.op(PE, lambda: nc.tensor.matmul(pbank[4][:, :], lhsT=mask_f[:, (3 if sample else 1), :], rhs=lwt[:], start=True, stop=True), r=[t_lwt] + RC, w=[t_pb[4]])
            GinS = XW[:, 0:512]
            P.op(A, lambda: nc.scalar.activation(out=GinS, in_=pbank[3][:, :], func=AF.Copy), r=[t_pb[3], t_lwt], w=[t_XW])
            P.op(A, lambda: nc.scalar.activation(out=GendS[:], in_=pbank[4][:, :], func=AF.Copy), r=[t_pb[4]], w=[t_GendS])
            P.op(A, lambda: nc.scalar.activation(out=EG[:], in_=GinS, func=AF.Exp), r=[t_XW], w=[t_EG])
            P.op(A, lambda: nc.scalar.activation(out=ENG[:], in_=GinS, func=AF.Exp, scale=-1.0), r=[t_XW], w=[t_ENG])
            P.op(V, lambda: nc.vector.tensor_tensor(out=GX[:], in0=GinS, in1=lwt[:], op=ALU.subtract), r=[t_XW, t_lwt], w=[t_GX])
            P.op(A, lambda: nc.scalar.activation(out=EGM[:], in_=GX[:], func=AF.Exp), r=[t_GX], w=[t_EGM])
            P.op(V, lambda: nc.vector.tensor_tensor(out=GX[:], in0=GendS[:], in1=GinS, op=ALU.subtract), r=[t_XW, t_GendS, t_EGM], w=[t_GX])
            P.op(A, lambda: nc.scalar.activation(out=EEND[:], in_=GX[:], func=AF.Exp), r=[t_GX], w=[t_EEND])
            P.op(A, lambda: nc.scalar.activation(out=EGE[:], in_=GendS[:], func=AF.Exp), r=[t_GendS], w=[t_EGE])
            if ti == 0:
                ckpt(65)
            b3, tb3 = bc("kk")
            P.op(V, lambda: nc.vector.tensor_tensor(out=KK[:], in0=kS[:], in1=b3[:], op=ALU.mult), r=[t_kS, tb3], w=[t_KK])
            P.op(A, lambda: nc.scalar.activation(out=T1[:], in_=KK[:], func=AF.Square), r=[t_KK], w=[t_T1])
            P.op(V, lambda: nc.vector.tensor_reduce(out=st8[:, 0:8], in_=h3(T1[:]), axis=AX.X, op=ALU.add), r=[t_T1], w=[t_st8])
            P.op(A, lambda: nc.scalar.activation(out=st8[:, 0:8], in_=st8[:, 0:8], func=AF.Ln, bias=tiny_c[:, 0:1]), r=[t_st8] + RC, w=[t_st8])
            P.op(A, lambda: nc.scalar.activation(out=st8[:, 0:8], in_=st8[:, 0:8], func=AF.Exp, scale=-0.5), r=[t_st8], w=[t_st8])
            P.op(V, lambda: nc.vector.tensor_tensor(out=h3(KK[:]), in0=h3(KK[:]), in1=st8[:, 0:8].unsqueeze(2).broadcast_to([128, 8, 64]), op=ALU.mult),
                 r=[t_KK, t_st8], w=[t_KK])
            b4, tb4 = bc("ka")
            P.op(V, lambda: nc.vector.scalar_tensor_tensor(out=KP[:], in0=AA, scalar=-1.0, in1=b4[:], op0=ALU.add, op1=ALU.mult), r=[t_XW, tb4], w=[t_KP])
            P.op(V, lambda: nc.vector.scalar_tensor_tensor(out=KP[:], in0=KP[:], scalar=1.0, in1=kS[:], op0=ALU.add, op1=ALU.mult), r=[t_KP, t_kS], w=[t_KP])
            P.op(V, lambda: nc.vector.tensor_tensor(out=BB[:], in0=KK[:], in1=AA, op=ALU.mult), r=[t_KK, t_XW], w=[t_BB])
            P.op(PL, lambda: nc.gpsimd.tensor_tensor(out=T1[:], in0=rS[:], in1=KP[:], op=ALU.mult), r=[t_rS, t_KP, t_st8], w=[t_T1])
            b5, tb5 = bc("rk")
            P.op(PL, lambda: nc.gpsimd.tensor_tensor(out=T1[:], in0=T1[:], in1=b5[:], op=ALU.mult), r=[t_T1, tb5], w=[t_T1])
            P.op(V, lambda: nc.vector.tensor_reduce(out=st8[:, 8:16], in_=h3(T1[:]), axis=AX.X, op=ALU.add), r=[t_T1], w=[t_st8])
            P.op(V, lambda: nc.vector.tensor_tensor(out=rt_[:], in0=rS[:], in1=EG[:], op=ALU.mult), r=[t_rS, t_EG], w=[t_rt])
            P.op(V, lambda: nc.vector.tensor_tensor(out=at_[:], in0=KK[:], in1=EGM[:], op=ALU.mult), r=[t_KK, t_EGM], w=[t_at])
            P.op(V, lambda: nc.vector.tensor_tensor(out=kt_[:], in0=KP[:], in1=ENG[:], op=ALU.mult), r=[t_KP, t_ENG], w=[t_kt])
            P.op(V, lambda: nc.vector.scalar_tensor_tensor(out=bt_[:], in0=BB[:], scalar=-1.0, in1=ENG[:], op0=ALU.mult, op1=ALU.mult), r=[t_BB, t_ENG], w=[t_bt])
            P.op(V, lambda: nc.vector.tensor_tensor(out=kh_[:], in0=KP[:], in1=EEND[:], op=ALU.mult), r=[t_KP, t_EEND], w=[t_kh])
            P.op(V, lambda: nc.vector.scalar_tensor_tensor(out=bh_[:], in0=BB[:], scalar=-1.0, in1=EEND[:], op0=ALU.mult, op1=ALU.mult), r=[t_BB, t_EEND], w=[t_bh])
            if ti == 0:
                ckpt(67)
            for (src, t_src, dst, t_dst, bk, eng) in [(rt_, t_rt, rT, t_rT, 0, A), (at_, t_at, aT, t_aT, 1, V), (kt_, t_kt, kT, t_kT, 2, A), (bt_, t_bt, bT, t_bT, 5, V)]:
                pv = pb_bf(bk)
                for h in range(8):
                    P.op(PE, lambda h=h, pv=pv, src=src: nc.tensor.transpose(pv[0:64, h * 128:(h + 1) * 128], src[:, h * 64:(h + 1) * 64], ident_b[:]),
                         r=[t_src] + CONST, w=[t_pb[bk]])
                if eng == A:
                    P.op(A, lambda pv=pv, dst=dst: nc.scalar.activation(out=dst[:], in_=pv[0:64, :].rearrange("p (h t) -> p h t", h=8), func=AF.Copy), r=[t_pb[bk]], w=[t_dst])
                else:
                    P.op(V, lambda pv=pv, dst=dst: nc.vector.tensor_copy(out=dst[:], in_=pv[0:64, :].rearrange("p (h t) -> p h t", h=8)), r=[t_pb[bk]], w=[t_dst])
            if ti == 0:
                ckpt(68)
            N0, t_N0 = Nm[0]
            M0, t_M0 = Mm[0]
            specs = [(aT, t_aT, bT, t_bT, N0, t_N0, MSL, 0), (bT, t_bT, aT, t_aT, M0, t_M0, MSU, 1), (bT, t_bT, rT, t_rT, ArbT, t_ArbT, MU, 2),
                     (kT, t_kT, aT, t_aT, AakT, t_AakT, MSU, 5), (kT, t_kT, rT, t_rT, ArkT, t_ArkT, MU, 6)]
            for hg in range(2):
                for (L_, tL, R_, tR, dst, t_dst, mk, bk) in specs:
                    for j in range(4):
                        h = hg * 4 + j
                        P.op(PE, lambda h=h, j=j, L_=L_, R_=R_, bk=bk: nc.tensor.matmul(pbank[bk][:, j * 128:(j + 1) * 128], lhsT=L_[:, h, :], rhs=R_[:, h, :],
                                                                                     start=True, stop=True), r=[tL, tR], w=[t_pb[bk]])
                    P.op(V, lambda hg=hg, dst=dst, mk=mk, bk=bk: nc.vector.tensor_tensor(
                        out=dst[:, hg * 4:hg * 4 + 4, :], in0=pbank[bk][:, :].rearrange("p (h t) -> p h t", h=4),
                        in1=mask_b[:, mk, :].unsqueeze(1).broadcast_to([128, 4, 128]), op=ALU.mult), r=[t_pb[bk]] + RC, w=[t_dst])
            if ti == 0:
                ckpt(69)
            for h in range(8):
                P.op(PE, lambda h=h: nc.tensor.matmul(pbank[7][:, h * 64:(h + 1) * 64], lhsT=AakT[:, h, :], rhs=vb[:, h * 64:(h + 1) * 64], start=True, stop=True),
                     r=[t_AakT, t_vb], w=[t_pb[7]])
            P.op(A, lambda: nc.scalar.activation(out=Xf[:, :, 64:128], in_=h3(pbank[7][:, :]), func=AF.Copy), r=[t_pb[7]], w=[t_Xf])
            P.op(A, lambda: nc.scalar.activation(out=Xf[:, :, 0:64], in_=h3(at_[:]), func=AF.Copy), r=[t_at], w=[t_Xf])
            P.op(A, lambda: nc.scalar.activation(out=Xb[:], in_=Xf[:], func=AF.Copy), r=[t_Xf], w=[t_Xb])
            if ti == 0:
                ckpt(70)
            cur = 0
            for lev in range(nlev):
                Mc, t_Mc = Mm[cur]
                Nc, t_Nc = Nm[cur]
                for h in range(8):
                    bk = 3 + h // 4
                    P.op(PE, lambda h=h, bk=bk, Mc=Mc: nc.tensor.matmul(pbank[bk][:, (h % 4) * 128:(h % 4 + 1) * 128], lhsT=Mc[:, h, :], rhs=Xb[:, h, :], start=True, stop=True),
                         r=[t_Mc, t_Xb], w=[t_pb[bk]])
                for hg in range(2):
                    P.op(V, lambda hg=hg: nc.vector.tensor_tensor(out=Xf[:, hg * 4:hg * 4 + 4, :], in0=pbank[3 + hg][:, :].rearrange("p (h t) -> p h t", h=4),
                                                                  in1=Xf[:, hg * 4:hg * 4 + 4, :], op=ALU.add), r=[t_pb[3 + hg], t_Xf], w=[t_Xf])
                P.op(A, lambda: nc.scalar.activation(out=Xb[:], in_=Xf[:], func=AF.Copy), r=[t_Xf], w=[t_Xb])
                if lev < nlev - 1:
                    Mn, t_Mn = Mm[1 - cur]
                    Nn, t_Nn = Nm[1 - cur]
                    for hg in range(2):
                        for j in range(4):
                            h = hg * 4 + j
                            P.op(PE, lambda h=h, j=j, hg=hg, Mc=Mc, Nc=Nc: nc.tensor.matmul(pbank[0 + hg][:, j * 128:(j + 1) * 128], lhsT=Nc[:, h, :], rhs=Mc[:, h, :], start=True, stop=True),
                                 r=[t_Mc, t_Nc], w=[t_pb[0 + hg]])
                        P.op(A, lambda hg=hg, Mn=Mn: nc.scalar.activation(out=Mn[:, hg * 4:hg * 4 + 4, :], in_=pbank[0 + hg][:, :].rearrange("p (h t) -> p h t", h=4), func=AF.Copy),
                             r=[t_pb[0 + hg]], w=[t_Mn])
                        if lev < nlev - 2:
                            bkn = 2 if hg == 0 else 5
                            for j in range(4):
                                h = hg * 4 + j
                                P.op(PE, lambda h=h, j=j, bkn=bkn, Mc=Mc, Nc=Nc: nc.tensor.matmul(pbank[bkn][:, j * 128:(j + 1) * 128], lhsT=Mc[:, h, :], rhs=Nc[:, h, :], start=True, stop=True),
                                     r=[t_Mc, t_Nc], w=[t_pb[bkn]])
                            P.op(A, lambda hg=hg, bkn=bkn, Nn=Nn: nc.scalar.activation(out=Nn[:, hg * 4:hg * 4 + 4, :], in_=pbank[bkn][:, :].rearrange("p (h t) -> p h t", h=4), func=AF.Copy),
                                 r=[t_pb[bkn]], w=[t_Nn])
                    cur = 1 - cur
            if ti == 0:
                ckpt(71)
            for h in range(8):
                bk = h // 4
                P.op(PE, lambda h=h, bk=bk: nc.tensor.matmul(pbank[bk][0:64, (h % 4) * 128:(h % 4 + 1) * 128], lhsT=Xb[:, h, 0:64], rhs=ArbT[:, h, :], start=True, stop=True),
                     r=[t_Xb, t_ArbT], w=[t_pb[bk]])
            for hg in range(2):
                P.op(V, lambda hg=hg: nc.vector.tensor_tensor(out=RT2[:, hg * 4:hg * 4 + 4, :], in0=pbank[hg][0:64, :].rearrange("p (h t) -> p h t", h=4),
                                                              in1=rT[:, hg * 4:hg * 4 + 4, :], op=ALU.add), r=[t_pb[hg], t_rT], w=[t_RT2])
            for h in range(8):
                P.op(PE, lambda h=h: nc.tensor.matmul(pbank[7][0:64, h * NCH:(h + 1) * NCH], lhsT=EGE[:, h * 64:(h + 1) * 64], rhs=selc, start=True, stop=True),
                     r=[t_EGE] + RC, w=[t_pb[7]])
            P.op(A, lambda: nc.scalar.activation(out=GAM[:, :, 0:NCH], in_=pbank[7][0:64, 0:8 * NCH].rearrange("p (h c) -> p h c", c=NCH), func=AF.Copy), r=[t_pb[7]], w=[t_GAM])
            if ti == 0:
                ckpt(72)
            P.op(PE, lambda: nc.tensor.matmul(pbank[2][:, :], lhsT=zero_b[0:1, 0:128], rhs=zero_b[0:1, 0:512], start=True, stop=False), r=CONST, w=[t_pb[2]])
            for h in range(8):
                P.op(PE, lambda h=h: nc.tensor.matmul(pbank[2][:, h * 64:(h + 1) * 64], lhsT=ArbT[:, h, :], rhs=Xb[:, h, 64:128], start=False, stop=False),
                     r=[t_ArbT, t_Xb], w=[t_pb[2]])
                P.op(PE, lambda h=h: nc.tensor.matmul(pbank[2][:, h * 64:(h + 1) * 64], lhsT=ArkT[:, h, :], rhs=vb[:, h * 64:(h + 1) * 64], start=False, stop=False),
                     r=[t_ArkT, t_vb], w=[t_pb[2]])
            if ti == 0:
                ckpt(73)
            for c in range(NCH):
                if sample:
                    P.dma(SP, lambda c=c: nc.sync.dma_start(out=S0[:], in_=swkv_in[c].rearrange("h v k -> v h k")), w=[t_S0])
                    for h in range(8):
                        P.op(PE, lambda h=h: nc.tensor.matmul(pbank[5][0:64, h * 64:(h + 1) * 64], lhsT=S0[:, h, :], rhs=ident_f[0:64, 0:64], start=True, stop=True),
                             r=[t_S0] + CONST, w=[t_pb[5]])
                    P.op(V, lambda: nc.vector.tensor_copy(out=Hf[:], in_=h3(pbank[5][0:64, :])), r=[t_pb[5]], w=[t_Hf])
                    P.op(A, lambda: nc.scalar.activation(out=Hb[:], in_=Hf[:], func=AF.Copy), r=[t_Hf], w=[t_Hb])
                P.op(V, lambda c=c: nc.vector.tensor_scalar(out=Bm[:], in0=bh_[:], scalar1=chi[:, c:c + 1], scalar2=None, op0=ALU.mult), r=[t_bh] + RC, w=[t_Bm])
                P.op(V, lambda c=c: nc.vector.tensor_scalar(out=Km[:], in0=kh_[:], scalar1=chi[:, c:c + 1], scalar2=None, op0=ALU.mult), r=[t_kh] + RC, w=[t_Km])
                P.op(V, lambda c=c: nc.vector.tensor_tensor(out=RTm[:], in0=RT2[:], in1=colm[:, cm0 + c, :].unsqueeze(1).broadcast_to([64, 8, 128]), op=ALU.mult),
                     r=[t_RT2] + RC, w=[t_RTm])
                for h in range(8):
                    P.op(PE, lambda h=h: nc.tensor.matmul(pbank[6][0:64, h * 64:(h + 1) * 64], lhsT=Xb[:, h, 0:64], rhs=Bm[:, h * 64:(h + 1) * 64], start=True, stop=True),
                         r=[t_Xb, t_Bm], w=[t_pb[6]])
                P.op(A, lambda: nc.scalar.activation(out=GT[:], in_=pbank[6][0:64, :], func=AF.Copy), r=[t_pb[6]], w=[t_GT])
                for h in range(8):
                    P.op(PE, lambda c=c, h=h: nc.tensor.matmul(pbank[2][:, h * 64:(h + 1) * 64], lhsT=RTm[:, h, :], rhs=Hb[:, h, :], start=False, stop=(c == NCH - 1 and h == 7)),
                         r=[t_RTm, t_Hb], w=[t_pb[2]])
                P.op(PE, lambda: nc.tensor.matmul(pbank[5][0:64, :], lhsT=zero_b[0:1, 0:64], rhs=zero_b[0:1, 0:512], start=True, stop=False), r=CONST, w=[t_pb[5]])
                for h in range(8):
                    P.op(PE, lambda c=c, h=h: nc.tensor.matmul(pbank[5][0:64, h * 64:(h + 1) * 64], lhsT=Bm[:, h * 64:(h + 1) * 64], rhs=Xb[:, h, 64:128], start=False, stop=False),
                         r=[t_Bm, t_Xb], w=[t_pb[5]])
                    P.op(PE, lambda c=c, h=h: nc.tensor.matmul(pbank[5][0:64, h * 64:(h + 1) * 64], lhsT=Km[:, h * 64:(h + 1) * 64], rhs=vb[:, h * 64:(h + 1) * 64], start=False, stop=False),
                         r=[t_Km, t_vb], w=[t_pb[5]])
                    P.op(PE, lambda c=c, h=h: nc.tensor.matmul(pbank[5][0:64, h * 64:(h + 1) * 64], lhsT=GT[:, h * 64:(h + 1) * 64], rhs=Hb[:, h, :], start=False, stop=(h == 7)),
                         r=[t_GT, t_Hb], w=[t_pb[5]])
                P.op(V, lambda c=c: nc.vector.tensor_tensor(out=Tg[:], in0=Hf[:], in1=GAM[:, :, c:c + 1].broadcast_to([64, 8, 64]), op=ALU.mult), r=[t_Hf, t_GAM], w=[t_Tg])
                P.op(V, lambda: nc.vector.tensor_tensor(out=Hf[:], in0=Tg[:], in1=h3(pbank[5][0:64, :]), op=ALU.add), r=[t_Tg, t_pb[5]], w=[t_Hf])
                P.op(A, lambda: nc.scalar.activation(out=Hb[:], in_=Hf[:], func=AF.Copy), r=[t_Hf], w=[t_Hb])
                if sample or (ti == NPT - 1 and c == NCH - 1):
                    for h in range(8):
                        P.op(PE, lambda h=h: nc.tensor.matmul(pbank[7][0:64, h * 64:(h + 1) * 64], lhsT=Hf[:, h, :], rhs=ident_f[0:64, 0:64], start=True, stop=True),
                             r=[t_Hf] + CONST, w=[t_pb[7]])
                    P.op(V, lambda: nc.vector.tensor_copy(out=So[:], in_=h3(pbank[7][0:64, :])), r=[t_pb[7]], w=[t_So])
                    dstw = wkv_s[c] if sample else wkv_p
                    P.dma(SP, lambda dstw=dstw: nc.sync.dma_start(out=dstw.rearrange("h v k -> v h k"), in_=So[:]), r=[t_So])
            if ti == 0:
                ckpt(74)
            P.op(A, lambda: nc.scalar.activation(out=YS[:], in_=pbank[2][:, :], func=AF.Copy), r=[t_pb[2]], w=[t_YS])
            P.op(V, lambda: nc.vector.tensor_reduce(out=st8[:, 16:24], in_=h3(YS[:]), axis=AX.X, op=ALU.add), r=[t_YS], w=[t_st8])
            P.op(A, lambda: nc.scalar.activation(out=T1[:], in_=YS[:], func=AF.Square), r=[t_YS, t_st8], w=[t_T1])
            P.op(V, lambda: nc.vector.tensor_reduce(out=st8[:, 24:32], in_=h3(T1[:]), axis=AX.X, op=ALU.add), r=[t_T1], w=[t_st8])
            P.op(V, lambda: nc.vector.tensor_scalar(out=st8[:, 16:24], in0=st8[:, 16:24], scalar1=1.0 / 64, scalar2=None, op0=ALU.mult), r=[t_st8], w=[t_st8])
            P.op(V, lambda: nc.vector.tensor_tensor(out=st8[:, 32:40], in0=st8[:, 16:24], in1=st8[:, 16:24], op=ALU.mult), r=[t_st8], w=[t_st8])
            P.op(V, lambda: nc.vector.scalar_tensor_tensor(out=st8[:, 24:32], in0=st8[:, 24:32], scalar=1.0 / 64, in1=st8[:, 32:40], op0=ALU.mult, op1=ALU.subtract),
                 r=[t_st8], w=[t_st8])
            P.op(A, lambda: nc.scalar.activation(out=st8[:, 24:32], in_=st8[:, 24:32], func=AF.Ln, bias=eps_c[:, 1:2]), r=[t_st8] + CONST, w=[t_st8])
            P.op(A, lambda: nc.scalar.activation(out=st8[:, 24:32], in_=st8[:, 24:32], func=AF.Exp, scale=-0.5), r=[t_st8], w=[t_st8])
            P.op(V, lambda: nc.vector.tensor_tensor(out=h3(YN[:]), in0=h3(YS[:]), in1=st8[:, 16:24].unsqueeze(2).broadcast_to([128, 8, 64]), op=ALU.subtract),
                 r=[t_YS, t_st8], w=[t_YN])
            P.op(V, lambda: nc.vector.tensor_tensor(out=h3(YN[:]), in0=h3(YN[:]), in1=st8[:, 24:32].unsqueeze(2).broadcast_to([128, 8, 64]), op=ALU.mult),
                 r=[t_YN, t_st8], w=[t_YN])
            b6, tb6 = bc("lng")
            P.op(V, lambda: nc.vector.tensor_tensor(out=YN[:], in0=YN[:], in1=b6[:], op=ALU.mult), r=[t_YN, tb6], w=[t_YN])
            b7, tb7 = bc("lnb")
            P.op(V, lambda: nc.vector.tensor_tensor(out=YN[:], in0=YN[:], in1=b7[:], op=ALU.add), r=[t_YN, tb7], w=[t_YN])
            P.op(V, lambda: nc.vector.tensor_tensor(out=h3(T1[:]), in0=h3(vS[:]), in1=st8[:, 8:16].unsqueeze(2).broadcast_to([128, 8, 64]), op=ALU.mult),
                 r=[t_vS, t_st8], w=[t_T1])
            P.op(V, lambda: nc.vector.tensor_tensor(out=YN[:], in0=YN[:], in1=T1[:], op=ALU.add), r=[t_YN, t_T1], w=[t_YN])
            P.op(V, lambda: nc.vector.tensor_tensor(out=ob_[:], in0=YN[:], in1=gS[:], op=ALU.mult), r=[t_YN, t_gS], w=[t_ob])
            p7b = pb_bf(7)
            for c in range(4):
                P.op(PE, lambda c=c: nc.tensor.transpose(p7b[:, c * 128:(c + 1) * 128], ob_[:, c * 128:(c + 1) * 128], ident_b[:]), r=[t_ob] + CONST, w=[t_pb[7]])
            P.op(A, lambda: nc.scalar.activation(out=orw[:], in_=p7b[:, 0:512].rearrange("p (c t) -> p c t", c=4), func=AF.Copy), r=[t_pb[7]], w=[t_orw])


        t_shdummy = P.tok("shd")

        def normB(tj):
            bj = tj % 2
            xTj, t_xTj = xnT_B[bj], t_xnT_B[bj]
            norm_tile(tj, xt_B[bj], t_xt_B[bj], junk_B, t_junk_B, xn_B, t_xn_B, stat_B, xTj, t_xTj, t_statB)
            if tj == NPT:
                P.op(V, lambda: nc.vector.tensor_copy(
                    out=xsh[:].rearrange("p c (s t) -> p c s t", s=4)[:, :, :, 1:32],
                    in_=xTj[:].rearrange("p c (s t) -> p c s t", s=4)[:, :, :, 0:31]), r=[t_xTj], w=[t_xsh])
                P.op(V, lambda: nc.vector.memset(xsh[:].rearrange("p c (s t) -> p c s t", s=4)[:, :, :, 0:1], 0.0), w=[t_xsh])
            else:
                P.op(V, lambda: nc.vector.tensor_copy(out=xsh[:, :, 1:128], in_=xTj[:, :, 0:127]), r=[t_xTj], w=[t_xsh])
                if tj == 0:
                    P.op(V, lambda: nc.vector.memset(xsh[:, :, 0:1], 0.0), w=[t_xsh])
                else:
                    xTp, t_xTp = xnT_B[1 - bj], t_xnT_B[1 - bj]
                    P.op(V, lambda: nc.vector.tensor_copy(out=xsh[:, :, 0:1], in_=xTp[:, :, 127:128]), r=[t_xTp], w=[t_xsh])

        normB(0)
        for ti in range(NT):
            sample = ti == NPT
            b = ti % 2
            xT, t_xT = xnT_B[b], t_xnT_B[b]
            if STAGE >= 3:
                rwkv_tile(ti, xT, t_xT, 1)
            if ti + 1 < NT:
                normB(ti + 1)
            if STAGE >= 3:
                rwkv_tile(ti, xT, t_xT, 2)
            else:
                P.op(V, lambda: nc.vector.memset(orw[:], 0.0), w=[t_orw])
            P.dma(A, lambda ti=ti: nc.scalar.dma_start(out=o_scr[ti, :, 4:8, :], in_=orw[:]), r=[t_orw])
            if ti == 1:
                ckpt(51)
            if ti == NPT - 1 or sample:
                nrow = 4 if sample else 1
                if sample:
                    lastc = sbB("lastc", [128, 8, 4], BF16)
                    t_lastc = P.tok("lastc")
                    for s_ in range(4):
                        P.op(V, lambda s_=s_, xT=xT: nc.vector.tensor_copy(out=lastc[:, :, s_:s_ + 1], in_=xT[:, :, 32 * s_ + 31:32 * s_ + 32]), r=[t_xT], w=[t_lastc])
                for gi, (c0, n) in enumerate([(0, 512), (512, 512), (1024, 512), (1536, 160)]):
                    bk = 4 + gi % 2
                    for c in range(8):
                        if sample:
                            lh, tl_ = lastc[:, c, :], t_lastc
                        else:
                            lh, tl_ = xT[:, c, 127:128], t_xT
                        P.op(PE, lambda c=c, c0=c0, n=n, bk=bk, lh=lh, nrow=nrow: nc.tensor.matmul(
                            pbank[bk][0:nrow, 0:n], lhsT=lh, rhs=W1[:, c, c0:c0 + n], start=(c == 0), stop=False), r=[tl_, t_W12], w=[t_pb[bk]])
                        P.op(PE, lambda c=c, c0=c0, n=n, bk=bk, lh=lh, nrow=nrow: nc.tensor.matmul(
                            pbank[bk][0:nrow, 0:n], lhsT=lh, rhs=W2[:, c, c0:c0 + n], start=False, stop=(c == 7)), r=[tl_, t_W12], w=[t_pb[bk]])
                    P.op(A, lambda c0=c0, n=n, bk=bk, nrow=nrow: nc.scalar.activation(out=shrow[0:nrow, 0:n], in_=pbank[bk][0:nrow, 0:n], func=AF.Copy),
                         r=[t_pb[bk]], w=[t_shrow])
                    dst = sh_s if sample else sh_p
                    P.dma(SP, lambda dst=dst, nrow=nrow, c0=c0, n=n: nc.sync.dma_start(out=dst[0:nrow, c0:c0 + n], in_=shrow[0:nrow, 0:n]), r=[t_shrow], w=[t_shdummy])

        ckpt(6)
        P.barrier()
        phB.close()

        ph2 = ExitStack()
        es.enter_context(ph2)

        def sb2(name, shape, dt=F32):
            return ph2.enter_context(nc.sbuf_tensor(name, list(shape), dt))

        hnT = sb2("hnT", [128, 8, NTOK], BF16)
        t_hnT = P.toks(NT, "hnT")
        yacc = sb2("yacc", [128, NT, D])
        t_yacc = P.toks(NT, "yacc")
        FFb = sb2("FFb", [128, 2, 16384], BF16)
        t_FF = P.toks(2, "FF")
        stg2 = [sb2("stg2_%d" % i, [128, 1024]) for i in range(2)]
        t_stg2 = P.toks(2, "stg2")
        ndma = [0]
        cvt_engs = [(A, lambda o, i: nc.scalar.activation(out=o, in_=i, func=AF.Copy)),
                    (V, lambda o, i: nc.vector.tensor_copy(out=o, in_=i)),
                    (PL, lambda o, i: nc.gpsimd.tensor_copy(out=o, in_=i))]

        def load_cvt(src_ap, dst_ap, t_dst):
            b = ndma[0] % 2
            ndma[0] += 1
            P.dma(SP, lambda: nc.sync.dma_start(out=stg2[b][:], in_=src_ap), w=[t_stg2[b]])
            for k in range(2):
                en, f = cvt_engs[(2 * ndma[0] + k) % 3]
                P.op(en, lambda k=k, f=f: f(dst_ap[:, k * 512:(k + 1) * 512], stg2[b][:, k * 512:(k + 1) * 512]),
                     r=[t_stg2[b]], w=[t_dst])

        WO = FFb[:, 1, 0:8192].rearrange("p (c n) -> p c n", c=8)
        for c in range(8):
            load_cvt(w_out[c * 128:(c + 1) * 128, :], WO[:, c, :], t_FF[1])

        def load_quarter(q):
            bq = q % 2
            f1 = FFb[:, bq, 0:8192].rearrange("p (c n) -> p c n", c=8)
            f2 = FFb[:, bq, 8192:16384].rearrange("p (c n) -> p c n", c=8)
            for c in range(8):
                load_cvt(w_ff1[c * 128:(c + 1) * 128, q * 1024:(q + 1) * 1024], f1[:, c, :], t_FF[bq])
            for j in range(8):
                load_cvt(w_ff2[(q * 8 + j) * 128:(q * 8 + j + 1) * 128, :], f2[:, j, :], t_FF[bq])

        load_quarter(0)
        ot = [sb2("ot%d" % i, [128, 8, 128], BF16) for i in range(2)]
        t_ot = P.toks(2, "ot")
        xt2 = [sb2("xt2_%d" % i, [128, D]) for i in range(2)]
        t_xt2 = P.toks(2, "xt2")
        hn = sb2("hn", [128, D], BF16)
        t_hn = P.tok("hn")
        stat2 = sb2("stat2", [128, 8])
        t_stat2p = P.tok("stat2p")
        def p2a_front(ti):
            b = ti % 2
            r0 = ti * 128
            P.dma(SP, lambda: nc.sync.dma_start(out=xt2[b][:], in_=x_all[r0:r0 + 128, :]), w=[t_xt2[b]])
            P.dma(SP, lambda: nc.sync.dma_start(out=ot[b][:], in_=o_scr[ti, :, :, :]), w=[t_ot[b]])
            for half in range(2):
                for c in range(8):
                    P.op(PE, lambda c=c, half=half: nc.tensor.matmul(
                        pbank[half][:, :], lhsT=ot[b][:, c, :], rhs=WO[:, c, half * 512:(half + 1) * 512],
                        start=(c == 0), stop=(c == 7)), r=[t_ot[b], t_FF[1]], w=[t_pb[half]])
                P.op(V, lambda half=half: nc.vector.tensor_tensor(out=yacc[:, ti, half * 512:(half + 1) * 512], in0=pbank[half][:, :],
                                                                  in1=xt2[b][:, half * 512:(half + 1) * 512], op=ALU.add),
                     r=[t_pb[half], t_xt2[b]], w=[t_yacc[ti]])

        def p2a_back(ti):
            r0 = ti * 128
            st_ = t_stat2p
            P.op(A, lambda: nc.scalar.activation(out=hn[:], in_=yacc[:, ti, :], func=AF.Square, accum_out=stat2[:, 0:1]), r=[t_yacc[ti]], w=[t_hn, st_])
            P.op(A, lambda: nc.scalar.activation(out=stat2[:, 1:2], in_=stat2[:, 0:1], func=AF.Ln, scale=1.0 / D, bias=eps_c[:, 0:1]), r=[st_] + CONST, w=[st_])
            P.op(A, lambda: nc.scalar.activation(out=stat2[:, 2:3], in_=stat2[:, 1:2], func=AF.Exp, scale=-0.5), r=[st_], w=[st_])
            P.op(V, lambda: nc.vector.tensor_scalar(out=hn[:], in0=yacc[:, ti, :], scalar1=stat2[:, 2:3], scalar2=None, op0=ALU.mult),
                 r=[t_yacc[ti], st_], w=[t_hn])
            pbT = pb_bf(7)
            for c in range(8):
                P.op(PE, lambda c=c: nc.tensor.transpose(pbT[:, c * 128:(c + 1) * 128], hn[:, c * 128:(c + 1) * 128], ident_b[:]),
                     r=[t_hn] + CONST, w=[t_pb[7]])
            P.op(V, lambda: nc.vector.tensor_tensor(
                out=hnT[:, :, r0:r0 + 128], in0=pbT.rearrange("p (c t) -> p c t", c=8),
                in1=gffn[:].unsqueeze(2).broadcast_to([128, 8, 128]), op=ALU.mult), r=[t_pb[7]] + CONST, w=[t_hnT[ti]])

        p2a_front(0)
        for ti in range(NT):
            if ti + 1 < NT:
                p2a_front(ti + 1)
            p2a_back(ti)

        ckpt(7)
        hidT = sb2("hidT", [128, 8, 512], BF16)
        t_hid = P.tok("hid")
        relu_t = [sb2("relu_t%d" % i, [128, 512], BF16) for i in range(2)]
        t_relu = P.toks(2, "relu")
        nblk = (NTOK + 511) // 512
        for q in range(4):
            bq = q % 2
            if q + 1 < 4:
                load_quarter(q + 1)
            f1 = FFb[:, bq, 0:8192].rearrange("p (c n) -> p c n", c=8)
            f2 = FFb[:, bq, 8192:16384].rearrange("p (c n) -> p c n", c=8)
            for blk in range(nblk):
                c0 = blk * 512
                ncol = min(512, NTOK - c0)
                tiles = list(range(c0 // 128, (c0 + ncol) // 128))
                for j in range(8):
                    bk = 2 + j % 4
                    for c in range(8):
                        P.op(PE, lambda c=c, j=j, bk=bk, c0=c0, ncol=ncol, f1=f1: nc.tensor.matmul(
                            pbank[bk][:, 0:ncol], lhsT=f1[:, c, j * 128:(j + 1) * 128], rhs=hnT[:, c, c0:c0 + ncol],
                            start=(c == 0), stop=(c == 7)), r=[t_FF[bq]] + [t_hnT[t] for t in tiles], w=[t_pb[bk]])
                    rb = j % 2
                    tr = t_relu[rb]
                    P.op(A, lambda bk=bk, ncol=ncol, rb=rb: nc.scalar.activation(out=relu_t[rb][:, 0:ncol], in_=pbank[bk][:, 0:ncol], func=AF.Relu),
                         r=[t_pb[bk]], w=[tr])
                    P.op(V, lambda j=j, bk=bk, ncol=ncol, rb=rb: nc.vector.tensor_tensor(out=hidT[:, j, 0:ncol], in0=pbank[bk][:, 0:ncol],
                                                                                in1=relu_t[rb][:, 0:ncol], op=ALU.mult),
                         r=[t_pb[bk], tr], w=[t_hid])
                for t in tiles:
                    r0 = t * 128
                    lo = r0 - c0
                    for half in range(2):
                        for j in range(8):
                            P.op(PE, lambda j=j, half=half, lo=lo, f2=f2: nc.tensor.matmul(
                                pbank[half][:, :], lhsT=hidT[:, j, lo:lo + 128], rhs=f2[:, j, half * 512:(half + 1) * 512],
                                start=(j == 0), stop=(j == 7)), r=[t_hid, t_FF[bq]], w=[t_pb[half]])
                        P.op(V, lambda t=t, half=half: nc.vector.tensor_tensor(out=yacc[:, t, half * 512:(half + 1) * 512], in0=pbank[half][:, :],
                                                                                in1=yacc[:, t, half * 512:(half + 1) * 512], op=ALU.add),
                             r=[t_pb[half], t_yacc[t]], w=[t_yacc[t]])
                    if q == 3:
                        P.dma(SP, lambda t=t, r0=r0: nc.sync.dma_start(out=y_all[r0:r0 + 128, :], in_=yacc[:, t, :]), r=[t_yacc[t]])

      except _Stop:
        pass
      info = P.emit()
      build_program.info = info
    return nc


_CACHE = {}


def _consts():
    ident = np.eye(128, dtype=np.float32)
    half = 8
    inv = (500000.0 ** (-np.arange(0, 16, 2, dtype=np.float32) / 16.0)).astype(np.float32)
    rope = np.zeros((128, NT, 16), np.float32)
    for t in range(NT):
        if t < NPT:
            pos = (t * 128 + np.arange(128)).astype(np.float32)
        else:
            pos = (PAST + (np.arange(128) % 32)).astype(np.float32)
        ang = pos[:, None] * inv[None, :]
        rope[:, t, 0:8] = np.cos(ang)
        rope[:, t, 8:16] = np.sin(ang)
    masks = np.zeros((128, 8, 128), np.float32)
    idx = np.arange(128)
    for base, C in ((0, 64), (3, 32)):
        same = (idx[:, None] // C) == (idx[None, :] // C)
        masks[:, base + 0, :] = same & (idx[:, None] > idx[None, :])
        masks[:, base + 1, :] = same & (idx[:, None] < idx[None, :])
        masks[:, base + 2, :] = same & (idx[:, None] <= idx[None, :])
        masks[:, 6 if C == 64 else 7, :] = same
    small = np.zeros((128, 16), np.float32)
    colm = np.zeros((64, 6, 128), np.float32)
    for c in range(2):
        small[c * 64, 0 + c] = 1.0
        small[:, 6 + c] = (idx // 64 == c)
        colm[:, 0 + c, :] = (idx // 64 == c)[None, :]
    for c in range(4):
        small[c * 32, 2 + c] = 1.0
        small[:, 8 + c] = (idx // 32 == c)
        colm[:, 2 + c, :] = (idx // 32 == c)[None, :]
    sels = np.zeros((4, 128), np.float32)
    for s_ in range(4):
        sels[s_, 32 * s_] = 1.0
    return ident, rope, masks, small, colm, sels


def kernel(**inputs):
    f = lambda a: np.ascontiguousarray(np.asarray(a, dtype=np.float32))
    if "nc" not in _CACHE:
        _CACHE["nc"] = build_program()
    nc = _CACHE["nc"]
    ident, rope, masks, small, colm, sels = _consts()
    xp = f(inputs["x_prompt"])
    xs = f(inputs["x_sample"])
    shared = {
        "w_in": f(inputs["w_in"][0]), "w_out": f(inputs["w_out"][0]), "w_ff1": f(inputs["w_ff1"][0]), "w_ff2": f(inputs["w_ff2"][0]),
        "norm_mix": f(inputs["norm_mix"][0]), "norm_ffn": f(inputs["norm_ffn"][0]), "q_gain": f(inputs["q_gain"][0]),
        "k_gain": f(inputs["k_gain"][0]), "kidx_ln_g": f(inputs["kidx_ln_g"][0]), "kidx_ln_b": f(inputs["kidx_ln_b"][0]),
        "mu_shift": f(inputs["mu_shift"][0]), "w0": f(inputs["w0"][0]), "w2": f(inputs["w2"][0]), "a0": f(inputs["a0"][0]),
        "a2": f(inputs["a2"][0]), "g2": f(inputs["g2"][0]), "k_k": f(inputs["k_k"][0]), "k_a": f(inputs["k_a"][0]),
        "r_k": f(inputs["r_k"][0]).reshape(512), "ln_x_g": f(inputs["ln_x_g"][0]), "ln_x_b": f(inputs["ln_x_b"][0]),
        "c_ident": ident, "c_rope": rope, "c_masks": masks, "c_small": small, "c_colm": colm, "c_sels": sels,
    }
    in_maps = []
    for c in range(NCORE):
        m = dict(shared)
        m["x_all"] = np.ascontiguousarray(np.concatenate([xp[c], xs[4 * c:4 * c + 4].reshape(128, D)], axis=0))
        m["ck"] = f(inputs["cache_k"][0, 4 * c:4 * c + 4]).reshape(4, PAST, 256)
        m["cv"] = f(inputs["cache_v"][0, 4 * c:4 * c + 4]).reshape(4, PAST, 256)
        m["cki"] = f(inputs["cache_kidx"][0, 4 * c:4 * c + 4])
        m["swkv"] = f(inputs["state_wkv"][0, 4 * c:4 * c + 4])
        m["ssh"] = f(inputs["state_shift"][0, 4 * c:4 * c + 4, 0])
        in_maps.append(m)
    res = run_bass_kernel_spmd(nc, in_maps, core_ids=list(range(NCORE)))
    R = res.results
    cat = lambda k: np.stack([np.asarray(R[c][k]) for c in range(NCORE)])
    y_all = cat("y_all")
    k_all = cat("k_all")
    v_all = cat("v_all")
    ki_all = cat("ki_all")
    y_p = y_all[:, :SEQ].reshape(8, SEQ, D)
    y_s = y_all[:, SEQ:].reshape(DEC_B, DEC_T, D)
    k_p = k_all[:, :SEQ].reshape(1, 8, SEQ, 4, 64)
    v_p = v_all[:, :SEQ].reshape(1, 8, SEQ, 4, 64)
    ki_p = ki_all[:, :SEQ].reshape(1, 8, SEQ, 64)
    k_s = k_all[:, SEQ:].reshape(1, DEC_B, DEC_T, 4, 64)
    v_s = v_all[:, SEQ:].reshape(1, DEC_B, DEC_T, 4, 64)
    ki_s = ki_all[:, SEQ:].reshape(1, DEC_B, DEC_T, 64)
    wkv_p = cat("wkv_p").reshape(1, 8, 8, 64, 64)
    wkv_s = cat("wkv_s").reshape(1, DEC_B, 8, 64, 64)
    sh_p = cat("sh_p").reshape(1, 8, 1, SW)
    sh_s = cat("sh_s").reshape(1, DEC_B, 1, SW)
    out = (y_p, y_s, k_p, v_p, ki_p, wkv_p, sh_p, k_s, v_s, ki_s, wkv_s, sh_s)
    return tuple(np.ascontiguousarray(o, dtype=np.float32) for o in out)
```
